# Optimizing a Trainium2 kernel written in Bass

```python
import math
import jax
import jax.numpy as jnp
import numpy as np

D_MODEL = 1024
BATCH = 4
SEQ = 4096
DEPTH = 2
DEC_BATCH = 128
DEC_SEQ = 1
PAST_LEN = 8192
PAGE_SIZE = 128

HEAD_DIM = 64
ROT_DIM = HEAD_DIM // 4
ROPE_THETA = 500000.0
N_A_LAYERS = DEPTH // 2
N_B_LAYERS = DEPTH - N_A_LAYERS
H_A = D_MODEL // HEAD_DIM
KVH_A = H_A // 4
WIN_A = 128
B_GROUPS = ((128, 1), (512, 4), (2048, 16))
N_BG = len(B_GROUPS)
H_B = D_MODEL // HEAD_DIM
KVH_B = H_B // 4
D_FF = 256 * ((8 * D_MODEL // 3 + 255) // 256)
EPS = 1e-6

kernel_name = 'yoco_swa_sink_dilated_macaron_step'


def _rmsnorm(x, g):
    xf = x.astype(jnp.float32)
    y = xf * jax.lax.rsqrt(jnp.mean(xf * xf, -1, keepdims=True) + EPS) * g.astype(jnp.float32)
    return y.astype(x.dtype)


def _swiglu(h, w_gu, w_dn):
    gate, up = jnp.split(h @ w_gu, 2, axis=-1)
    return (jax.nn.silu(gate) * up) @ w_dn


def _rope_tables(pos):
    inv = ROPE_THETA ** (-jnp.arange(0, ROT_DIM, 2, dtype=jnp.float32) / ROT_DIM)
    ang = pos[:, None] * inv[None, :]
    return jnp.cos(ang), jnp.sin(ang)


def _rope(x, cos, sin):
    shp = cos.shape[:1] + (1,) * (x.ndim - 3) + cos.shape[1:]
    c, s = cos.reshape(shp), sin.reshape(shp)
    x = x.astype(jnp.float32)
    half = ROT_DIM // 2
    x1, x2 = x[..., :half], x[..., half:ROT_DIM]
    return jnp.concatenate([x1 * c - x2 * s, x2 * c + x1 * s, x[..., ROT_DIM:]], axis=-1)


def _attend(s, sink):
    m = jnp.max(s, -1, keepdims=True)
    if sink is not None:
        m = jnp.maximum(m, sink)
    p = jnp.exp(s - m)
    l = jnp.sum(p, -1, keepdims=True)
    if sink is not None:
        l = l + jnp.exp(sink - m)
    return p / l, (m + jnp.log(l))[..., 0]


def _last_rows(kv, w):
    pad = [(0, 0), (w, 0)] + [(0, 0)] * (kv.ndim - 2)
    return jnp.pad(kv, pad)[:, -w:]


def _banded(q, k, v, win, sink):
    n, length, kvh, grp, hd = q.shape
    bq = math.gcd(length, win)
    nb = length // bq
    nk = bq + win
    pad = ((0, 0), (win, 0), (0, 0), (0, 0))
    idx = jnp.arange(nb)[:, None] * bq + jnp.arange(nk)[None, :]
    kb = jnp.pad(k.astype(jnp.float32), pad)[:, idx]
    vb = jnp.pad(v.astype(jnp.float32), pad)[:, idx]
    qb = q.reshape(n, nb, bq, kvh, grp, hd)
    s = jnp.einsum('nbqkgd,nbjkd->nbkgqj', qb, kb)
    i = jnp.arange(bq)[:, None]
    j = jnp.arange(nk)[None, :]
    dist = i - j + win
    kpos = jnp.arange(nb)[:, None, None] * bq + j[None] - win
    valid = (dist >= 0) & (dist <= win) & (kpos >= 0)
    s = jnp.where(valid[None, :, None, None], s, -jnp.inf)
    p, lse = _attend(s, None if sink is None else sink[:, :, None, None])
    o = jnp.einsum('nbkgqj,nbjkd->nbqkgd', p, vb).reshape(n, length, kvh, grp, hd)
    lse = lse.transpose(0, 1, 4, 2, 3).reshape(n, length, kvh, grp)
    return o, lse


def _dilated_prompt(q, k, v, win, dil):
    b, s_len = q.shape[:2]
    length = s_len // dil

    def fold(t):
        t = t.reshape((b, length, dil) + t.shape[2:])
        return jnp.swapaxes(t, 1, 2).reshape((b * dil, length) + t.shape[3:])

    def unfold(t):
        t = t.reshape((b, dil, length) + t.shape[2:])
        return jnp.swapaxes(t, 1, 2).reshape((b, s_len) + t.shape[3:])

    o, lse = _banded(fold(q), fold(k), fold(v), win // dil, None)
    return unfold(o), unfold(lse)


def _strided_decode(q, kv_all, win, dil, sink):
    ds = q.shape[1]
    taps = win // dil + 1
    idx = win + jnp.arange(ds)[:, None] - jnp.arange(taps)[None, :] * dil
    valid = idx + (PAST_LEN - win) >= 0
    g = kv_all[:, idx].astype(jnp.float32)
    s = jnp.einsum('bqkgd,bqjkd->bqkgj', q, g[:, :, :, 0])
    s = jnp.where(valid[None, :, None, None, :], s, -jnp.inf)
    p, lse = _attend(s, None if sink is None else sink[:, :, None])
    o = jnp.einsum('bqkgj,bqjkd->bqkgd', p, g[:, :, :, 1])
    return o, lse


def _mixer_a(h, w_qkv, w_o, sink, cos, sin, buf):
    b, s_len = h.shape[:2]
    grp = H_A // KVH_A
    qkv = h @ w_qkv
    q = qkv[..., :H_A * HEAD_DIM].reshape(b, s_len, KVH_A, grp, HEAD_DIM)
    k = qkv[..., H_A * HEAD_DIM:(H_A + KVH_A) * HEAD_DIM].reshape(b, s_len, KVH_A, HEAD_DIM)
    v = qkv[..., (H_A + KVH_A) * HEAD_DIM:].reshape(b, s_len, KVH_A, HEAD_DIM)
    q = _rope(q, cos, sin) * HEAD_DIM ** -0.5
    kv = jnp.stack([_rope(k, cos, sin).astype(h.dtype), v], axis=2)
    sk = sink.astype(jnp.float32).reshape(KVH_A, grp)
    if buf is None:
        o, _ = _banded(q, kv[:, :, 0], kv[:, :, 1], WIN_A, sk)
        new_buf = _last_rows(kv, WIN_A)
    else:
        kv_all = jnp.concatenate([buf.astype(kv.dtype), kv], axis=1)
        o, _ = _strided_decode(q, kv_all, WIN_A, 1, sk)
        new_buf = kv_all[:, -WIN_A:]
    return o.reshape(b, s_len, H_A * HEAD_DIM).astype(h.dtype) @ w_o, new_buf


def _shared_kv_b(x, g_kv, w_kv, cos, sin, bufs):
    b, s_len = x.shape[:2]
    kv = (_rmsnorm(x, g_kv) @ w_kv).reshape(b, s_len, N_BG, 2, KVH_B, HEAD_DIM)
    k = _rope(kv[:, :, :, 0], cos, sin).astype(x.dtype)
    kv = jnp.stack([k, kv[:, :, :, 1]], axis=3)
    srcs, new = [], []
    for gi, (win, _) in enumerate(B_GROUPS):
        kv_g = kv[:, :, gi]
        if bufs is None:
            srcs.append(kv_g)
            new.append(_last_rows(kv_g, win))
        else:
            src = jnp.concatenate([bufs[gi].astype(kv_g.dtype), kv_g], axis=1)
            srcs.append(src)
            new.append(src[:, -win:])
    return srcs, new


def _mixer_b(h, w_q, w_o, srcs, cos, sin, decode):
    b, s_len = h.shape[:2]
    grp = H_B // KVH_B
    q_all = (h @ w_q).reshape(b, s_len, N_BG, KVH_B, grp, HEAD_DIM)
    outs, lses = [], []
    for gi, (win, dil) in enumerate(B_GROUPS):
        q = _rope(q_all[:, :, gi], cos, sin) * HEAD_DIM ** -0.5
        if decode:
            o, lse = _strided_decode(q, srcs[gi], win, dil, None)
        else:
            o, lse = _dilated_prompt(q, srcs[gi][:, :, 0], srcs[gi][:, :, 1], win, dil)
        outs.append(o)
        lses.append(lse)
    wts = jax.nn.softmax(jnp.stack(lses, 0), axis=0)
    o = jnp.sum(wts[..., None] * jnp.stack(outs, 0), axis=0)
    return o.reshape(b, s_len, H_B * HEAD_DIM).astype(h.dtype) @ w_o


def _trunk(x, cos, sin, a_bufs, b_bufs, norm_g, w_ffn_gu, w_ffn_dn, w_qkv_a, sink_a, w_o_a,
           g_kv_b, w_kv_b, w_q_b, w_o_b):
    decode = b_bufs is not None
    new_a, srcs, new_b = [], None, None
    for l in range(DEPTH):
        x = x + 0.5 * _rmsnorm(_swiglu(_rmsnorm(x, norm_g[l, 0]), w_ffn_gu[l, 0], w_ffn_dn[l, 0]), norm_g[l, 1])
        h = _rmsnorm(x, norm_g[l, 2])
        if l < N_A_LAYERS:
            mix, nbuf = _mixer_a(h, w_qkv_a[l], w_o_a[l], sink_a[l], cos, sin,
                                 a_bufs[l] if decode else None)
            new_a.append(nbuf)
        else:
            if l == N_A_LAYERS:
                srcs, new_b = _shared_kv_b(x, g_kv_b, w_kv_b, cos, sin, b_bufs)
            mix = _mixer_b(h, w_q_b[l - N_A_LAYERS], w_o_b[l - N_A_LAYERS], srcs, cos, sin, decode)
        x = x + _rmsnorm(mix, norm_g[l, 3])
        x = x + 0.5 * _rmsnorm(_swiglu(_rmsnorm(x, norm_g[l, 4]), w_ffn_gu[l, 1], w_ffn_dn[l, 1]), norm_g[l, 5])
    return x, jnp.stack(new_a, 0), new_b


def setup_inputs(seed: int = 0) -> dict:
    key = jax.random.key(seed)
    ks = jax.random.split(key, 20)

    def nrm(k, shape, scale):
        return jax.random.normal(k, shape, jnp.float32) * scale

    return {
        'x_prompt': nrm(ks[0], (BATCH, SEQ, D_MODEL), 1.0),
        'x_sample': nrm(ks[1], (DEC_BATCH, DEC_SEQ, D_MODEL), 1.0),
        'cache_a_kv': nrm(ks[2], (N_A_LAYERS, DEC_BATCH, WIN_A, 2, KVH_A, HEAD_DIM), 1.0),
        'cache_b_kv_w128': nrm(ks[3], (DEC_BATCH, B_GROUPS[0][0], 2, KVH_B, HEAD_DIM), 1.0),
        'cache_b_kv_w512': nrm(ks[4], (DEC_BATCH, B_GROUPS[1][0], 2, KVH_B, HEAD_DIM), 1.0),
        'cache_b_kv_w2048': nrm(ks[5], (DEC_BATCH, B_GROUPS[2][0], 2, KVH_B, HEAD_DIM), 1.0),
        'norm_g': 1.0 + nrm(ks[6], (DEPTH, 6, D_MODEL), 0.1),
        'w_ffn_gu': nrm(ks[7], (DEPTH, 2, D_MODEL, 2 * D_FF), D_MODEL ** -0.5),
        'w_ffn_dn': nrm(ks[8], (DEPTH, 2, D_FF, D_MODEL), D_FF ** -0.5),
        'w_qkv_a': nrm(ks[9], (N_A_LAYERS, D_MODEL, (H_A + 2 * KVH_A) * HEAD_DIM), D_MODEL ** -0.5),
        'sink_a': nrm(ks[10], (N_A_LAYERS, H_A), 0.5),
        'w_o_a': nrm(ks[11], (N_A_LAYERS, H_A * HEAD_DIM, D_MODEL), (H_A * HEAD_DIM) ** -0.5),
        'g_kv_b': 1.0 + nrm(ks[12], (D_MODEL,), 0.1),
        'w_kv_b': nrm(ks[13], (D_MODEL, N_BG * 2 * KVH_B * HEAD_DIM), D_MODEL ** -0.5),
        'w_q_b': nrm(ks[14], (N_B_LAYERS, D_MODEL, N_BG * H_B * HEAD_DIM), D_MODEL ** -0.5),
        'w_o_b': nrm(ks[15], (N_B_LAYERS, H_B * HEAD_DIM, D_MODEL), (H_B * HEAD_DIM) ** -0.5),
    }


def reference(x_prompt, x_sample, cache_a_kv, cache_b_kv_w128, cache_b_kv_w512, cache_b_kv_w2048,
              norm_g, w_ffn_gu, w_ffn_dn, w_qkv_a, sink_a, w_o_a, g_kv_b, w_kv_b, w_q_b, w_o_b):
    cos_p, sin_p = _rope_tables(jnp.arange(x_prompt.shape[1], dtype=jnp.float32))
    cos_s, sin_s = _rope_tables(jnp.arange(x_sample.shape[1], dtype=jnp.float32) + PAST_LEN)
    y_p, a_p, b_p = _trunk(x_prompt, cos_p, sin_p, None, None, norm_g, w_ffn_gu, w_ffn_dn,
                           w_qkv_a, sink_a, w_o_a, g_kv_b, w_kv_b, w_q_b, w_o_b)
    y_s, a_s, b_s = _trunk(x_sample, cos_s, sin_s, cache_a_kv,
                           (cache_b_kv_w128, cache_b_kv_w512, cache_b_kv_w2048),
                           norm_g, w_ffn_gu, w_ffn_dn, w_qkv_a, sink_a, w_o_a,
                           g_kv_b, w_kv_b, w_q_b, w_o_b)
    return (y_p, y_s, a_p, b_p[0], b_p[1], b_p[2], a_s, b_s[0], b_s[1], b_s[2])
```

```python
import contextlib
import math
import os
DEVK = os.environ.get('DEVK', '')
import numpy as np
import concourse.bass as bass
import concourse.mybir as mybir
from concourse.bass_utils import run_bass_kernel_spmd

F32 = mybir.dt.float32
BF16 = mybir.dt.bfloat16
AF = mybir.ActivationFunctionType
ALU = mybir.AluOpType
AX = mybir.AxisListType

NP_ = 2048
NS = 16
NT = NP_ + NS
KC = 8
HB = 22
DFF = 2816
EPS = 1e-6
PAGE = 256
DILS = (1, 4, 16)
WINS = (128, 512, 2048)
RG = [[0, 1], [2, 3], [4, 5], [6, 7]]


class Op:
    __slots__ = ("eng", "fn", "deps", "sig", "count", "dma", "dsem", "dcount", "prev_on_sem", "waits", "dinc", "pidx")

    def __init__(self, eng, fn, dma):
        self.eng = eng
        self.fn = fn
        self.dma = dma
        self.deps = []
        self.sig = False
        self.count = 0
        self.dsem = None
        self.dcount = 0
        self.prev_on_sem = None
        self.waits = []
        self.dinc = 16


class Sched:
    ENGS = ("pe", "act", "dve", "pool", "sp")

    def __init__(self, n_dma_sems=8):
        self.ops = {e: [] for e in self.ENGS}
        self.last_writer = {}
        self.readers = {}
        self.n_dma_sems = n_dma_sems
        self.dma_rr = {e: 0 for e in self.ENGS}
        self.dma_last = {}

    def add(self, eng, fn, reads=(), writes=(), dma=False, dinc=16, slot=None):
        op = Op(eng, fn, dma)
        op.dinc = dinc
        psr = [k for k in reads if isinstance(k, tuple) and k[0] == "ps"]
        if psr:
            writes = list(writes) + psr
        deps = {}
        lw = self.last_writer
        rd = self.readers
        for k in reads:
            w = lw.get(k)
            if w is not None:
                deps[id(w)] = w
        for k in writes:
            w = lw.get(k)
            if w is not None and (dma or w.dma or w.eng != eng or eng != "pe"):
                deps[id(w)] = w
            rr = rd.get(k)
            if rr:
                for r in rr:
                    if dma or r.dma or r.eng != eng or eng != "pe":
                        deps[id(r)] = r
        op.deps = list(deps.values())
        for k in writes:
            lw[k] = op
            rd[k] = []
        for k in reads:
            l = rd.get(k)
            if l is None:
                rd[k] = [op]
            elif not l or l[-1] is not op:
                l.append(op)
        if dma:
            if slot is None:
                slot = (eng, self.dma_rr[eng] % self.n_dma_sems)
                self.dma_rr[eng] += 1
            op.dsem = slot
            op.prev_on_sem = self.dma_last.get(slot)
            op.dcount = (op.prev_on_sem.dcount if op.prev_on_sem else 0) + dinc
            self.dma_last[slot] = op
        self.ops[eng].append(op)
        return op

    def dma(self, eng, out, in_, reads=(), writes=()):
        return self.add(eng, lambda e: e.dma_start(out=out, in_=in_), reads, writes, dma=True)

    def emit(self, nc, es):
        for e in self.ENGS:
            for idx, op in enumerate(self.ops[e]):
                op.pidx = idx
        for e in self.ENGS:
            seen = {}
            for op in self.ops[e]:
                best = {}
                nd = []
                for d in op.deps:
                    if d.dma:
                        nd.append(d)
                    else:
                        b = best.get(d.eng)
                        if b is None or d.pidx > b.pidx:
                            best[d.eng] = d
                for pe_, d in best.items():
                    if seen.get(pe_, -1) >= d.pidx:
                        continue
                    seen[pe_] = d.pidx
                    nd.append(d)
                op.deps = nd
        for e in self.ENGS:
            for op in self.ops[e]:
                for d in op.deps:
                    d.sig = True
        esem = {}
        for e in self.ENGS:
            esem[e] = es.enter_context(nc.semaphore("s_" + e))
            c = 0
            for op in self.ops[e]:
                if op.sig and not op.dma:
                    c += 1
                    op.count = c
        dsems = {}
        for slot in self.dma_last:
            dsems[slot] = es.enter_context(nc.semaphore("d_%s_%d" % slot))
        for e in self.ENGS:
            waited = {}
            for op in self.ops[e]:
                w = {}
                if op.dma and op.prev_on_sem is not None:
                    w[op.dsem] = ("d", op.dsem, op.prev_on_sem.dcount)
                for d in op.deps:
                    if d.dma:
                        key, val, kind = d.dsem, d.dcount, "d"
                    else:
                        key, val, kind = d.eng, d.count, "e"
                    if waited.get(key, 0) >= val:
                        continue
                    if key in w and w[key][2] >= val:
                        continue
                    w[key] = (kind, key, val)
                for key, (kind, k, val) in w.items():
                    if waited.get(key, 0) < val:
                        waited[key] = val
                op.waits = list(w.values())
        finals = [(slot, op.dcount) for slot, op in self.dma_last.items()]
        engobj = {"pe": "tensor", "act": "scalar", "dve": "vector", "pool": "gpsimd", "sp": "sync"}
        with nc.Block() as block:
            def make(ename):
                def body(eng):
                    for op in self.ops[ename]:
                        for kind, k, val in op.waits:
                            eng.wait_ge(esem[k] if kind == "e" else dsems[k], val)
                        ins = op.fn(eng)
                        if op.dma:
                            ins.then_inc(dsems[op.dsem], op.dinc)
                        elif op.sig:
                            ins.then_inc(esem[ename], 1)
                    if ename == "sp":
                        for slot, cnt in finals:
                            eng.wait_ge(dsems[slot], cnt)
                return body
            for ename in self.ENGS:
                getattr(block, engobj[ename])(make(ename))


class Buf:
    def __init__(self, arena, off, dims, dt):
        assert off % 4 == 0
        self.off = off
        self.dims = list(dims)
        self.dt = dt
        self.es = 4 if dt == F32 else 2
        n = int(np.prod(dims))
        self.nbytes = n * self.es
        v = arena[:, off // 2: off // 2 + self.nbytes // 2]
        if dt == F32:
            v = v.bitcast(F32)
        if len(dims) == 2:
            v = v.rearrange("p (a b) -> p a b", a=dims[0])
        elif len(dims) == 3:
            v = v.rearrange("p (a b c) -> p a b c", a=dims[0], b=dims[1])
        elif len(dims) == 4:
            v = v.rearrange("p (a b c d) -> p a b c d", a=dims[0], b=dims[1], c=dims[2])
        self.ap = v
        self.strides = [int(np.prod(dims[i + 1:])) for i in range(len(dims))]

    def keys(self, *idx):
        idx = list(idx) + [slice(None)] * (len(self.dims) - len(idx))
        rngs = []
        for i, ix in enumerate(idx):
            if isinstance(ix, int):
                rngs.append((ix, ix + 1))
            else:
                a, b, _ = ix.indices(self.dims[i])
                rngs.append((a, b))
        outer = [0]
        last = len(rngs) - 1
        while last > 0 and rngs[last] == (0, self.dims[last]):
            last -= 1
        for i in range(last):
            outer = [o + j * self.strides[i] for o in outer for j in range(rngs[i][0], rngs[i][1])]
        lo = rngs[last][0] * self.strides[last]
        hi = rngs[last][1] * self.strides[last]
        ks = set()
        for o in outer:
            b0 = self.off + (o + lo) * self.es
            b1 = self.off + (o + hi) * self.es
            ks.update(range(b0 // PAGE, (b1 - 1) // PAGE + 1))
        return ks

    def __call__(self, *idx, p=None):
        sl = (slice(None) if p is None else slice(p[0], p[1]),) + tuple(idx)
        return self.ap[sl], self.keys(*idx)


class Alloc:
    def __init__(self, base, limit):
        self.cur = base
        self.limit = limit

    def take(self, nbytes):
        o = self.cur
        self.cur = (o + nbytes + PAGE - 1) // PAGE * PAGE
        assert self.cur <= self.limit, (self.cur, self.limit)
        return o


def tile_cols(d, i):
    if i == 16:
        return slice(NP_, NP_ + NS), NS
    nb = 16 // d
    r, m = i // nb, i % nb
    st = r + d * 128 * m
    return slice(st, st + d * 127 + 1, d), 128


def build_program(stage=99, copies=True, sub=99, small=False):
    nc = bass.Bass("TRN2", target_bir_lowering=False)

    def din(name, shape):
        return nc.dram_tensor(name, list(shape), F32, kind="ExternalInput").ap()

    def dout(name, shape):
        return nc.dram_tensor(name, list(shape), F32, kind="ExternalOutput").ap()

    xT_d = din("xT", [1024, NT])
    wgu_d = din("w_gu", [1 if small else 4, 1024, 2 * DFF])
    wdn_d = din("w_dn", [1 if small else 4, DFF, 1024])
    wqkva_d = din("w_qkv_a", [1024, 1536])
    woa_d = din("w_o_a", [1024, 1024])
    wkvb_d = din("w_kv_b", [1024, 1536])
    wqb_d = din("w_q_b", [1024, 3072])
    wob_d = din("w_o_b", [1024, 1024])
    gains_d = din("gains", [128, 13 * 8])
    sink_d = din("sinkl", [128, 8])
    rope_d = din("rope", [3, 128, 4 * 17 * 8])
    cbf_d = din("cbf", [128, 3520])
    cf32_d = din("cf32", [128, 256])
    ca_d = din("ca", [NS, 128, 512])
    cb_d = [din("cb0", [NS, 128, 512]), din("cb1", [NS, 2 if small else 512, 512]), din("cb2", [NS, 2 if small else 2048, 512])]

    yT_d = dout("yT", [1024, NT])
    ap_d = dout("a_p", [128, 512])
    bp_d = [dout("b0_p", [128, 512]), dout("b1_p", [512, 512]), dout("b2_p", [2048, 512])]
    as_d = dout("a_s", [NS, 128, 512])
    bs_d = [dout("b0_s", [NS, 128, 512]), dout("b1_s", [NS, 512, 512]), dout("b2_s", [NS, 2048, 512])]

    def dint(name, shape, dt=BF16):
        return nc.dram_tensor(name, list(shape), dt, kind="Internal").ap()

    bnA = dint("bnA", [128, 768])
    gaA = dint("gaA", [256, 768])
    NH = 21
    NHC = 7
    bnBs = [dint("bnB%d" % k, [128, NHC * 768]) for k in range(3)]
    gaBs = [dint("gaB%d" % k, [256, NHC * 768]) for k in range(3)]
    ktsB = dint("ktsB", [128, 3 * 4 * 17 * 128])
    vtsB = dint("vtsB", [128, 3 * 4 * 17 * 64])
    vtnB = dint("vtnB", [128, 3 * 4 * 16])
    qsB = dint("qsB", [3 * 8 * 128, 17 * 128])
    qssB = dint("qssB", [3 * 16, 1024])
    ktsB4 = ktsB.rearrange("p (g k i n) -> p g k i n", g=3, k=4, i=17)
    vtsB4 = vtsB.rearrange("p (g k i n) -> p g k i n", g=3, k=4, i=17)
    vtnB3 = vtnB.rearrange("p (g k n) -> p g k n", g=3, k=4)
    bnB3s = [t.rearrange("p (h n) -> p h n", h=NHC) for t in bnBs]
    gaB3s = [t.rearrange("p (h n) -> p h n", h=NHC) for t in gaBs]
    qsB4 = qsB.rearrange("(g c p) n -> g c p n", g=3, c=8)
    qssB3 = qssB.rearrange("(g b) n -> g b n", g=3)

    S = Sched()
    es = contextlib.ExitStack()
    with es:
        SB_BYTES = 212736
        arena = es.enter_context(nc.sbuf_tensor("arena", [128, SB_BYTES // 2], BF16))
        ps = es.enter_context(nc.psum_tensor("ps", [128, 4096], F32))

        st = {"rr": 0, "held": set()}

        def bank():
            while True:
                b = st["rr"] % 8
                st["rr"] += 1
                if b not in st["held"]:
                    return b

        def psf(b, lo=0, hi=512, p=None):
            sl = slice(None) if p is None else slice(p[0], p[1])
            return ps[sl, b * 512 + lo: b * 512 + hi]

        def psb(b, p=None):
            sl = slice(None) if p is None else slice(p[0], p[1])
            return ps[sl, b * 512:(b + 1) * 512].bitcast(BF16)

        def PK(b):
            return [("ps", b)]

        XT = Buf(arena, 0, [KC, NT], F32)
        GN = Buf(arena, 66048, [13, 8], F32)
        IDN = Buf(arena, 66560, [128], BF16)
        ONE = Buf(arena, 66816, [128], BF16)
        PB = 67072

        def mm(out, lhsT, rhs, start, stop, reads, writes):
            S.add("pe", lambda e: e.matmul(out, lhsT=lhsT, rhs=rhs, start=start, stop=stop), reads, writes)

        def act(out, in_, func, reads, writes, scale=1.0, bias=0.0):
            S.add("act", lambda e: e.activation(out=out, in_=in_, func=func, scale=scale, bias=bias), reads, writes)

        def tt(eng, out, in0, in1, op, reads, writes):
            S.add(eng, lambda e: e.tensor_tensor(out=out, in0=in0, in1=in1, op=op), reads, writes)

        def stt(eng, out, in0, scalar, in1, op0, op1, reads, writes):
            S.add(eng, lambda e: e.scalar_tensor_tensor(out=out, in0=in0, scalar=scalar, in1=in1, op0=op0, op1=op1), reads, writes)

        def cp(eng, out, in_, reads, writes):
            if eng == "act":
                S.add(eng, lambda e: e.activation(out=out, in_=in_, func=AF.Copy), reads, writes)
            else:
                S.add(eng, lambda e: e.tensor_copy(out=out, in_=in_), reads, writes)

        def U(*ks):
            r = set()
            for k in ks:
                r.update(k)
            return r

        for c in range(KC):
            a, k = XT(c)
            S.dma("sp", a, xT_d[c * 128:(c + 1) * 128, :], writes=k)
        a, k = GN()
        S.dma("sp", a.rearrange("p a b -> p (a b)"), gains_d, writes=k)
        a, k = IDN()
        S.dma("pool", a, cbf_d[:, 0:128], writes=k)
        a, k = ONE()
        S.dma("pool", a, cbf_d[:, 128:256], writes=k)

        copy_jobs = []
        if copies:
            copy_jobs.append((as_d[:, 0:127, :], ca_d[:, 1:128, :]))
            copy_jobs.append((bs_d[0][:, 0:127, :], cb_d[0][:, 1:128, :]))
            for b in range(NS):
                copy_jobs.append((bs_d[1][b, 0:511, :], cb_d[1][b, 1:512, :]))
            for b in range(NS):
                copy_jobs.append((bs_d[2][b, 0:1024, :], cb_d[2][b, 1:1025, :]))
                copy_jobs.append((bs_d[2][b, 1024:2047, :], cb_d[2][b, 1025:2048, :]))

        def copy_burst(n):
            for _ in range(min(n, len(copy_jobs))):
                o_, i_ = copy_jobs.pop()
                S.dma("sp", o_, i_)

        def rstd_tile(src_fn, n, al_bufs, tag, factor=1.0):
            SQ, RS = al_bufs
            b = bank()
            for c in range(KC):
                sa, sk = src_fn(c)
                qa, qk = SQ[c % len(SQ)](slice(0, n))
                act(qa, sa, AF.Square, sk, qk)
                mm(psf(b, 0, n), ONE.ap, qa, c == 0, c == KC - 1, U(qk, ONE.keys()), PK(b))
            ra, rk = RS[st.setdefault("rs", 0) % len(RS)](slice(0, n))
            st["rs"] += 1
            act(ra, psf(b, 0, n), AF.Sqrt, PK(b), rk, scale=1.0 / (1024.0 * factor * factor), bias=EPS / (factor * factor))
            S.add("dve", lambda e: e.reciprocal(out=ra, in_=ra), rk, rk)
            return ra, rk

        def prenorm(gi, tiles, dst_fn, bufs):
            for (c0, n) in tiles:
                ra, rk = rstd_tile(lambda c: XT(c, slice(c0, c0 + n)), n, bufs, "pre")
                for c in range(KC):
                    xa, xk = XT(c, slice(c0, c0 + n))
                    da, dk = dst_fn(c, c0, n)
                    stt("dve", da, xa, GN.ap[:, gi, c:c + 1], ra, ALU.mult, ALU.mult, U(xk, rk, GN.keys()), dk)

        def postnorm_residual(gi, c0, n, y_fn, factor, bufs):
            ra, rk = rstd_tile(y_fn, n, bufs, "post", factor)
            for c in range(KC):
                ya, yk = y_fn(c)
                xa, xk = XT(c, slice(c0, c0 + n))
                stt("dve", ya, ya, GN.ap[:, gi, c:c + 1], ra, ALU.mult, ALU.mult, U(yk, rk, GN.keys()), yk)
                tt("pool", xa, xa, ya, ALU.add, U(yk, xk), xk)

        def ffn(f, gpre, gpost):
            for (h0, hn) in ((0, 1024), (1024, NT - 1024)):
                copy_burst(7)
                al = Alloc(PB, SB_BYTES)
                W = 1040
                HN = Buf(arena, al.take(KC * W * 2), [KC, W], BF16)
                ACTB = Buf(arena, al.take(HB * W * 2), [HB, W], BF16)
                YST = Buf(arena, al.take(KC * W * 4), [KC, W], F32)
                WGU = [Buf(arena, al.take(8 * 256 * 2), [8, 256], BF16) for _ in range(5)]
                WDN = [Buf(arena, al.take(HB * 128 * 2), [HB, 128], BF16) for _ in range(3)]
                SG = [Buf(arena, al.take(1024), [512], BF16) for _ in range(2)]
                SQ = [Buf(arena, al.take(1024), [512], BF16) for _ in range(4)]
                RS = [Buf(arena, al.take(2048), [512], F32) for _ in range(2)]
                tiles = []
                o = 0
                while o < hn:
                    n = min(512, hn - o)
                    tiles.append((o, n))
                    o += n
                prenorm(gpre, [(h0 + o, n) for (o, n) in tiles],
                        lambda c, c0, n: HN(c, slice(c0 - h0, c0 - h0 + n)), (SQ, RS))
                sgi = 0
                for j in range(HB):
                    wb = WGU[j % 5]
                    wa, wk = wb(slice(None), slice(0, 128))
                    S.dma("pool", wa, wgu_d[f, :, j * 128:(j + 1) * 128].rearrange("(k p) n -> p k n", p=128), writes=wk)
                    wa, wk2 = wb(slice(None), slice(128, 256))
                    S.dma("pool", wa, wgu_d[f, :, DFF + j * 128:DFF + (j + 1) * 128].rearrange("(k p) n -> p k n", p=128), writes=wk2)
                    wkk = U(wk, wk2)
                    for (o, n) in tiles:
                        bg, bu = bank(), bank()
                        for k in range(KC):
                            ha, hk = HN(k, slice(o, o + n))
                            mm(psf(bg, 0, n), wb.ap[:, k, 0:128], ha, k == 0, k == KC - 1, U(wkk, hk), PK(bg))
                        for k in range(KC):
                            ha, hk = HN(k, slice(o, o + n))
                            mm(psf(bu, 0, n), wb.ap[:, k, 128:256], ha, k == 0, k == KC - 1, U(wkk, hk), PK(bu))
                        sa, sk = SG[sgi % 2](slice(0, n))
                        sgi += 1
                        act(sa, psf(bg, 0, n), AF.Silu, PK(bg), sk)
                        aa, ak = ACTB(j, slice(o, o + n))
                        tt("dve", aa, sa, psf(bu, 0, n), ALU.mult, U(sk, PK(bu)), ak)
                for c in range(KC):
                    wb = WDN[c % 3]
                    wa, wk = wb()
                    S.dma("pool", wa, wdn_d[f, :, c * 128:(c + 1) * 128].rearrange("(j p) n -> p j n", p=128), writes=wk)
                    for (o, n) in tiles:
                        b = bank()
                        for j in range(HB):
                            aa, ak = ACTB(j, slice(o, o + n))
                            mm(psf(b, 0, n), wb.ap[:, j, :], aa, j == 0, j == HB - 1, U(wk, ak), PK(b))
                        ya, yk = YST(c, slice(o, o + n))
                        act(ya, psf(b, 0, n), AF.Copy, PK(b), yk)
                for (o, n) in tiles:
                    postnorm_residual(gpost, h0 + o, n, lambda c: YST(c, slice(o, o + n)), 0.5, (SQ, RS))

        def rope_ops(src3, skeys, M, H, dst3, dkeys, cosa, sina, tkeys, RT):
            x1 = src3[:, :, 0:8]
            x2 = src3[:, :, 8:16]
            cb = cosa.unsqueeze(1).to_broadcast([M, H, 8])
            sb_ = sina.unsqueeze(1).to_broadcast([M, H, 8])
            t = []
            for i in range(4):
                a, k = RT[i](slice(0, H), p=(0, M))
                t.append((a, k))
            rk = U(skeys, tkeys)
            tt("dve", t[0][0], x1, cb, ALU.mult, rk, t[0][1])
            tt("dve", t[1][0], x2, sb_, ALU.mult, rk, t[1][1])
            tt("dve", t[2][0], x2, cb, ALU.mult, rk, t[2][1])
            tt("dve", t[3][0], x1, sb_, ALU.mult, rk, t[3][1])
            tt("dve", dst3[:, :, 0:8], t[0][0], t[1][0], ALU.subtract, U(t[0][1], t[1][1]), dkeys)
            tt("dve", dst3[:, :, 8:16], t[2][0], t[3][0], ALU.add, U(t[2][1], t[3][1]), dkeys)

        def proc_kv(b, M, ropek, KTM, VTM, KDUP, RT, out_k_dma, out_v_dma, kt_dst, v_dst, vtn_dst=None, VDUP=None):
            ka, kk = KTM(p=(0, M))
            act(ka, psf(b, 0, 256, p=(0, M)), AF.Copy, PK(b), kk)
            cosa, sina, tkeys = ropek
            k3 = ka.rearrange("p (h d) -> p h d", h=4)
            rope_ops(k3, kk, M, 4, k3, kk, cosa, sina, tkeys, RT)
            va, vk = VTM(p=(0, M))
            act(va, psf(b, 256, 512, p=(0, M)), AF.Copy, PK(b), vk)
            for fn in out_k_dma:
                fn(ka, kk)
            for fn in out_v_dma:
                fn(va, vk)
            da, dk = KDUP(p=(0, M))
            cp("dve", da.rearrange("p (h r d) -> p h r d", h=4, r=2),
               ka.rearrange("p (h d) -> p h d", h=4).unsqueeze(2).to_broadcast([M, 4, 2, 64]), kk, dk)
            bT = bank()
            for h in range(4):
                S.add("pe", lambda e, h=h: e.transpose(out=psb(bT)[:, h * 128:h * 128 + M], in_=da[:, h * 128:(h + 1) * 128],
                                                        identity=IDN.ap[0:M, 0:M]), U(dk, IDN.keys()), PK(bT))
            kt_dst(bT)
            v_dst(va, vk)
            if vtn_dst is not None:
                da2, dk2 = VDUP(p=(0, M))
                cp("dve", da2.rearrange("p (h r d) -> p h r d", h=4, r=2),
                   va.rearrange("p (h d) -> p h d", h=4).unsqueeze(2).to_broadcast([M, 4, 2, 64]), vk, dk2)
                bT2 = bank()
                for h in range(4):
                    S.add("pe", lambda e, h=h: e.transpose(out=psb(bT2)[:, h * 128:h * 128 + M], in_=da2[:, h * 128:(h + 1) * 128],
                                                            identity=IDN.ap[0:M, 0:M]), U(dk2, IDN.keys()), PK(bT2))
                vtn_dst(bT2)

        def proc_q(b, M, ropeq, qtm, qk, RT, qt_dst, QR):
            act(qtm, psf(b, 0, 512, p=(0, M)), AF.Copy, PK(b), qk, scale=0.125)
            qra, qrk = QR(p=(0, M))
            act(qra, psf(b, 0, 512, p=(0, M)).rearrange("p (h d) -> p h d", h=8)[:, :, 0:16], AF.Copy, PK(b), qrk, scale=0.125)
            cosa, sina, tkeys = ropeq
            rope_ops(qra, qrk, M, 8, qtm.rearrange("p (h d) -> p h d", h=8), qk, cosa, sina, tkeys, RT)
            bT = bank()
            for j in range(4):
                S.add("pe", lambda e, j=j: e.transpose(out=psb(bT)[:, j * 128:j * 128 + M], in_=qtm[:, j * 128:(j + 1) * 128],
                                                        identity=IDN.ap[0:M, 0:M]), U(qk, IDN.keys()), PK(bT))
            qt_dst(bT)

        def project(HN, d, i, WG, wk):
            cols, M = tile_cols(d, i)
            b = bank()
            for k in range(KC):
                ha, hk = HN(k, cols)
                mm(psf(b, 0, 512, p=(0, M)), ha, WG.ap[:, k, :], k == 0, k == KC - 1, U(hk, wk), PK(b))
            return b, M

        def attn_unit(kt_fn, vt_fn, q_ap, q_keys, MASK, mk, ONESEL, ok, PB3, pi):
            bss = [bank(), bank()]
            for hp in range(2):
                for kb in range(2):
                    ka, kk = kt_fn(hp, kb)
                    mm(psf(bss[hp], kb * 128, kb * 128 + 128), ka, q_ap(hp), True, True, U(kk, q_keys), PK(bss[hp]))
            pa, pk = PB3[pi % len(PB3)]()
            for hp in range(2):
                act(pa[:, hp * 256:(hp + 1) * 256], psf(bss[hp], 0, 256), AF.Exp, PK(bss[hp]), pk)
            tt("pool", pa, pa, MASK, ALU.mult, U(pk, mk), pk)
            yield None
            bn = bank()
            idx = 0
            for hp in range(2):
                for kb in range(2):
                    va, vk = vt_fn(hp, kb)
                    blk = hp * 2 + kb
                    mm(psf(bn, 0, 128), va, pa[:, blk * 128:(blk + 1) * 128], idx == 0, idx == 3, U(vk, pk), PK(bn))
                    idx += 1
            idx = 0
            for hp in range(2):
                for kb in range(2):
                    blk = hp * 2 + kb
                    mm(psf(bn, 128, 256), ONESEL.ap[:, 64 - 64 * hp:192 - 64 * hp], pa[:, blk * 128:(blk + 1) * 128],
                       idx == 0, idx == 3, U(ok, pk), PK(bn))
                    idx += 1
            return bn

        class UnitPipe:
            def __init__(self):
                self.prev = None

            def _finish(self):
                g, fin = self.prev
                self.prev = None
                try:
                    next(g)
                    raise RuntimeError("unit generator did not finish")
                except StopIteration as e:
                    fin(e.value)

            def push(self, gen, fin):
                next(gen)
                if self.prev is not None:
                    self._finish()
                self.prev = (gen, fin)

            def flush(self):
                if self.prev is not None:
                    self._finish()

        def oproj(OT, wo_d, gi, al):
            WO = [Buf(arena, al.take(8 * 512 * 2), [8, 512], BF16) for _ in range(2)]
            YS = Buf(arena, al.take(KC * 512 * 4), [KC, 512], F32)
            SQ = [Buf(arena, al.take(1024), [512], BF16) for _ in range(4)]
            RS = [Buf(arena, al.take(2048), [512], F32) for _ in range(2)]
            wks = []
            for h in range(2):
                wa, wk = WO[h]()
                S.dma("pool", wa, wo_d[:, h * 512:(h + 1) * 512].rearrange("(k p) n -> p k n", p=128), writes=wk)
                wks.append(wk)
            o = 0
            while o < NT:
                n = min(512, NT - o)
                for cp_ in range(KC):
                    b = bank()
                    h, cc = cp_ // 4, cp_ % 4
                    for c in range(KC):
                        oa, okk = OT(c, slice(o, o + n))
                        mm(psf(b, 0, n), WO[h].ap[:, c, cc * 128:(cc + 1) * 128], oa, c == 0, c == KC - 1, U(okk, wks[h]), PK(b))
                    ya, yk = YS(cp_, slice(0, n))
                    act(ya, psf(b, 0, n), AF.Copy, PK(b), yk)
                postnorm_residual(gi, o, n, lambda c: YS(c, slice(0, n)), 1.0, (SQ, RS))
                o += n

        def sample_attn(al, cache_d, WR, d, QS, qsk, QTs_fn, KTN, ktnk, VTN, vtnk, sink, accum):
            TC = [Buf(arena, al.take(768 * 4), [768], F32) for _ in range(2)]
            PR = Buf(arena, al.take(1024 * 4), [16, 64], F32)
            SS = [Buf(arena, al.take(64), [16], F32) for _ in range(2)]
            PP = [Buf(arena, al.take(64), [16], F32) for _ in range(2)]
            PD = Buf(arena, al.take(128 * 4), [8, 16], F32)
            P0 = Buf(arena, al.take(128 * 4), [8, 16], F32)
            TN = Buf(arena, al.take(128 * 4), [8, 16], F32)
            TD = Buf(arena, al.take(128 * 4), [8, 16], F32)
            OH = Buf(arena, al.take(2048 * 2), [16, 128], BF16)
            CF = Buf(arena, al.take(256 * 4), [2, 128], F32)
            a, ohk = OH()
            S.dma("pool", a.rearrange("p a b -> p (a b)"), cbf_d[:, 1472:3520], writes=ohk)
            a, cfk = CF()
            S.dma("sp", a.rearrange("p a b -> p (a b)"), cf32_d, writes=cfk)
            qa, qk_ = QTs_fn()
            pa, pk = PD()
            tt("dve", pa.rearrange("p (k r) b -> p k r b", r=2), qa.rearrange("p (k r) b -> p k r b", r=2),
               KTN.unsqueeze(2).to_broadcast([128, 4, 2, 16]), ALU.mult, U(qk_, ktnk), pk)
            b0 = bank()
            mm(psf(b0, 0, 128), CF.ap[:, 0, :], pa.rearrange("p a b -> p (a b)"), True, True, U(cfk, pk), PK(b0))
            p0a, p0k = P0()
            act(p0a.rearrange("p a b -> p (a b)"), psf(b0, 0, 128), AF.Exp, PK(b0), p0k)
            bN = bank()
            st["held"].add(bN)
            for b in range(NS):
                yield None
                tc = TC[b % 2]
                ta, tk = tc()
                src_k = bass.AP(cache_d.tensor, b * WR * 512, [[d * 512, 128], [1, 256]])
                S.dma("sp", ta[:, 0:256], src_k, writes=tk)
                src_v = bass.AP(cache_d.tensor, b * WR * 512 + 256, [[d * 512, 128], [64, 4], [1, 64]])
                for r_ in range(2):
                    S.dma("sp", ta[:, 256:768].rearrange("p (k r d) -> p k r d", k=4, r=2)[:, :, r_, :], src_v, writes=tk)
                bq = [bank(), bank()]
                for h in range(2):
                    mm(psf(bq[h]), OH.ap[0:16, b, :], QS[0:16, h * 512:(h + 1) * 512], True, True, U(ohk, qsk), PK(bq[h]))
                pra, prk = PR()
                for h in range(2):
                    tt("dve", pra[:, h * 8:(h + 1) * 8, :].rearrange("p (k j) d -> p k j d", k=2),
                       psf(bq[h]).rearrange("p (k j d) -> p k j d", k=2, j=4),
                       ta[:, h * 128:(h + 1) * 128].rearrange("p (k d) -> p k d", k=2).unsqueeze(2).to_broadcast([128, 2, 4, 64]),
                       ALU.mult, U(PK(bq[h]), tk), prk)
                sa, sk = SS[b % 2]()
                S.add("dve", lambda e, sa=sa, pra=pra: e.tensor_reduce(out=sa, in_=pra, axis=AX.X, op=ALU.add), prk, sk)
                ppa, ppk = PP[b % 2]()
                act(ppa, sa, AF.Exp, sk, ppk)
                for kv in range(4):
                    mm(psf(bN, b * 16 + kv * 4, b * 16 + kv * 4 + 4), ta[:, 256 + kv * 128:256 + (kv + 1) * 128],
                       ppa[:, kv * 4:(kv + 1) * 4], True, True, U(tk, ppk), PK(bN))
                mm(psf(bN, 256 + b * 16, 256 + b * 16 + 16), CF.ap[:, 1, :], ppa, True, True, U(cfk, ppk), PK(bN))
            tna, tnk = TN()
            tda, tdk = TD()
            tt("dve", tna.rearrange("p (k r) b -> p k r b", r=2), p0a.rearrange("p (k r) b -> p k r b", r=2),
               VTN.unsqueeze(2).to_broadcast([128, 4, 2, 16]), ALU.mult, U(p0k, vtnk), tnk)
            for hp in range(2):
                pr = (64 * hp, 64 * hp + 64)
                srcN = psf(bN, 0, 256, p=pr).rearrange("p (b c r) -> p c b r", b=16, c=8)[:, :, :, hp]
                srcD = psf(bN, 256, 512, p=pr).rearrange("p (b c r) -> p c b r", b=16, c=8)[:, :, :, hp]
                tt("dve", tna[pr[0]:pr[1]], tna[pr[0]:pr[1]], srcN, ALU.add, U(tnk, PK(bN)), tnk)
                if sink is not None:
                    ska, skk = sink
                    tt("dve", tda[pr[0]:pr[1]], p0a[pr[0]:pr[1]], ska[pr[0]:pr[1]].unsqueeze(2).to_broadcast([64, 8, 16]), ALU.add, U(p0k, skk), tdk)
                    tt("dve", tda[pr[0]:pr[1]], tda[pr[0]:pr[1]], srcD, ALU.add, U(tdk, PK(bN)), tdk)
                else:
                    tt("dve", tda[pr[0]:pr[1]], p0a[pr[0]:pr[1]], srcD, ALU.add, U(p0k, PK(bN)), tdk)
            st["held"].discard(bN)
            accum(tna, tda, U(tnk, tdk))

        def load_consts(al, d_idx):
            MR = Buf(arena, al.take(1024), [512], BF16)
            MF = Buf(arena, al.take(1024), [512], BF16)
            OS = Buf(arena, al.take(384), [192], BF16)
            a, k1 = MR()
            S.dma("pool", a, cbf_d[:, 448:960], writes=k1)
            a, k2 = MF()
            S.dma("pool", a, cbf_d[:, 960:1472], writes=k2)
            a, k3 = OS()
            S.dma("pool", a, cbf_d[:, 256:448], writes=k3)
            return MR, MF, OS

        def load_rope(al, d_idx):
            RP = Buf(arena, al.take(4 * 17 * 8 * 4), [4, 17, 8], F32)
            a, k = RP()
            S.dma("sp", a.rearrange("p a b c -> p (a b c)"), rope_d[d_idx], writes=k)
            return RP, k

        ffn(0, 0, 1)

        def attention_a():
            al = Alloc(PB, SB_BYTES)
            HN = Buf(arena, al.take(KC * NT * 2), [KC, NT], BF16)
            WG = [Buf(arena, al.take(8 * 512 * 2), [8, 512], BF16) for _ in range(2)]
            QT = Buf(arena, al.take(KC * NT * 2), [KC, NT], BF16)
            KT = Buf(arena, al.take(4 * 18 * 128 * 2), [4, 18, 128], BF16)
            VT = Buf(arena, al.take(18 * 4 * 192 * 2), [18, 4, 192], BF16)
            MR, MF, OS = load_consts(al, 0)
            RP, rpk = load_rope(al, 0)
            SK = Buf(arena, al.take(32), [8], F32)
            QS = Buf(arena, al.take(2048), [1024], BF16)
            KTM = Buf(arena, al.take(1024), [256], F32)
            VTM = Buf(arena, al.take(1024), [256], F32)
            KDUP = Buf(arena, al.take(1024), [512], BF16)
            VDUP = KDUP
            VTN = Buf(arena, al.take(128), [4, 16], BF16)
            QTM = [Buf(arena, al.take(1024), [512], BF16) for _ in range(2)]
            QR = Buf(arena, al.take(512), [8, 16], F32)
            RT = [Buf(arena, al.take(256), [8, 8], F32) for _ in range(4)]
            PB3 = [Buf(arena, al.take(1024), [512], BF16) for _ in range(2)]
            FT = [Buf(arena, al.take(512), [128], F32) for _ in range(2)]
            alq = Alloc(QT.off, SB_BYTES)
            SQ = [Buf(arena, alq.take(1024), [512], BF16) for _ in range(4)]
            RS = [Buf(arena, alq.take(2048), [512], F32) for _ in range(2)]
            tiles = [(o, min(512, NT - o)) for o in range(0, NT, 512)]
            prenorm(2, tiles, lambda c, c0, n: HN(c, slice(c0, c0 + n)), (SQ, RS))
            a, vk_all = VT()
            S.add("pool", lambda e, a=a: e.memset(a, 0.0), (), vk_all)
            a, k = SK()
            S.dma("sp", a, sink_d, writes=k)
            S.add("act", lambda e, a=a: e.activation(out=a, in_=a, func=AF.Exp), k, k)
            grp_cols = [(1024, 1536), (0, 512), (512, 1024)]
            for gi, (g0, g1) in enumerate(grp_cols):
                if gi >= 1 and 'C' in DEVK:
                    continue
                wb = WG[gi % 2]
                wa, wk = wb()
                S.dma("pool", wa, wqkva_d[:, g0:g1].rearrange("(k p) n -> p k n", p=128), writes=wk)
                def tile_body(gi, i, b, M, wb=wb, wk=wk):
                    slot = i + 1
                    if gi == 0:
                        okd, ovd = [], []
                        if i == 15 and 'B' not in DEVK:
                            okd.append(lambda ka, kk: S.dma("sp", ap_d[:, 0:256], ka, reads=kk))
                            ovd.append(lambda va, vk: S.dma("sp", ap_d[:, 256:512], va, reads=vk))
                        if i == 16 and 'B' not in DEVK:
                            okd.append(lambda ka, kk: S.dma("sp", as_d[:, 127, 0:256], ka, reads=kk))
                            ovd.append(lambda va, vk: S.dma("sp", as_d[:, 127, 256:512], va, reads=vk))

                        def kt_dst(bT, slot=slot, M=M):
                            da, dk = KT(slice(None), slot, slice(0, M))
                            cp("act", da, psb(bT)[:, 0:512].rearrange("p (h n) -> p h n", h=4)[:, :, 0:M], PK(bT), dk)

                        def v_dst(va, vk, slot=slot, M=M):
                            da, dk = VT(slot, slice(None), slice(64, 128), p=(0, M))
                            cp("dve", da, va.rearrange("p (h d) -> p h d", h=4), vk, dk)

                        def vtn_dst(bT2):
                            da, dk = VTN()
                            cp("act", da, psb(bT2)[:, 0:512].rearrange("p (h n) -> p h n", h=4)[:, :, 0:16], PK(bT2), dk)
                        if 'D' not in DEVK:
                            proc_kv(b, M, (RP.ap[0:M, 0, i, :], RP.ap[0:M, 1, i, :], rpk), KTM, VTM, KDUP, RT, okd, ovd, kt_dst, v_dst,
                                    vtn_dst if i == 16 else None, VDUP)
                        if i == 15 and 'A' not in DEVK:
                            ka, kk = KT(slice(None), 16, slice(None))
                            S.dma("sp", bnA[:, 0:512].rearrange("p (h n) -> p h n", h=4), ka, reads=kk, writes=[("bnA", 0)])
                            va, vk = VT(16, slice(None), slice(64, 128))
                            S.dma("sp", bnA[:, 512:768].rearrange("p (h n) -> p h n", h=4), va, reads=vk, writes=[("bnA", 1)])
                            S.add("pool", lambda e: e.collective_compute("AllGather", ALU.bypass, replica_groups=RG, ins=[bnA], outs=[gaA]),
                                  [("bnA", 0), ("bnA", 1)], ["gaA"], dma=True, dinc=1, slot=("cc", 0))
                            ka, kk = KT(slice(None), 0, slice(None))
                            S.dma("sp", ka, gaA[0:128, 0:512].rearrange("p (h n) -> p h n", h=4), reads=["gaA"], writes=kk)
                            va, vk = VT(0, slice(None), slice(64, 128))
                            S.dma("sp", va, gaA[0:128, 512:768].rearrange("p (h n) -> p h n", h=4), reads=["gaA"], writes=vk)
                    else:
                        c0 = (gi - 1) * 4
                        if i == 16:
                            qa, qk = QS(slice(c0 * 128, c0 * 128 + 512), p=(0, M))
                        else:
                            qa, qk = QTM[i % 2]()

                        def qt_dst(bT, c0=c0, i=i, M=M):
                            da, dk = QT(slice(c0, c0 + 4), slice(i * 128, i * 128 + M))
                            cp("act", da, psb(bT)[:, 0:512].rearrange("p (h n) -> p h n", h=4)[:, :, 0:M], PK(bT), dk)
                        proc_q(b, M, (RP.ap[0:M, 0, i, :], RP.ap[0:M, 1, i, :], rpk), qa, qk, RT, qt_dst, QR)
                pend = None
                for i in range(17):
                    b, M = project(HN, 1, i, wb, wk)
                    if pend is not None:
                        tile_body(*pend)
                    pend = (gi, i, b, M)
                tile_body(*pend)
            al2 = Alloc(PB, QT.off)

            def accum(tna, tda, keys):
                S.add("dve", lambda e: e.reciprocal(out=tda, in_=tda), keys, keys)
                da, dk = QT(slice(None), slice(NP_, NT))
                tt("dve", da, tna, tda, ALU.mult, keys, dk)
            ktn_a, ktn_k = KT(slice(None), 17, slice(0, 16))
            vtn_a, vtn_k = VTN()
            qsa, qsk = QS(p=(0, 16))
            sgen = iter(())
            if sub >= 3:
                sgen = sample_attn(al2, ca_d, 128, 1, QS.ap, qsk, lambda: QT(slice(None), slice(NP_, NT)), ktn_a, ktn_k, vtn_a, vtn_k,
                                   (SK.ap, SK.keys()), accum)
            pi = 0
            pipeA = UnitPipe()
            for i in (list(range(1, 16)) + [0] if sub >= 2 else []):
                MASK = MF if i == 0 else MR
                ma, mk = MASK()
                for c in range(KC):
                    kvh = c // 2
                    qa_full, qk = QT(c, slice(i * 128, i * 128 + 128))

                    def kt_fn(hp, kb, kvh=kvh, i=i):
                        return KT(kvh, i + kb, slice(None), p=(64 * hp, 64 * hp + 64))

                    def vt_fn(hp, kb, kvh=kvh, i=i):
                        return VT(i + kb, kvh, slice(64 - 64 * hp, 192 - 64 * hp))
                    gen = attn_unit(kt_fn, vt_fn, lambda hp, q=qa_full: q[64 * hp:64 * hp + 64], qk, ma, mk, OS, OS.keys(), PB3, pi)
                    pi += 1

                    def fin(bn, c=c, qa_full=qa_full, qk=qk, fi=pi):
                        fa, fk = FT[fi % 2]()
                        S.add("dve", lambda e, fa=fa, bn=bn, c=c: e.tensor_scalar_add(out=fa, in0=psf(bn, 128, 256), scalar1=SK.ap[:, c:c + 1]), U(PK(bn), SK.keys()), fk)
                        S.add("dve", lambda e, fa=fa: e.reciprocal(out=fa, in_=fa), fk, fk)
                        tt("dve", qa_full, psf(bn, 0, 128), fa, ALU.mult, U(PK(bn), fk), qk)
                    pipeA.push(gen, fin)
                    if pi % 8 == 0:
                        next(sgen, None)
            pipeA.flush()
            for _ in sgen:
                pass
            al3 = Alloc(PB, QT.off)
            if 'E' not in DEVK:
                oproj(QT, woa_d, 3, al3)

        if stage >= 2:
            attention_a()
        if stage >= 3:
            ffn(1, 4, 5)
        if stage >= 4:
            ffn(2, 6, 7)

        HALO_IDX = {}
        hcount = 0
        for g in range(3):
            d = DILS[g]
            nb = 16 // d
            for r in range(d):
                HALO_IDX[(g, r * nb + nb - 1)] = hcount
                hcount += 1
        assert hcount == NH
        HBASE = [0, 1, 5]

        def kv_b():
            al = Alloc(PB, SB_BYTES)
            HN = Buf(arena, al.take(KC * NT * 2), [KC, NT], BF16)
            WG = [Buf(arena, al.take(8 * 512 * 2), [8, 512], BF16) for _ in range(2)]
            RPs = [load_rope(al, g) for g in range(3)]
            KTM = [Buf(arena, al.take(1024), [256], F32) for _ in range(2)]
            VTM = [Buf(arena, al.take(1024), [256], F32) for _ in range(2)]
            KDUP = Buf(arena, al.take(1024), [512], BF16)
            VDUP = Buf(arena, al.take(1024), [512], BF16)
            KST = [Buf(arena, al.take(1024), [4, 128], BF16) for _ in range(2)]
            VST = [Buf(arena, al.take(512), [256], BF16) for _ in range(2)]
            VNS = Buf(arena, al.take(128), [4, 16], BF16)
            RT = [Buf(arena, al.take(256), [8, 8], F32) for _ in range(4)]
            SQ = [Buf(arena, al.take(1024), [512], BF16) for _ in range(4)]
            RS = [Buf(arena, al.take(2048), [512], F32) for _ in range(2)]
            tiles = [(o, min(512, NT - o)) for o in range(0, NT, 512)]
            prenorm(12, tiles, lambda c, c0, n: HN(c, slice(c0, c0 + n)), (SQ, RS))
            cnt = 0
            for g in range(3):
                d = DILS[g]
                W = WINS[g]
                nb = 16 // d
                wb = WG[g % 2]
                wa, wk = wb()
                S.dma("pool", wa, wkvb_d[:, g * 512:(g + 1) * 512].rearrange("(k p) n -> p k n", p=128), writes=wk)
                RP, rpk = RPs[g]
                def tile_body(i, b, M, g=g, d=d, W=W, nb=nb, RP=RP, rpk=rpk):
                    nonlocal cnt
                    okd, ovd = [], []
                    hidx = HALO_IDX.get((g, i))
                    if hidx is not None:
                        r = i // nb
                        okd.append(lambda ka, kk, g=g, r=r, d=d: S.dma("sp", bp_d[g][r::d, 0:256], ka, reads=kk))
                        ovd.append(lambda va, vk, g=g, r=r, d=d: S.dma("sp", bp_d[g][r::d, 256:512], va, reads=vk))
                    if i == 16:
                        okd.append(lambda ka, kk, g=g, W=W: S.dma("sp", bs_d[g][:, W - 1, 0:256], ka, reads=kk))
                        ovd.append(lambda va, vk, g=g, W=W: S.dma("sp", bs_d[g][:, W - 1, 256:512], va, reads=vk))
                    kst = KST[cnt % 2]
                    vst = VST[cnt % 2]
                    ktm = KTM[cnt % 2]
                    vtm = VTM[cnt % 2]
                    cnt += 1

                    def kt_dst(bT, g=g, i=i, M=M, kst=kst, hidx=hidx):
                        da, dk = kst(slice(None), slice(0, M))
                        cp("act", da, psb(bT)[:, 0:512].rearrange("p (h n) -> p h n", h=4)[:, :, 0:M], PK(bT), dk)
                        S.dma("sp", ktsB4[:, g, :, i, 0:M], da, reads=dk, writes=[("kts", g, i)])
                        if hidx is not None:
                            S.dma("sp", bnB3s[hidx // NHC][:, hidx % NHC, 0:512].rearrange("p (h n) -> p h n", h=4), da, reads=dk, writes=[("bnB", hidx, 0)])

                    def v_dst(va, vk, g=g, i=i, M=M, vst=vst, hidx=hidx):
                        da, dk = vst(p=(0, M))
                        cp("dve", da, va, vk, dk)
                        S.dma("sp", vtsB4[0:M, g, :, i, :], da.rearrange("p (h d) -> p h d", h=4), reads=dk, writes=[("vts", g, i)])
                        if hidx is not None:
                            S.dma("sp", bnB3s[hidx // NHC][0:M, hidx % NHC, 512:768], da, reads=dk, writes=[("bnB", hidx, 1)])

                    def vtn_dst(bT2, g=g):
                        da, dk = VNS()
                        cp("act", da, psb(bT2)[:, 0:512].rearrange("p (h n) -> p h n", h=4)[:, :, 0:16], PK(bT2), dk)
                        S.dma("sp", vtnB3[:, g, :, :], da, reads=dk, writes=[("vtn", g)])
                    proc_kv(b, M, (RP.ap[0:M, 0, i, :], RP.ap[0:M, 1, i, :], rpk), ktm, vtm, KDUP, RT, okd, ovd, kt_dst, v_dst,
                            vtn_dst if i == 16 else None, VDUP)
                pend = None
                for i in range(17):
                    b, M = project(HN, d, i, wb, wk)
                    if pend is not None:
                        tile_body(*pend)
                    pend = (i, b, M)
                tile_body(*pend)
            for kx in range(3):
                S.add("pool", lambda e, kx=kx: e.collective_compute("AllGather", ALU.bypass, replica_groups=RG, ins=[bnBs[kx]], outs=[gaBs[kx]]),
                      [("bnB", h, j) for h in range(kx * NHC, (kx + 1) * NHC) for j in range(2)], [("gaB", kx)], dma=True, dinc=1, slot=("cc", 1 + kx))

        if stage >= 5:
            kv_b()

        def q_b():
            al = Alloc(PB, SB_BYTES)
            HN = Buf(arena, al.take(KC * NT * 2), [KC, NT], BF16)
            WG = [Buf(arena, al.take(8 * 512 * 2), [8, 512], BF16) for _ in range(2)]
            RPs = [load_rope(al, g) for g in range(3)]
            QTM = [Buf(arena, al.take(1024), [512], BF16) for _ in range(2)]
            QST = [Buf(arena, al.take(1024), [4, 128], BF16) for _ in range(2)]
            QR = Buf(arena, al.take(512), [8, 16], F32)
            RT = [Buf(arena, al.take(256), [8, 8], F32) for _ in range(4)]
            SQ = [Buf(arena, al.take(1024), [512], BF16) for _ in range(4)]
            RS = [Buf(arena, al.take(2048), [512], F32) for _ in range(2)]
            tiles = [(o, min(512, NT - o)) for o in range(0, NT, 512)]
            prenorm(8, tiles, lambda c, c0, n: HN(c, slice(c0, c0 + n)), (SQ, RS))
            cnt = 0
            for g in range(3):
                d = DILS[g]
                RP, rpk = RPs[g]
                for h in range(2):
                    wb = WG[cnt % 2]
                    wa, wk = wb()
                    S.dma("pool", wa, wqb_d[:, g * 1024 + h * 512:g * 1024 + (h + 1) * 512].rearrange("(k p) n -> p k n", p=128), writes=wk)
                    def tile_body(i, b, M, g=g, d=d, h=h, RP=RP, rpk=rpk):
                        nonlocal cnt
                        qa, qk = QTM[cnt % 2](p=(0, M))
                        qst = QST[cnt % 2]
                        cnt += 1

                        def qt_dst(bT, g=g, h=h, i=i, M=M, qst=qst):
                            da, dk = qst(slice(None), slice(0, M))
                            cp("act", da, psb(bT)[:, 0:512].rearrange("p (h n) -> p h n", h=4)[:, :, 0:M], PK(bT), dk)
                            S.dma("sp", qsB4[g, h * 4:h * 4 + 4, :, i * 128:i * 128 + M].rearrange("c p n -> p c n"), da, reads=dk, writes=[("qs", g, h, i)])
                        proc_q(b, M, (RP.ap[0:M, 0, i, :], RP.ap[0:M, 1, i, :], rpk), qa, qk, RT, qt_dst, QR)
                        if i == 16:
                            S.dma("sp", qssB3[g, :, h * 512:(h + 1) * 512], qa, reads=qk, writes=[("qss", g)])
                    pend = None
                    for i in range(17):
                        b, M = project(HN, d, i, wb, wk)
                        if pend is not None:
                            tile_body(*pend)
                        pend = (i, b, M)
                    tile_body(*pend)

        if stage >= 6:
            q_b()

        def halo_pieces(h0, d):
            out = []
            h = h0
            while h < h0 + d:
                kx = h // NHC
                hi = min(h0 + d, (kx + 1) * NHC)
                out.append((kx, h - kx * NHC, hi - kx * NHC, h - h0))
                h = hi
            return out

        def attention_b():
            al = Alloc(PB, SB_BYTES)
            OT = Buf(arena, al.take(KC * NT * 2), [KC, NT], BF16)
            MR, MF, OS = load_consts(al, 0)
            al_kv = al.cur
            KTB = [Buf(arena, al.take((DILS[g] + 16) * 128 * 2), [DILS[g] + 16, 128], BF16) for g in range(3)]
            VTB = [Buf(arena, al.take((DILS[g] + 16) * 192 * 2), [DILS[g] + 16, 192], BF16) for g in range(3)]
            ACN = Buf(arena, al.take(NP_ * 4), [NP_], F32)
            ACD = Buf(arena, al.take(NP_ * 4), [NP_], F32)
            QTR = [Buf(arena, al.take(16 * 128 * 2), [16 * 128], BF16) for _ in range(2)]
            PB3 = [Buf(arena, al.take(1024), [512], BF16) for _ in range(3)]
            al_tmp = al.cur
            for g in range(3):
                a, k = VTB[g]()
                S.add("pool", lambda e, a=a: e.memset(a, 0.0), (), k)
            def sample_b_gen():
                al2 = Alloc(al_tmp, SB_BYTES)
                SN = Buf(arena, al2.take(512), [8, 16], F32)
                SD = Buf(arena, al2.take(512), [8, 16], F32)
                QS = Buf(arena, al2.take(2048), [1024], BF16)
                QTS = Buf(arena, al2.take(256), [8, 16], BF16)
                KTN = Buf(arena, al2.take(128), [4, 16], BF16)
                VTN = Buf(arena, al2.take(128), [4, 16], BF16)
                base2 = al2.cur
                for g in range(3):
                    al3 = Alloc(base2, SB_BYTES)
                    qsa, qsk = QS(p=(0, 16))
                    S.dma("sp", qsa, qssB3[g], reads=[("qss", g)], writes=qsk)
                    qta, qtk = QTS()
                    S.dma("sp", qta, qsB4[g, :, :, 2048:2064].rearrange("c p n -> p c n"), reads=[("qs", g, h, 16) for h in range(2)], writes=qtk)
                    kna, knk = KTN()
                    S.dma("sp", kna, ktsB4[:, g, :, 16, 0:16], reads=[("kts", g, 16)], writes=knk)
                    vna, vnk = VTN()
                    S.dma("sp", vna, vtnB3[:, g, :, :], reads=[("vtn", g)], writes=vnk)

                    def accum(tna, tda, keys, g=g):
                        sna, snk = SN()
                        sda, sdk = SD()
                        if g == 0:
                            cp("dve", sna, tna, keys, snk)
                            cp("dve", sda, tda, keys, sdk)
                        else:
                            tt("dve", sna, sna, tna, ALU.add, U(snk, keys), snk)
                            tt("dve", sda, sda, tda, ALU.add, U(sdk, keys), sdk)
                    yield from sample_attn(al3, cb_d[g], WINS[g], DILS[g], QS.ap, qsk, lambda: QTS(), kna, knk, vna, vnk, None, accum)
                sna, snk = SN()
                sda, sdk = SD()
                S.add("dve", lambda e: e.reciprocal(out=sda, in_=sda), sdk, sdk)
                da, dk = OT(slice(None), slice(NP_, NT))
                tt("dve", da, sna, sda, ALU.mult, U(snk, sdk), dk)
            sgenB = sample_b_gen()
            pi = 0
            qcnt = 0
            pipeB = UnitPipe()
            for kvh in range(4):
                for g in range(3):
                    d = DILS[g]
                    h0 = HBASE[g]
                    for (kx, lo, hi, so) in halo_pieces(h0, d):
                        ka, kk = KTB[g](slice(so, so + hi - lo), slice(None))
                        S.dma("sp", ka, gaB3s[kx][0:128, lo:hi, kvh * 128:(kvh + 1) * 128], reads=[("gaB", kx)], writes=kk)
                        va, vk = VTB[g](slice(so, so + hi - lo), slice(64, 128))
                        S.dma("sp", va, gaB3s[kx][0:128, lo:hi, 512 + kvh * 64:512 + (kvh + 1) * 64], reads=[("gaB", kx)], writes=vk)
                    ka, kk = KTB[g](slice(d, d + 16), slice(None))
                    S.dma("sp", ka, ktsB4[:, g, kvh, 0:16, :], reads=[("kts", g, i) for i in range(16)], writes=kk)
                    va, vk = VTB[g](slice(d, d + 16), slice(64, 128))
                    S.dma("sp", va, vtsB4[:, g, kvh, 0:16, :], reads=[("vts", g, i) for i in range(16)], writes=vk)
                for c in (2 * kvh, 2 * kvh + 1):
                    for g in range(3):
                        d = DILS[g]
                        nb = 16 // d
                        qt = QTR[qcnt % 2]
                        qcnt += 1
                        qa_all, qk = qt()
                        S.dma("sp", qa_all, qsB4[g, c, :, 0:16 * 128], reads=[("qs", g, c // 4, i) for i in range(16)], writes=qk)
                        for i in range(16):
                            r, m = i // nb, i % nb
                            own = d + i
                            prev = (d + i - 1) if m >= 1 else r
                            MASK = MR if m >= 1 else MF
                            ma, mk = MASK()
                            q_i = qa_all[:, i * 128:(i + 1) * 128]

                            def kt_fn(hp, kb, g=g, own=own, prev=prev):
                                return KTB[g](own if kb else prev, slice(None), p=(64 * hp, 64 * hp + 64))

                            def vt_fn(hp, kb, g=g, own=own, prev=prev):
                                return VTB[g](own if kb else prev, slice(64 - 64 * hp, 192 - 64 * hp))
                            gen = attn_unit(kt_fn, vt_fn, lambda hp, q=q_i: q[64 * hp:64 * hp + 64], qk, ma, mk, OS, OS.keys(), PB3, pi)
                            pi += 1

                            def fin(bn, g=g, d=d, i=i):
                                cols, _ = tile_cols(d, i)
                                na, nk = ACN(cols)
                                dda, ddk = ACD(cols)
                                if g == 0:
                                    cp("dve", na, psf(bn, 0, 128), PK(bn), nk)
                                    cp("dve", dda, psf(bn, 128, 256), PK(bn), ddk)
                                else:
                                    tt("dve", na, na, psf(bn, 0, 128), ALU.add, U(nk, PK(bn)), nk)
                                    tt("dve", dda, dda, psf(bn, 128, 256), ALU.add, U(ddk, PK(bn)), ddk)
                            pipeB.push(gen, fin)
                            if pi % 8 == 0:
                                next(sgenB, None)
                    pipeB.flush()
                    for q4 in range(4):
                        sl = slice(q4 * 512, (q4 + 1) * 512)
                        dda, ddk = ACD(sl)
                        na, nk = ACN(sl)
                        S.add("dve", lambda e, dda=dda: e.reciprocal(out=dda, in_=dda), ddk, ddk)
                        oa, ok_ = OT(c, sl)
                        tt("pool", oa, na, dda, ALU.mult, U(nk, ddk), ok_)
            for _ in sgenB:
                pass
            al4 = Alloc(al_kv, al_tmp)
            oproj(OT, wob_d, 9, al4)

        if stage >= 7:
            attention_b()
        if stage >= 8:
            ffn(3, 10, 11)

        copy_burst(1000)
        for c in range(KC):
            a, k = XT(c)
            S.dma("sp", yT_d[c * 128:(c + 1) * 128, :], a, reads=k)

        S.emit(nc, es)
    return nc


def _rope_tables(core):
    half = core % 2
    inv = (500000.0 ** (-np.arange(0, 16, 2, dtype=np.float32) / 16.0)).astype(np.float32)
    out = np.zeros((3, 128, 4, 17, 8), np.float32)
    for gi, d in enumerate(DILS):
        nb = 16 // d
        for i in range(17):
            if i == 16:
                pos = np.full(128, 8192.0, np.float32)
            else:
                r, m = i // nb, i % nb
                pos = (half * 2048 + r + d * (128 * m + np.arange(128))).astype(np.float32)
            ang = pos[:, None] * inv[None, :]
            c, s = np.cos(ang).astype(np.float32), np.sin(ang).astype(np.float32)
            out[gi, :, 0, i] = c
            out[gi, :, 1, i] = s
            out[gi, :, 2, i] = c * np.float32(0.125)
            out[gi, :, 3, i] = s * np.float32(0.125)
    return out.reshape(3, 128, 4 * 17 * 8)


def _consts(core):
    cbf = np.zeros((128, 3520), np.float32)
    cbf[:, 0:128] = np.eye(128, dtype=np.float32)
    cbf[:, 128:256] = 1.0
    cbf[:, 256 + 64:256 + 128] = 1.0
    k = np.arange(128)[:, None]
    q = np.arange(128)[None, :]
    mA = (k >= q).astype(np.float32)
    mB = (k <= q).astype(np.float32)
    mr = np.concatenate([mA, mB, mA, mB], 1)
    cbf[:, 448:960] = mr
    mf = mr.copy()
    if core % 2 == 0:
        mf[:, 0:128] = 0.0
        mf[:, 256:384] = 0.0
    cbf[:, 960:1472] = mf
    oh = np.zeros((128, 16, 128), np.float32)
    for b in range(16):
        oh[b, b, :] = 1.0
    cbf[:, 1472:3520] = oh.reshape(128, 2048)
    cf = np.zeros((128, 256), np.float32)
    cf[0:64, 0:64] = 1.0
    cf[64:128, 64:128] = 1.0
    cf[:, 128:256] = 1.0
    return cbf, cf


_NC_CACHE = {}
_BUILD_ARGS = ()


def kernel(x_prompt, x_sample, cache_a_kv, cache_b_kv_w128, cache_b_kv_w512, cache_b_kv_w2048,
           norm_g, w_ffn_gu, w_ffn_dn, w_qkv_a, sink_a, w_o_a, g_kv_b, w_kv_b, w_q_b, w_o_b):
    f = lambda a: np.ascontiguousarray(np.asarray(a, dtype=np.float32))
    x_prompt, x_sample = f(x_prompt), f(x_sample)
    gains = np.concatenate([f(norm_g).reshape(12, 1024), f(g_kv_b).reshape(1, 1024)], 0)
    gains_l = np.ascontiguousarray(gains.reshape(13, 8, 128).transpose(2, 0, 1).reshape(128, 104))
    sk = f(sink_a).reshape(16)
    sinkl = np.zeros((128, 8), np.float32)
    for c in range(8):
        sinkl[0:64, c] = sk[2 * c]
        sinkl[64:128, c] = sk[2 * c + 1]
    shared = {
        "w_gu": f(w_ffn_gu).reshape(4, 1024, 2 * DFF), "w_dn": f(w_ffn_dn).reshape(4, DFF, 1024),
        "w_qkv_a": f(w_qkv_a).reshape(1024, 1536), "w_o_a": f(w_o_a).reshape(1024, 1024),
        "w_kv_b": f(w_kv_b), "w_q_b": f(w_q_b).reshape(1024, 3072), "w_o_b": f(w_o_b).reshape(1024, 1024),
        "gains": gains_l, "sinkl": sinkl,
    }
    ca = f(cache_a_kv).reshape(128, 128, 512)
    cb = [f(cache_b_kv_w128).reshape(128, 128, 512), f(cache_b_kv_w512).reshape(128, 512, 512),
          f(cache_b_kv_w2048).reshape(128, 2048, 512)]
    in_maps = []
    for core in range(8):
        b, half = core // 2, core % 2
        xt = np.empty((1024, NT), np.float32)
        xt[:, 0:NP_] = x_prompt[b, half * NP_:(half + 1) * NP_, :].T
        xt[:, NP_:NT] = x_sample[core * NS:(core + 1) * NS, 0, :].T
        cbf, cf = _consts(core)
        m = dict(shared)
        m.update({"xT": xt, "rope": _rope_tables(core), "cbf": cbf, "cf32": cf,
                  "ca": np.ascontiguousarray(ca[core * NS:(core + 1) * NS]),
                  "cb0": np.ascontiguousarray(cb[0][core * NS:(core + 1) * NS]),
                  "cb1": np.ascontiguousarray(cb[1][core * NS:(core + 1) * NS]),
                  "cb2": np.ascontiguousarray(cb[2][core * NS:(core + 1) * NS])})
        in_maps.append(m)
    if "nc" not in _NC_CACHE:
        _NC_CACHE["nc"] = build_program(*_BUILD_ARGS)
    if len(_BUILD_ARGS) > 3 and _BUILD_ARGS[3]:
        for m in in_maps:
            m["w_gu"] = np.ascontiguousarray(m["w_gu"][0:1]); m["w_dn"] = np.ascontiguousarray(m["w_dn"][0:1])
            m["cb1"] = np.ascontiguousarray(m["cb1"][:, 0:2]); m["cb2"] = np.ascontiguousarray(m["cb2"][:, 0:2])
    res = run_bass_kernel_spmd(_NC_CACHE["nc"], in_maps, core_ids=list(range(8)))
    R = res.results
    y_p = np.empty((4, 4096, 1024), np.float32)
    y_s = np.empty((128, 1, 1024), np.float32)
    for core in range(8):
        b, half = core // 2, core % 2
        yt = R[core]["yT"]
        y_p[b, half * NP_:(half + 1) * NP_, :] = yt[:, 0:NP_].T
        y_s[core * NS:(core + 1) * NS, 0, :] = yt[:, NP_:NT].T
    a_p = np.stack([R[2 * b + 1]["a_p"].reshape(128, 2, 4, 64) for b in range(4)], 0)[None]
    b_p = [np.stack([R[2 * b + 1]["b%d_p" % g].reshape(WINS[g], 2, 4, 64) for b in range(4)], 0) for g in range(3)]
    a_s = np.concatenate([R[c]["a_s"].reshape(NS, 128, 2, 4, 64) for c in range(8)], 0)[None]
    b_s = [np.concatenate([R[c]["b%d_s" % g].reshape(NS, WINS[g], 2, 4, 64) for c in range(8)], 0) for g in range(3)]
    return (y_p, y_s, np.ascontiguousarray(a_p), b_p[0], b_p[1], b_p[2], np.ascontiguousarray(a_s), b_s[0], b_s[1], b_s[2])
```

```python
import contextlib
import math
import os
DEVK = os.environ.get('DEVK', '')
import numpy as np
import concourse.bass as bass
import concourse.mybir as mybir
from concourse.bass_utils import run_bass_kernel_spmd

F32 = mybir.dt.float32
BF16 = mybir.dt.bfloat16
AF = mybir.ActivationFunctionType
ALU = mybir.AluOpType
AX = mybir.AxisListType

NP_ = 2048
NS = 16
NT = NP_ + NS
KC = 8
HB = 22
DFF = 2816
EPS = 1e-6
PAGE = 256
DILS = (1, 4, 16)
WINS = (128, 512, 2048)
RG = [[0, 1], [2, 3], [4, 5], [6, 7]]


class Op:
    __slots__ = ("eng", "fn", "deps", "sig", "count", "dma", "dsem", "dcount", "prev_on_sem", "waits", "dinc", "pidx")

    def __init__(self, eng, fn, dma):
        self.eng = eng
        self.fn = fn
        self.dma = dma
        self.deps = []
        self.sig = False
        self.count = 0
        self.dsem = None
        self.dcount = 0
        self.prev_on_sem = None
        self.waits = []
        self.dinc = 16


class Sched:
    ENGS = ("pe", "act", "dve", "pool", "sp")

    def __init__(self, n_dma_sems=8):
        self.ops = {e: [] for e in self.ENGS}
        self.last_writer = {}
        self.readers = {}
        self.n_dma_sems = n_dma_sems
        self.dma_rr = {e: 0 for e in self.ENGS}
        self.dma_last = {}

    def add(self, eng, fn, reads=(), writes=(), dma=False, dinc=16, slot=None):
        op = Op(eng, fn, dma)
        op.dinc = dinc
        psr = [k for k in reads if isinstance(k, tuple) and k[0] == "ps"]
        if psr:
            writes = list(writes) + psr
        deps = {}
        lw = self.last_writer
        rd = self.readers
        for k in reads:
            w = lw.get(k)
            if w is not None:
                deps[id(w)] = w
        for k in writes:
            w = lw.get(k)
            if w is not None and (dma or w.dma or w.eng != eng or eng != "pe"):
                deps[id(w)] = w
            rr = rd.get(k)
            if rr:
                for r in rr:
                    if dma or r.dma or r.eng != eng or eng != "pe":
                        deps[id(r)] = r
        op.deps = list(deps.values())
        for k in writes:
            lw[k] = op
            rd[k] = []
        for k in reads:
            l = rd.get(k)
            if l is None:
                rd[k] = [op]
            elif not l or l[-1] is not op:
                l.append(op)
        if dma:
            if slot is None:
                slot = (eng, self.dma_rr[eng] % self.n_dma_sems)
                self.dma_rr[eng] += 1
            op.dsem = slot
            op.prev_on_sem = self.dma_last.get(slot)
            op.dcount = (op.prev_on_sem.dcount if op.prev_on_sem else 0) + dinc
            self.dma_last[slot] = op
        self.ops[eng].append(op)
        return op

    def dma(self, eng, out, in_, reads=(), writes=()):
        return self.add(eng, lambda e: e.dma_start(out=out, in_=in_), reads, writes, dma=True)

    def emit(self, nc, es):
        for e in self.ENGS:
            for idx, op in enumerate(self.ops[e]):
                op.pidx = idx
        for e in self.ENGS:
            seen = {}
            for op in self.ops[e]:
                best = {}
                nd = []
                for d in op.deps:
                    if d.dma:
                        nd.append(d)
                    else:
                        b = best.get(d.eng)
                        if b is None or d.pidx > b.pidx:
                            best[d.eng] = d
                for pe_, d in best.items():
                    if seen.get(pe_, -1) >= d.pidx:
                        continue
                    seen[pe_] = d.pidx
                    nd.append(d)
                op.deps = nd
        for e in self.ENGS:
            for op in self.ops[e]:
                for d in op.deps:
                    d.sig = True
        esem = {}
        for e in self.ENGS:
            esem[e] = es.enter_context(nc.semaphore("s_" + e))
            c = 0
            for op in self.ops[e]:
                if op.sig and not op.dma:
                    c += 1
                    op.count = c
        dsems = {}
        for slot in self.dma_last:
            dsems[slot] = es.enter_context(nc.semaphore("d_%s_%d" % slot))
        for e in self.ENGS:
            waited = {}
            for op in self.ops[e]:
                w = {}
                if op.dma and op.prev_on_sem is not None:
                    w[op.dsem] = ("d", op.dsem, op.prev_on_sem.dcount)
                for d in op.deps:
                    if d.dma:
                        key, val, kind = d.dsem, d.dcount, "d"
                    else:
                        key, val, kind = d.eng, d.count, "e"
                    if waited.get(key, 0) >= val:
                        continue
                    if key in w and w[key][2] >= val:
                        continue
                    w[key] = (kind, key, val)
                for key, (kind, k, val) in w.items():
                    if waited.get(key, 0) < val:
                        waited[key] = val
                op.waits = list(w.values())
        finals = [(slot, op.dcount) for slot, op in self.dma_last.items()]
        engobj = {"pe": "tensor", "act": "scalar", "dve": "vector", "pool": "gpsimd", "sp": "sync"}
        with nc.Block() as block:
            def make(ename):
                def body(eng):
                    for op in self.ops[ename]:
                        for kind, k, val in op.waits:
                            eng.wait_ge(esem[k] if kind == "e" else dsems[k], val)
                        ins = op.fn(eng)
                        if op.dma:
                            ins.then_inc(dsems[op.dsem], op.dinc)
                        elif op.sig:
                            ins.then_inc(esem[ename], 1)
                    if ename == "sp":
                        for slot, cnt in finals:
                            eng.wait_ge(dsems[slot], cnt)
                return body
            for ename in self.ENGS:
                getattr(block, engobj[ename])(make(ename))


class Buf:
    def __init__(self, arena, off, dims, dt):
        assert off % 4 == 0
        self.off = off
        self.dims = list(dims)
        self.dt = dt
        self.es = 4 if dt == F32 else 2
        n = int(np.prod(dims))
        self.nbytes = n * self.es
        v = arena[:, off // 2: off // 2 + self.nbytes // 2]
        if dt == F32:
            v = v.bitcast(F32)
        if len(dims) == 2:
            v = v.rearrange("p (a b) -> p a b", a=dims[0])
        elif len(dims) == 3:
            v = v.rearrange("p (a b c) -> p a b c", a=dims[0], b=dims[1])
        elif len(dims) == 4:
            v = v.rearrange("p (a b c d) -> p a b c d", a=dims[0], b=dims[1], c=dims[2])
        self.ap = v
        self.strides = [int(np.prod(dims[i + 1:])) for i in range(len(dims))]

    def keys(self, *idx):
        idx = list(idx) + [slice(None)] * (len(self.dims) - len(idx))
        rngs = []
        for i, ix in enumerate(idx):
            if isinstance(ix, int):
                rngs.append((ix, ix + 1))
            else:
                a, b, _ = ix.indices(self.dims[i])
                rngs.append((a, b))
        outer = [0]
        last = len(rngs) - 1
        while last > 0 and rngs[last] == (0, self.dims[last]):
            last -= 1
        for i in range(last):
            outer = [o + j * self.strides[i] for o in outer for j in range(rngs[i][0], rngs[i][1])]
        lo = rngs[last][0] * self.strides[last]
        hi = rngs[last][1] * self.strides[last]
        ks = set()
        for o in outer:
            b0 = self.off + (o + lo) * self.es
            b1 = self.off + (o + hi) * self.es
            ks.update(range(b0 // PAGE, (b1 - 1) // PAGE + 1))
        return ks

    def __call__(self, *idx, p=None):
        sl = (slice(None) if p is None else slice(p[0], p[1]),) + tuple(idx)
        return self.ap[sl], self.keys(*idx)


class Alloc:
    def __init__(self, base, limit):
        self.cur = base
        self.limit = limit

    def take(self, nbytes):
        o = self.cur
        self.cur = (o + nbytes + PAGE - 1) // PAGE * PAGE
        assert self.cur <= self.limit, (self.cur, self.limit)
        return o


def tile_cols(d, i):
    if i == 16:
        return slice(NP_, NP_ + NS), NS
    nb = 16 // d
    r, m = i // nb, i % nb
    st = r + d * 128 * m
    return slice(st, st + d * 127 + 1, d), 128


def build_program(stage=99, copies=True, sub=99, small=False):
    nc = bass.Bass("TRN2", target_bir_lowering=False)

    def din(name, shape):
        return nc.dram_tensor(name, list(shape), F32, kind="ExternalInput").ap()

    def dout(name, shape):
        return nc.dram_tensor(name, list(shape), F32, kind="ExternalOutput").ap()

    xT_d = din("xT", [1024, NT])
    wgu_d = din("w_gu", [1 if small else 4, 1024, 2 * DFF])
    wdn_d = din("w_dn", [1 if small else 4, DFF, 1024])
    wqkva_d = din("w_qkv_a", [1024, 1536])
    woa_d = din("w_o_a", [1024, 1024])
    wkvb_d = din("w_kv_b", [1024, 1536])
    wqb_d = din("w_q_b", [1024, 3072])
    wob_d = din("w_o_b", [1024, 1024])
    gains_d = din("gains", [128, 13 * 8])
    sink_d = din("sinkl", [128, 8])
    rope_d = din("rope", [3, 128, 4 * 17 * 8])
    cbf_d = din("cbf", [128, 3520])
    cf32_d = din("cf32", [128, 256])
    ca_d = din("ca", [NS, 128, 512])
    cb_d = [din("cb0", [NS, 128, 512]), din("cb1", [NS, 2 if small else 512, 512]), din("cb2", [NS, 2 if small else 2048, 512])]

    yT_d = dout("yT", [1024, NT])
    ap_d = dout("a_p", [128, 512])
    bp_d = [dout("b0_p", [128, 512]), dout("b1_p", [512, 512]), dout("b2_p", [2048, 512])]
    as_d = dout("a_s", [NS, 128, 512])
    bs_d = [dout("b0_s", [NS, 128, 512]), dout("b1_s", [NS, 512, 512]), dout("b2_s", [NS, 2048, 512])]

    def dint(name, shape, dt=BF16):
        return nc.dram_tensor(name, list(shape), dt, kind="Internal").ap()

    bnA = dint("bnA", [128, 768])
    gaA = dint("gaA", [256, 768])
    NH = 21
    NHC = 7
    bnBs = [dint("bnB%d" % k, [128, NHC * 768]) for k in range(3)]
    gaBs = [dint("gaB%d" % k, [256, NHC * 768]) for k in range(3)]
    ktsB = dint("ktsB", [128, 3 * 4 * 17 * 128])
    vtsB = dint("vtsB", [128, 3 * 4 * 17 * 64])
    vtnB = dint("vtnB", [128, 3 * 4 * 16])
    qsB = dint("qsB", [3 * 8 * 128, 17 * 128])
    qssB = dint("qssB", [3 * 16, 1024])
    ktsB4 = ktsB.rearrange("p (g k i n) -> p g k i n", g=3, k=4, i=17)
    vtsB4 = vtsB.rearrange("p (g k i n) -> p g k i n", g=3, k=4, i=17)
    vtnB3 = vtnB.rearrange("p (g k n) -> p g k n", g=3, k=4)
    bnB3s = [t.rearrange("p (h n) -> p h n", h=NHC) for t in bnBs]
    gaB3s = [t.rearrange("p (h n) -> p h n", h=NHC) for t in gaBs]
    qsB4 = qsB.rearrange("(g c p) n -> g c p n", g=3, c=8)
    qssB3 = qssB.rearrange("(g b) n -> g b n", g=3)

    S = Sched()
    es = contextlib.ExitStack()
    with es:
        SB_BYTES = 212736
        arena = es.enter_context(nc.sbuf_tensor("arena", [128, SB_BYTES // 2], BF16))
        ps = es.enter_context(nc.psum_tensor("ps", [128, 4096], F32))

        st = {"rr": 0, "held": set()}

        def bank():
            while True:
                b = st["rr"] % 8
                st["rr"] += 1
                if b not in st["held"]:
                    return b

        def psf(b, lo=0, hi=512, p=None):
            sl = slice(None) if p is None else slice(p[0], p[1])
            return ps[sl, b * 512 + lo: b * 512 + hi]

        def psb(b, p=None):
            sl = slice(None) if p is None else slice(p[0], p[1])
            return ps[sl, b * 512:(b + 1) * 512].bitcast(BF16)

        def PK(b):
            return [("ps", b)]

        XT = Buf(arena, 0, [KC, NT], F32)
        GN = Buf(arena, 66048, [13, 8], F32)
        IDN = Buf(arena, 66560, [128], BF16)
        ONE = Buf(arena, 66816, [128], BF16)
        PB = 67072

        def mm(out, lhsT, rhs, start, stop, reads, writes):
            S.add("pe", lambda e: e.matmul(out, lhsT=lhsT, rhs=rhs, start=start, stop=stop), reads, writes)

        def act(out, in_, func, reads, writes, scale=1.0, bias=0.0):
            S.add("act", lambda e: e.activation(out=out, in_=in_, func=func, scale=scale, bias=bias), reads, writes)

        def tt(eng, out, in0, in1, op, reads, writes):
            S.add(eng, lambda e: e.tensor_tensor(out=out, in0=in0, in1=in1, op=op), reads, writes)

        def stt(eng, out, in0, scalar, in1, op0, op1, reads, writes):
            S.add(eng, lambda e: e.scalar_tensor_tensor(out=out, in0=in0, scalar=scalar, in1=in1, op0=op0, op1=op1), reads, writes)

        def cp(eng, out, in_, reads, writes):
            if eng == "act":
                S.add(eng, lambda e: e.activation(out=out, in_=in_, func=AF.Copy), reads, writes)
            else:
                S.add(eng, lambda e: e.tensor_copy(out=out, in_=in_), reads, writes)

        def U(*ks):
            r = set()
            for k in ks:
                r.update(k)
            return r

        for c in range(KC):
            a, k = XT(c)
            S.dma("sp", a, xT_d[c * 128:(c + 1) * 128, :], writes=k)
        a, k = GN()
        S.dma("sp", a.rearrange("p a b -> p (a b)"), gains_d, writes=k)
        a, k = IDN()
        S.dma("pool", a, cbf_d[:, 0:128], writes=k)
        a, k = ONE()
        S.dma("pool", a, cbf_d[:, 128:256], writes=k)

        copy_jobs = []
        if copies:
            copy_jobs.append((as_d[:, 0:127, :], ca_d[:, 1:128, :]))
            copy_jobs.append((bs_d[0][:, 0:127, :], cb_d[0][:, 1:128, :]))
            for b in range(NS):
                copy_jobs.append((bs_d[1][b, 0:511, :], cb_d[1][b, 1:512, :]))
            for b in range(NS):
                copy_jobs.append((bs_d[2][b, 0:1024, :], cb_d[2][b, 1:1025, :]))
                copy_jobs.append((bs_d[2][b, 1024:2047, :], cb_d[2][b, 1025:2048, :]))

        def copy_burst(n):
            for _ in range(min(n, len(copy_jobs))):
                o_, i_ = copy_jobs.pop()
                S.dma("sp", o_, i_)

        def rstd_tile(src_fn, n, al_bufs, tag, factor=1.0):
            SQ, RS = al_bufs
            b = bank()
            for c in range(KC):
                sa, sk = src_fn(c)
                qa, qk = SQ[c % len(SQ)](slice(0, n))
                act(qa, sa, AF.Square, sk, qk)
                mm(psf(b, 0, n), ONE.ap, qa, c == 0, c == KC - 1, U(qk, ONE.keys()), PK(b))
            ra, rk = RS[st.setdefault("rs", 0) % len(RS)](slice(0, n))
            st["rs"] += 1
            act(ra, psf(b, 0, n), AF.Sqrt, PK(b), rk, scale=1.0 / (1024.0 * factor * factor), bias=EPS / (factor * factor))
            S.add("dve", lambda e: e.reciprocal(out=ra, in_=ra), rk, rk)
            return ra, rk

        def prenorm(gi, tiles, dst_fn, bufs):
            for (c0, n) in tiles:
                ra, rk = rstd_tile(lambda c: XT(c, slice(c0, c0 + n)), n, bufs, "pre")
                for c in range(KC):
                    xa, xk = XT(c, slice(c0, c0 + n))
                    da, dk = dst_fn(c, c0, n)
                    stt("dve", da, xa, GN.ap[:, gi, c:c + 1], ra, ALU.mult, ALU.mult, U(xk, rk, GN.keys()), dk)

        def postnorm_residual(gi, c0, n, y_fn, factor, bufs):
            ra, rk = rstd_tile(y_fn, n, bufs, "post", factor)
            for c in range(KC):
                ya, yk = y_fn(c)
                xa, xk = XT(c, slice(c0, c0 + n))
                stt("dve", ya, ya, GN.ap[:, gi, c:c + 1], ra, ALU.mult, ALU.mult, U(yk, rk, GN.keys()), yk)
                tt("pool", xa, xa, ya, ALU.add, U(yk, xk), xk)

        def ffn(f, gpre, gpost):
            for (h0, hn) in ((0, 1032), (1032, NT - 1032)):
                copy_burst(7)
                al = Alloc(PB, SB_BYTES)
                W = 1040
                HN = Buf(arena, al.take(KC * W * 2), [KC, W], BF16)
                ACTB = Buf(arena, al.take(HB * W * 2), [HB, W], BF16)
                YST = Buf(arena, al.take(KC * W * 4), [KC, W], F32)
                WGU = [Buf(arena, al.take(8 * 256 * 2), [8, 256], BF16) for _ in range(5)]
                WDN = [Buf(arena, al.take(HB * 128 * 2), [HB, 128], BF16) for _ in range(3)]
                SG = [Buf(arena, al.take(1024), [512], BF16) for _ in range(2)]
                SQ = [Buf(arena, al.take(1024), [512], BF16) for _ in range(4)]
                RS = [Buf(arena, al.take(2048), [512], F32) for _ in range(2)]
                tiles = []
                o = 0
                while o < hn:
                    n = min(344, hn - o)
                    tiles.append((o, n))
                    o += n
                prenorm(gpre, [(h0 + o, n) for (o, n) in tiles],
                        lambda c, c0, n: HN(c, slice(c0 - h0, c0 - h0 + n)), (SQ, RS))
                sgi = 0
                for j in range(HB):
                    wb = WGU[j % 5]
                    wa, wk = wb(slice(None), slice(0, 128))
                    S.dma("pool", wa, wgu_d[f, :, j * 128:(j + 1) * 128].rearrange("(k p) n -> p k n", p=128), writes=wk)
                    wa, wk2 = wb(slice(None), slice(128, 256))
                    S.dma("pool", wa, wgu_d[f, :, DFF + j * 128:DFF + (j + 1) * 128].rearrange("(k p) n -> p k n", p=128), writes=wk2)
                    wkk = U(wk, wk2)
                    for (o, n) in tiles:
                        bg, bu = bank(), bank()
                        for k in range(KC):
                            ha, hk = HN(k, slice(o, o + n))
                            mm(psf(bg, 0, n), wb.ap[:, k, 0:128], ha, k == 0, k == KC - 1, U(wkk, hk), PK(bg))
                        for k in range(KC):
                            ha, hk = HN(k, slice(o, o + n))
                            mm(psf(bu, 0, n), wb.ap[:, k, 128:256], ha, k == 0, k == KC - 1, U(wkk, hk), PK(bu))
                        sa, sk = SG[sgi % 2](slice(0, n))
                        sgi += 1
                        act(sa, psf(bg, 0, n), AF.Silu, PK(bg), sk)
                        aa, ak = ACTB(j, slice(o, o + n))
                        tt("dve", aa, sa, psf(bu, 0, n), ALU.mult, U(sk, PK(bu)), ak)
                for c in range(KC):
                    wb = WDN[c % 3]
                    wa, wk = wb()
                    S.dma("pool", wa, wdn_d[f, :, c * 128:(c + 1) * 128].rearrange("(j p) n -> p j n", p=128), writes=wk)
                    for (o, n) in tiles:
                        b = bank()
                        for j in range(HB):
                            aa, ak = ACTB(j, slice(o, o + n))
                            mm(psf(b, 0, n), wb.ap[:, j, :], aa, j == 0, j == HB - 1, U(wk, ak), PK(b))
                        ya, yk = YST(c, slice(o, o + n))
                        act(ya, psf(b, 0, n), AF.Copy, PK(b), yk)
                for (o, n) in tiles:
                    postnorm_residual(gpost, h0 + o, n, lambda c: YST(c, slice(o, o + n)), 0.5, (SQ, RS))

        def rope_ops(src3, skeys, M, H, dst3, dkeys, cosa, sina, tkeys, RT):
            x1 = src3[:, :, 0:8]
            x2 = src3[:, :, 8:16]
            cb = cosa.unsqueeze(1).to_broadcast([M, H, 8])
            sb_ = sina.unsqueeze(1).to_broadcast([M, H, 8])
            t = []
            for i in range(4):
                a, k = RT[i](slice(0, H), p=(0, M))
                t.append((a, k))
            rk = U(skeys, tkeys)
            tt("dve", t[0][0], x1, cb, ALU.mult, rk, t[0][1])
            tt("dve", t[1][0], x2, sb_, ALU.mult, rk, t[1][1])
            tt("dve", t[2][0], x2, cb, ALU.mult, rk, t[2][1])
            tt("dve", t[3][0], x1, sb_, ALU.mult, rk, t[3][1])
            tt("dve", dst3[:, :, 0:8], t[0][0], t[1][0], ALU.subtract, U(t[0][1], t[1][1]), dkeys)
            tt("dve", dst3[:, :, 8:16], t[2][0], t[3][0], ALU.add, U(t[2][1], t[3][1]), dkeys)

        def proc_kv(b, M, ropek, KTM, VTM, KDUP, RT, out_k_dma, out_v_dma, kt_dst, v_dst, vtn_dst=None, VDUP=None):
            ka, kk = KTM(p=(0, M))
            act(ka, psf(b, 0, 256, p=(0, M)), AF.Copy, PK(b), kk)
            cosa, sina, tkeys = ropek
            k3 = ka.rearrange("p (h d) -> p h d", h=4)
            rope_ops(k3, kk, M, 4, k3, kk, cosa, sina, tkeys, RT)
            va, vk = VTM(p=(0, M))
            act(va, psf(b, 256, 512, p=(0, M)), AF.Copy, PK(b), vk)
            for fn in out_k_dma:
                fn(ka, kk)
            for fn in out_v_dma:
                fn(va, vk)
            da, dk = KDUP(p=(0, M))
            cp("dve", da.rearrange("p (h r d) -> p h r d", h=4, r=2),
               ka.rearrange("p (h d) -> p h d", h=4).unsqueeze(2).to_broadcast([M, 4, 2, 64]), kk, dk)
            bT = bank()
            for h in range(4):
                S.add("pe", lambda e, h=h: e.transpose(out=psb(bT)[:, h * 128:h * 128 + M], in_=da[:, h * 128:(h + 1) * 128],
                                                        identity=IDN.ap[0:M, 0:M]), U(dk, IDN.keys()), PK(bT))
            kt_dst(bT)
            v_dst(va, vk)
            if vtn_dst is not None:
                da2, dk2 = VDUP(p=(0, M))
                cp("dve", da2.rearrange("p (h r d) -> p h r d", h=4, r=2),
                   va.rearrange("p (h d) -> p h d", h=4).unsqueeze(2).to_broadcast([M, 4, 2, 64]), vk, dk2)
                bT2 = bank()
                for h in range(4):
                    S.add("pe", lambda e, h=h: e.transpose(out=psb(bT2)[:, h * 128:h * 128 + M], in_=da2[:, h * 128:(h + 1) * 128],
                                                            identity=IDN.ap[0:M, 0:M]), U(dk2, IDN.keys()), PK(bT2))
                vtn_dst(bT2)

        def proc_q(b, M, ropeq, qtm, qk, RT, qt_dst, QR):
            act(qtm, psf(b, 0, 512, p=(0, M)), AF.Copy, PK(b), qk, scale=0.125)
            qra, qrk = QR(p=(0, M))
            act(qra, psf(b, 0, 512, p=(0, M)).rearrange("p (h d) -> p h d", h=8)[:, :, 0:16], AF.Copy, PK(b), qrk, scale=0.125)
            cosa, sina, tkeys = ropeq
            rope_ops(qra, qrk, M, 8, qtm.rearrange("p (h d) -> p h d", h=8), qk, cosa, sina, tkeys, RT)
            bT = bank()
            for j in range(4):
                S.add("pe", lambda e, j=j: e.transpose(out=psb(bT)[:, j * 128:j * 128 + M], in_=qtm[:, j * 128:(j + 1) * 128],
                                                        identity=IDN.ap[0:M, 0:M]), U(qk, IDN.keys()), PK(bT))
            qt_dst(bT)

        def project(HN, d, i, WG, wk):
            cols, M = tile_cols(d, i)
            b = bank()
            for k in range(KC):
                ha, hk = HN(k, cols)
                mm(psf(b, 0, 512, p=(0, M)), ha, WG.ap[:, k, :], k == 0, k == KC - 1, U(hk, wk), PK(b))
            return b, M

        def attn_unit(kt_fn, vt_fn, q_ap, q_keys, MASK, mk, ONESEL, ok, PB3, pi):
            bss = [bank(), bank()]
            for hp in range(2):
                for kb in range(2):
                    ka, kk = kt_fn(hp, kb)
                    mm(psf(bss[hp], kb * 128, kb * 128 + 128), ka, q_ap(hp), True, True, U(kk, q_keys), PK(bss[hp]))
            pa, pk = PB3[pi % len(PB3)]()
            for hp in range(2):
                act(pa[:, hp * 256:(hp + 1) * 256], psf(bss[hp], 0, 256), AF.Exp, PK(bss[hp]), pk)
            tt("pool", pa, pa, MASK, ALU.mult, U(pk, mk), pk)
            yield None
            bn = bank()
            idx = 0
            for hp in range(2):
                for kb in range(2):
                    va, vk = vt_fn(hp, kb)
                    blk = hp * 2 + kb
                    mm(psf(bn, 0, 128), va, pa[:, blk * 128:(blk + 1) * 128], idx == 0, idx == 3, U(vk, pk), PK(bn))
                    idx += 1
            idx = 0
            for hp in range(2):
                for kb in range(2):
                    blk = hp * 2 + kb
                    mm(psf(bn, 128, 256), ONESEL.ap[:, 64 - 64 * hp:192 - 64 * hp], pa[:, blk * 128:(blk + 1) * 128],
                       idx == 0, idx == 3, U(ok, pk), PK(bn))
                    idx += 1
            return bn

        class UnitPipe:
            def __init__(self):
                self.prev = None

            def _finish(self):
                g, fin = self.prev
                self.prev = None
                try:
                    next(g)
                    raise RuntimeError("unit generator did not finish")
                except StopIteration as e:
                    fin(e.value)

            def push(self, gen, fin):
                next(gen)
                if self.prev is not None:
                    self._finish()
                self.prev = (gen, fin)

            def flush(self):
                if self.prev is not None:
                    self._finish()

        def oproj(OT, wo_d, gi, al):
            WO = [Buf(arena, al.take(8 * 512 * 2), [8, 512], BF16) for _ in range(2)]
            YS = Buf(arena, al.take(KC * 512 * 4), [KC, 512], F32)
            SQ = [Buf(arena, al.take(1024), [512], BF16) for _ in range(4)]
            RS = [Buf(arena, al.take(2048), [512], F32) for _ in range(2)]
            wks = []
            for h in range(2):
                wa, wk = WO[h]()
                S.dma("pool", wa, wo_d[:, h * 512:(h + 1) * 512].rearrange("(k p) n -> p k n", p=128), writes=wk)
                wks.append(wk)
            o = 0
            while o < NT:
                n = min(512, NT - o)
                for cp_ in range(KC):
                    b = bank()
                    h, cc = cp_ // 4, cp_ % 4
                    for c in range(KC):
                        oa, okk = OT(c, slice(o, o + n))
                        mm(psf(b, 0, n), WO[h].ap[:, c, cc * 128:(cc + 1) * 128], oa, c == 0, c == KC - 1, U(okk, wks[h]), PK(b))
                    ya, yk = YS(cp_, slice(0, n))
                    act(ya, psf(b, 0, n), AF.Copy, PK(b), yk)
                postnorm_residual(gi, o, n, lambda c: YS(c, slice(0, n)), 1.0, (SQ, RS))
                o += n

        def sample_attn(al, cache_d, WR, d, QS, qsk, QTs_fn, KTN, ktnk, VTN, vtnk, sink, accum):
            TC = [Buf(arena, al.take(768 * 4), [768], F32) for _ in range(2)]
            PR = Buf(arena, al.take(1024 * 4), [16, 64], F32)
            SS = [Buf(arena, al.take(64), [16], F32) for _ in range(2)]
            PP = [Buf(arena, al.take(64), [16], F32) for _ in range(2)]
            PD = Buf(arena, al.take(128 * 4), [8, 16], F32)
            P0 = Buf(arena, al.take(128 * 4), [8, 16], F32)
            TN = Buf(arena, al.take(128 * 4), [8, 16], F32)
            TD = Buf(arena, al.take(128 * 4), [8, 16], F32)
            OH = Buf(arena, al.take(2048 * 2), [16, 128], BF16)
            CF = Buf(arena, al.take(256 * 4), [2, 128], F32)
            a, ohk = OH()
            S.dma("pool", a.rearrange("p a b -> p (a b)"), cbf_d[:, 1472:3520], writes=ohk)
            a, cfk = CF()
            S.dma("sp", a.rearrange("p a b -> p (a b)"), cf32_d, writes=cfk)
            qa, qk_ = QTs_fn()
            pa, pk = PD()
            tt("dve", pa.rearrange("p (k r) b -> p k r b", r=2), qa.rearrange("p (k r) b -> p k r b", r=2),
               KTN.unsqueeze(2).to_broadcast([128, 4, 2, 16]), ALU.mult, U(qk_, ktnk), pk)
            b0 = bank()
            mm(psf(b0, 0, 128), CF.ap[:, 0, :], pa.rearrange("p a b -> p (a b)"), True, True, U(cfk, pk), PK(b0))
            p0a, p0k = P0()
            act(p0a.rearrange("p a b -> p (a b)"), psf(b0, 0, 128), AF.Exp, PK(b0), p0k)
            bN = bank()
            st["held"].add(bN)
            for b in range(NS):
                yield None
                tc = TC[b % 2]
                ta, tk = tc()
                src_k = bass.AP(cache_d.tensor, b * WR * 512, [[d * 512, 128], [1, 256]])
                S.dma("sp", ta[:, 0:256], src_k, writes=tk)
                src_v = bass.AP(cache_d.tensor, b * WR * 512 + 256, [[d * 512, 128], [64, 4], [1, 64]])
                for r_ in range(2):
                    S.dma("sp", ta[:, 256:768].rearrange("p (k r d) -> p k r d", k=4, r=2)[:, :, r_, :], src_v, writes=tk)
                bq = [bank(), bank()]
                for h in range(2):
                    mm(psf(bq[h]), OH.ap[0:16, b, :], QS[0:16, h * 512:(h + 1) * 512], True, True, U(ohk, qsk), PK(bq[h]))
                pra, prk = PR()
                for h in range(2):
                    tt("dve", pra[:, h * 8:(h + 1) * 8, :].rearrange("p (k j) d -> p k j d", k=2),
                       psf(bq[h]).rearrange("p (k j d) -> p k j d", k=2, j=4),
                       ta[:, h * 128:(h + 1) * 128].rearrange("p (k d) -> p k d", k=2).unsqueeze(2).to_broadcast([128, 2, 4, 64]),
                       ALU.mult, U(PK(bq[h]), tk), prk)
                sa, sk = SS[b % 2]()
                S.add("dve", lambda e, sa=sa, pra=pra: e.tensor_reduce(out=sa, in_=pra, axis=AX.X, op=ALU.add), prk, sk)
                ppa, ppk = PP[b % 2]()
                act(ppa, sa, AF.Exp, sk, ppk)
                for kv in range(4):
                    mm(psf(bN, b * 16 + kv * 4, b * 16 + kv * 4 + 4), ta[:, 256 + kv * 128:256 + (kv + 1) * 128],
                       ppa[:, kv * 4:(kv + 1) * 4], True, True, U(tk, ppk), PK(bN))
                mm(psf(bN, 256 + b * 16, 256 + b * 16 + 16), CF.ap[:, 1, :], ppa, True, True, U(cfk, ppk), PK(bN))
            tna, tnk = TN()
            tda, tdk = TD()
            tt("dve", tna.rearrange("p (k r) b -> p k r b", r=2), p0a.rearrange("p (k r) b -> p k r b", r=2),
               VTN.unsqueeze(2).to_broadcast([128, 4, 2, 16]), ALU.mult, U(p0k, vtnk), tnk)
            for hp in range(2):
                pr = (64 * hp, 64 * hp + 64)
                srcN = psf(bN, 0, 256, p=pr).rearrange("p (b c r) -> p c b r", b=16, c=8)[:, :, :, hp]
                srcD = psf(bN, 256, 512, p=pr).rearrange("p (b c r) -> p c b r", b=16, c=8)[:, :, :, hp]
                tt("dve", tna[pr[0]:pr[1]], tna[pr[0]:pr[1]], srcN, ALU.add, U(tnk, PK(bN)), tnk)
                if sink is not None:
                    ska, skk = sink
                    tt("dve", tda[pr[0]:pr[1]], p0a[pr[0]:pr[1]], ska[pr[0]:pr[1]].unsqueeze(2).to_broadcast([64, 8, 16]), ALU.add, U(p0k, skk), tdk)
                    tt("dve", tda[pr[0]:pr[1]], tda[pr[0]:pr[1]], srcD, ALU.add, U(tdk, PK(bN)), tdk)
                else:
                    tt("dve", tda[pr[0]:pr[1]], p0a[pr[0]:pr[1]], srcD, ALU.add, U(p0k, PK(bN)), tdk)
            st["held"].discard(bN)
            accum(tna, tda, U(tnk, tdk))

        def load_consts(al, d_idx):
            MR = Buf(arena, al.take(1024), [512], BF16)
            MF = Buf(arena, al.take(1024), [512], BF16)
            OS = Buf(arena, al.take(384), [192], BF16)
            a, k1 = MR()
            S.dma("pool", a, cbf_d[:, 448:960], writes=k1)
            a, k2 = MF()
            S.dma("pool", a, cbf_d[:, 960:1472], writes=k2)
            a, k3 = OS()
            S.dma("pool", a, cbf_d[:, 256:448], writes=k3)
            return MR, MF, OS

        def load_rope(al, d_idx):
            RP = Buf(arena, al.take(4 * 17 * 8 * 4), [4, 17, 8], F32)
            a, k = RP()
            S.dma("sp", a.rearrange("p a b c -> p (a b c)"), rope_d[d_idx], writes=k)
            return RP, k

        ffn(0, 0, 1)

        def attention_a():
            al = Alloc(PB, SB_BYTES)
            HN = Buf(arena, al.take(KC * NT * 2), [KC, NT], BF16)
            WG = [Buf(arena, al.take(8 * 512 * 2), [8, 512], BF16) for _ in range(2)]
            QT = Buf(arena, al.take(KC * NT * 2), [KC, NT], BF16)
            KT = Buf(arena, al.take(4 * 18 * 128 * 2), [4, 18, 128], BF16)
            VT = Buf(arena, al.take(18 * 4 * 192 * 2), [18, 4, 192], BF16)
            MR, MF, OS = load_consts(al, 0)
            RP, rpk = load_rope(al, 0)
            SK = Buf(arena, al.take(32), [8], F32)
            QS = Buf(arena, al.take(2048), [1024], BF16)
            KTM = Buf(arena, al.take(1024), [256], F32)
            VTM = Buf(arena, al.take(1024), [256], F32)
            KDUP = Buf(arena, al.take(1024), [512], BF16)
            VDUP = KDUP
            VTN = Buf(arena, al.take(128), [4, 16], BF16)
            QTM = [Buf(arena, al.take(1024), [512], BF16) for _ in range(2)]
            QR = Buf(arena, al.take(512), [8, 16], F32)
            RT = [Buf(arena, al.take(256), [8, 8], F32) for _ in range(4)]
            PB3 = [Buf(arena, al.take(1024), [512], BF16) for _ in range(2)]
            FT = [Buf(arena, al.take(512), [128], F32) for _ in range(2)]
            alq = Alloc(QT.off, SB_BYTES)
            SQ = [Buf(arena, alq.take(1024), [512], BF16) for _ in range(4)]
            RS = [Buf(arena, alq.take(2048), [512], F32) for _ in range(2)]
            tiles = [(o, min(512, NT - o)) for o in range(0, NT, 512)]
            prenorm(2, tiles, lambda c, c0, n: HN(c, slice(c0, c0 + n)), (SQ, RS))
            a, vk_all = VT()
            S.add("pool", lambda e, a=a: e.memset(a, 0.0), (), vk_all)
            a, k = SK()
            S.dma("sp", a, sink_d, writes=k)
            S.add("act", lambda e, a=a: e.activation(out=a, in_=a, func=AF.Exp), k, k)
            grp_cols = [(1024, 1536), (0, 512), (512, 1024)]
            for gi, (g0, g1) in enumerate(grp_cols):
                if gi >= 1 and 'C' in DEVK:
                    continue
                wb = WG[gi % 2]
                wa, wk = wb()
                S.dma("pool", wa, wqkva_d[:, g0:g1].rearrange("(k p) n -> p k n", p=128), writes=wk)
                def tile_body(gi, i, b, M, wb=wb, wk=wk):
                    slot = i + 1
                    if gi == 0:
                        okd, ovd = [], []
                        if i == 15 and 'B' not in DEVK:
                            okd.append(lambda ka, kk: S.dma("sp", ap_d[:, 0:256], ka, reads=kk))
                            ovd.append(lambda va, vk: S.dma("sp", ap_d[:, 256:512], va, reads=vk))
                        if i == 16 and 'B' not in DEVK:
                            okd.append(lambda ka, kk: S.dma("sp", as_d[:, 127, 0:256], ka, reads=kk))
                            ovd.append(lambda va, vk: S.dma("sp", as_d[:, 127, 256:512], va, reads=vk))

                        def kt_dst(bT, slot=slot, M=M):
                            da, dk = KT(slice(None), slot, slice(0, M))
                            cp("act", da, psb(bT)[:, 0:512].rearrange("p (h n) -> p h n", h=4)[:, :, 0:M], PK(bT), dk)

                        def v_dst(va, vk, slot=slot, M=M):
                            da, dk = VT(slot, slice(None), slice(64, 128), p=(0, M))
                            cp("dve", da, va.rearrange("p (h d) -> p h d", h=4), vk, dk)

                        def vtn_dst(bT2):
                            da, dk = VTN()
                            cp("act", da, psb(bT2)[:, 0:512].rearrange("p (h n) -> p h n", h=4)[:, :, 0:16], PK(bT2), dk)
                        if 'D' not in DEVK:
                            proc_kv(b, M, (RP.ap[0:M, 0, i, :], RP.ap[0:M, 1, i, :], rpk), KTM, VTM, KDUP, RT, okd, ovd, kt_dst, v_dst,
                                    vtn_dst if i == 16 else None, VDUP)
                        if i == 15 and 'A' not in DEVK:
                            ka, kk = KT(slice(None), 16, slice(None))
                            S.dma("sp", bnA[:, 0:512].rearrange("p (h n) -> p h n", h=4), ka, reads=kk, writes=[("bnA", 0)])
                            va, vk = VT(16, slice(None), slice(64, 128))
                            S.dma("sp", bnA[:, 512:768].rearrange("p (h n) -> p h n", h=4), va, reads=vk, writes=[("bnA", 1)])
                            S.add("pool", lambda e: e.collective_compute("AllGather", ALU.bypass, replica_groups=RG, ins=[bnA], outs=[gaA]),
                                  [("bnA", 0), ("bnA", 1)], ["gaA"], dma=True, dinc=1, slot=("cc", 0))
                            ka, kk = KT(slice(None), 0, slice(None))
                            S.dma("sp", ka, gaA[0:128, 0:512].rearrange("p (h n) -> p h n", h=4), reads=["gaA"], writes=kk)
                            va, vk = VT(0, slice(None), slice(64, 128))
                            S.dma("sp", va, gaA[0:128, 512:768].rearrange("p (h n) -> p h n", h=4), reads=["gaA"], writes=vk)
                    else:
                        c0 = (gi - 1) * 4
                        if i == 16:
                            qa, qk = QS(slice(c0 * 128, c0 * 128 + 512), p=(0, M))
                        else:
                            qa, qk = QTM[i % 2]()

                        def qt_dst(bT, c0=c0, i=i, M=M):
                            da, dk = QT(slice(c0, c0 + 4), slice(i * 128, i * 128 + M))
                            cp("act", da, psb(bT)[:, 0:512].rearrange("p (h n) -> p h n", h=4)[:, :, 0:M], PK(bT), dk)
                        proc_q(b, M, (RP.ap[0:M, 0, i, :], RP.ap[0:M, 1, i, :], rpk), qa, qk, RT, qt_dst, QR)
                pend = None
                for i in range(17):
                    b, M = project(HN, 1, i, wb, wk)
                    if pend is not None:
                        tile_body(*pend)
                    pend = (gi, i, b, M)
                tile_body(*pend)
            al2 = Alloc(PB, QT.off)

            def accum(tna, tda, keys):
                S.add("dve", lambda e: e.reciprocal(out=tda, in_=tda), keys, keys)
                da, dk = QT(slice(None), slice(NP_, NT))
                tt("dve", da, tna, tda, ALU.mult, keys, dk)
            ktn_a, ktn_k = KT(slice(None), 17, slice(0, 16))
            vtn_a, vtn_k = VTN()
            qsa, qsk = QS(p=(0, 16))
            sgen = iter(())
            if sub >= 3:
                sgen = sample_attn(al2, ca_d, 128, 1, QS.ap, qsk, lambda: QT(slice(None), slice(NP_, NT)), ktn_a, ktn_k, vtn_a, vtn_k,
                                   (SK.ap, SK.keys()), accum)
            pi = 0
            pipeA = UnitPipe()
            for i in (list(range(1, 16)) + [0] if sub >= 2 else []):
                MASK = MF if i == 0 else MR
                ma, mk = MASK()
                for c in range(KC):
                    kvh = c // 2
                    qa_full, qk = QT(c, slice(i * 128, i * 128 + 128))

                    def kt_fn(hp, kb, kvh=kvh, i=i):
                        return KT(kvh, i + kb, slice(None), p=(64 * hp, 64 * hp + 64))

                    def vt_fn(hp, kb, kvh=kvh, i=i):
                        return VT(i + kb, kvh, slice(64 - 64 * hp, 192 - 64 * hp))
                    gen = attn_unit(kt_fn, vt_fn, lambda hp, q=qa_full: q[64 * hp:64 * hp + 64], qk, ma, mk, OS, OS.keys(), PB3, pi)
                    pi += 1

                    def fin(bn, c=c, qa_full=qa_full, qk=qk, fi=pi):
                        fa, fk = FT[fi % 2]()
                        S.add("dve", lambda e, fa=fa, bn=bn, c=c: e.tensor_scalar_add(out=fa, in0=psf(bn, 128, 256), scalar1=SK.ap[:, c:c + 1]), U(PK(bn), SK.keys()), fk)
                        S.add("dve", lambda e, fa=fa: e.reciprocal(out=fa, in_=fa), fk, fk)
                        tt("dve", qa_full, psf(bn, 0, 128), fa, ALU.mult, U(PK(bn), fk), qk)
                    pipeA.push(gen, fin)
                    if pi % 8 == 0:
                        next(sgen, None)
            pipeA.flush()
            for _ in sgen:
                pass
            al3 = Alloc(PB, QT.off)
            if 'E' not in DEVK:
                oproj(QT, woa_d, 3, al3)

        if stage >= 2:
            attention_a()
        if stage >= 3:
            ffn(1, 4, 5)
        if stage >= 4:
            ffn(2, 6, 7)

        HALO_IDX = {}
        hcount = 0
        for g in range(3):
            d = DILS[g]
            nb = 16 // d
            for r in range(d):
                HALO_IDX[(g, r * nb + nb - 1)] = hcount
                hcount += 1
        assert hcount == NH
        HBASE = [0, 1, 5]

        def kv_b():
            al = Alloc(PB, SB_BYTES)
            HN = Buf(arena, al.take(KC * NT * 2), [KC, NT], BF16)
            WG = [Buf(arena, al.take(8 * 512 * 2), [8, 512], BF16) for _ in range(2)]
            RPs = [load_rope(al, g) for g in range(3)]
            KTM = [Buf(arena, al.take(1024), [256], F32) for _ in range(2)]
            VTM = [Buf(arena, al.take(1024), [256], F32) for _ in range(2)]
            KDUP = Buf(arena, al.take(1024), [512], BF16)
            VDUP = Buf(arena, al.take(1024), [512], BF16)
            KST = [Buf(arena, al.take(1024), [4, 128], BF16) for _ in range(2)]
            VST = [Buf(arena, al.take(512), [256], BF16) for _ in range(2)]
            VNS = Buf(arena, al.take(128), [4, 16], BF16)
            RT = [Buf(arena, al.take(256), [8, 8], F32) for _ in range(4)]
            SQ = [Buf(arena, al.take(1024), [512], BF16) for _ in range(4)]
            RS = [Buf(arena, al.take(2048), [512], F32) for _ in range(2)]
            tiles = [(o, min(512, NT - o)) for o in range(0, NT, 512)]
            prenorm(12, tiles, lambda c, c0, n: HN(c, slice(c0, c0 + n)), (SQ, RS))
            cnt = 0
            for g in range(3):
                d = DILS[g]
                W = WINS[g]
                nb = 16 // d
                wb = WG[g % 2]
                wa, wk = wb()
                S.dma("pool", wa, wkvb_d[:, g * 512:(g + 1) * 512].rearrange("(k p) n -> p k n", p=128), writes=wk)
                RP, rpk = RPs[g]
                def tile_body(i, b, M, g=g, d=d, W=W, nb=nb, RP=RP, rpk=rpk):
                    nonlocal cnt
                    okd, ovd = [], []
                    hidx = HALO_IDX.get((g, i))
                    if hidx is not None:
                        r = i // nb
                        okd.append(lambda ka, kk, g=g, r=r, d=d: S.dma("sp", bp_d[g][r::d, 0:256], ka, reads=kk))
                        ovd.append(lambda va, vk, g=g, r=r, d=d: S.dma("sp", bp_d[g][r::d, 256:512], va, reads=vk))
                    if i == 16:
                        okd.append(lambda ka, kk, g=g, W=W: S.dma("sp", bs_d[g][:, W - 1, 0:256], ka, reads=kk))
                        ovd.append(lambda va, vk, g=g, W=W: S.dma("sp", bs_d[g][:, W - 1, 256:512], va, reads=vk))
                    kst = KST[cnt % 2]
                    vst = VST[cnt % 2]
                    ktm = KTM[cnt % 2]
                    vtm = VTM[cnt % 2]
                    cnt += 1

                    def kt_dst(bT, g=g, i=i, M=M, kst=kst, hidx=hidx):
                        da, dk = kst(slice(None), slice(0, M))
                        cp("act", da, psb(bT)[:, 0:512].rearrange("p (h n) -> p h n", h=4)[:, :, 0:M], PK(bT), dk)
                        S.dma("sp", ktsB4[:, g, :, i, 0:M], da, reads=dk, writes=[("kts", g, i)])
                        if hidx is not None:
                            S.dma("sp", bnB3s[hidx // NHC][:, hidx % NHC, 0:512].rearrange("p (h n) -> p h n", h=4), da, reads=dk, writes=[("bnB", hidx, 0)])

                    def v_dst(va, vk, g=g, i=i, M=M, vst=vst, hidx=hidx):
                        da, dk = vst(p=(0, M))
                        cp("dve", da, va, vk, dk)
                        S.dma("sp", vtsB4[0:M, g, :, i, :], da.rearrange("p (h d) -> p h d", h=4), reads=dk, writes=[("vts", g, i)])
                        if hidx is not None:
                            S.dma("sp", bnB3s[hidx // NHC][0:M, hidx % NHC, 512:768], da, reads=dk, writes=[("bnB", hidx, 1)])

                    def vtn_dst(bT2, g=g):
                        da, dk = VNS()
                        cp("act", da, psb(bT2)[:, 0:512].rearrange("p (h n) -> p h n", h=4)[:, :, 0:16], PK(bT2), dk)
                        S.dma("sp", vtnB3[:, g, :, :], da, reads=dk, writes=[("vtn", g)])
                    proc_kv(b, M, (RP.ap[0:M, 0, i, :], RP.ap[0:M, 1, i, :], rpk), ktm, vtm, KDUP, RT, okd, ovd, kt_dst, v_dst,
                            vtn_dst if i == 16 else None, VDUP)
                pend = None
                for i in range(17):
                    b, M = project(HN, d, i, wb, wk)
                    if pend is not None:
                        tile_body(*pend)
                    pend = (i, b, M)
                tile_body(*pend)
            for kx in range(3):
                S.add("pool", lambda e, kx=kx: e.collective_compute("AllGather", ALU.bypass, replica_groups=RG, ins=[bnBs[kx]], outs=[gaBs[kx]]),
                      [("bnB", h, j) for h in range(kx * NHC, (kx + 1) * NHC) for j in range(2)], [("gaB", kx)], dma=True, dinc=1, slot=("cc", 1 + kx))

        if stage >= 5:
            kv_b()

        def q_b():
            al = Alloc(PB, SB_BYTES)
            HN = Buf(arena, al.take(KC * NT * 2), [KC, NT], BF16)
            WG = [Buf(arena, al.take(8 * 512 * 2), [8, 512], BF16) for _ in range(2)]
            RPs = [load_rope(al, g) for g in range(3)]
            QTM = [Buf(arena, al.take(1024), [512], BF16) for _ in range(2)]
            QST = [Buf(arena, al.take(1024), [4, 128], BF16) for _ in range(2)]
            QR = Buf(arena, al.take(512), [8, 16], F32)
            RT = [Buf(arena, al.take(256), [8, 8], F32) for _ in range(4)]
            SQ = [Buf(arena, al.take(1024), [512], BF16) for _ in range(4)]
            RS = [Buf(arena, al.take(2048), [512], F32) for _ in range(2)]
            tiles = [(o, min(512, NT - o)) for o in range(0, NT, 512)]
            prenorm(8, tiles, lambda c, c0, n: HN(c, slice(c0, c0 + n)), (SQ, RS))
            cnt = 0
            for g in range(3):
                d = DILS[g]
                RP, rpk = RPs[g]
                for h in range(2):
                    wb = WG[cnt % 2]
                    wa, wk = wb()
                    S.dma("pool", wa, wqb_d[:, g * 1024 + h * 512:g * 1024 + (h + 1) * 512].rearrange("(k p) n -> p k n", p=128), writes=wk)
                    def tile_body(i, b, M, g=g, d=d, h=h, RP=RP, rpk=rpk):
                        nonlocal cnt
                        qa, qk = QTM[cnt % 2](p=(0, M))
                        qst = QST[cnt % 2]
                        cnt += 1

                        def qt_dst(bT, g=g, h=h, i=i, M=M, qst=qst):
                            da, dk = qst(slice(None), slice(0, M))
                            cp("act", da, psb(bT)[:, 0:512].rearrange("p (h n) -> p h n", h=4)[:, :, 0:M], PK(bT), dk)
                            S.dma("sp", qsB4[g, h * 4:h * 4 + 4, :, i * 128:i * 128 + M].rearrange("c p n -> p c n"), da, reads=dk, writes=[("qs", g, h, i)])
                        proc_q(b, M, (RP.ap[0:M, 0, i, :], RP.ap[0:M, 1, i, :], rpk), qa, qk, RT, qt_dst, QR)
                        if i == 16:
                            S.dma("sp", qssB3[g, :, h * 512:(h + 1) * 512], qa, reads=qk, writes=[("qss", g)])
                    pend = None
                    for i in range(17):
                        b, M = project(HN, d, i, wb, wk)
                        if pend is not None:
                            tile_body(*pend)
                        pend = (i, b, M)
                    tile_body(*pend)

        if stage >= 6:
            q_b()

        def halo_pieces(h0, d):
            out = []
            h = h0
            while h < h0 + d:
                kx = h // NHC
                hi = min(h0 + d, (kx + 1) * NHC)
                out.append((kx, h - kx * NHC, hi - kx * NHC, h - h0))
                h = hi
            return out

        def attention_b():
            al = Alloc(PB, SB_BYTES)
            OT = Buf(arena, al.take(KC * NT * 2), [KC, NT], BF16)
            MR, MF, OS = load_consts(al, 0)
            al_kv = al.cur
            KTB = [Buf(arena, al.take((DILS[g] + 16) * 128 * 2), [DILS[g] + 16, 128], BF16) for g in range(3)]
            VTB = [Buf(arena, al.take((DILS[g] + 16) * 192 * 2), [DILS[g] + 16, 192], BF16) for g in range(3)]
            ACN = Buf(arena, al.take(NP_ * 4), [NP_], F32)
            ACD = Buf(arena, al.take(NP_ * 4), [NP_], F32)
            QTR = [Buf(arena, al.take(16 * 128 * 2), [16 * 128], BF16) for _ in range(2)]
            PB3 = [Buf(arena, al.take(1024), [512], BF16) for _ in range(3)]
            al_tmp = al.cur
            for g in range(3):
                a, k = VTB[g]()
                S.add("pool", lambda e, a=a: e.memset(a, 0.0), (), k)
            def sample_b_gen():
                al2 = Alloc(al_tmp, SB_BYTES)
                SN = Buf(arena, al2.take(512), [8, 16], F32)
                SD = Buf(arena, al2.take(512), [8, 16], F32)
                QS = Buf(arena, al2.take(2048), [1024], BF16)
                QTS = Buf(arena, al2.take(256), [8, 16], BF16)
                KTN = Buf(arena, al2.take(128), [4, 16], BF16)
                VTN = Buf(arena, al2.take(128), [4, 16], BF16)
                base2 = al2.cur
                for g in range(3):
                    al3 = Alloc(base2, SB_BYTES)
                    qsa, qsk = QS(p=(0, 16))
                    S.dma("sp", qsa, qssB3[g], reads=[("qss", g)], writes=qsk)
                    qta, qtk = QTS()
                    S.dma("sp", qta, qsB4[g, :, :, 2048:2064].rearrange("c p n -> p c n"), reads=[("qs", g, h, 16) for h in range(2)], writes=qtk)
                    kna, knk = KTN()
                    S.dma("sp", kna, ktsB4[:, g, :, 16, 0:16], reads=[("kts", g, 16)], writes=knk)
                    vna, vnk = VTN()
                    S.dma("sp", vna, vtnB3[:, g, :, :], reads=[("vtn", g)], writes=vnk)

                    def accum(tna, tda, keys, g=g):
                        sna, snk = SN()
                        sda, sdk = SD()
                        if g == 0:
                            cp("dve", sna, tna, keys, snk)
                            cp("dve", sda, tda, keys, sdk)
                        else:
                            tt("dve", sna, sna, tna, ALU.add, U(snk, keys), snk)
                            tt("dve", sda, sda, tda, ALU.add, U(sdk, keys), sdk)
                    yield from sample_attn(al3, cb_d[g], WINS[g], DILS[g], QS.ap, qsk, lambda: QTS(), kna, knk, vna, vnk, None, accum)
                sna, snk = SN()
                sda, sdk = SD()
                S.add("dve", lambda e: e.reciprocal(out=sda, in_=sda), sdk, sdk)
                da, dk = OT(slice(None), slice(NP_, NT))
                tt("dve", da, sna, sda, ALU.mult, U(snk, sdk), dk)
            sgenB = sample_b_gen()
            pi = 0
            qcnt = 0
            pipeB = UnitPipe()
            for kvh in range(4):
                for g in range(3):
                    d = DILS[g]
                    h0 = HBASE[g]
                    for (kx, lo, hi, so) in halo_pieces(h0, d):
                        ka, kk = KTB[g](slice(so, so + hi - lo), slice(None))
                        S.dma("sp", ka, gaB3s[kx][0:128, lo:hi, kvh * 128:(kvh + 1) * 128], reads=[("gaB", kx)], writes=kk)
                        va, vk = VTB[g](slice(so, so + hi - lo), slice(64, 128))
                        S.dma("sp", va, gaB3s[kx][0:128, lo:hi, 512 + kvh * 64:512 + (kvh + 1) * 64], reads=[("gaB", kx)], writes=vk)
                    ka, kk = KTB[g](slice(d, d + 16), slice(None))
                    S.dma("sp", ka, ktsB4[:, g, kvh, 0:16, :], reads=[("kts", g, i) for i in range(16)], writes=kk)
                    va, vk = VTB[g](slice(d, d + 16), slice(64, 128))
                    S.dma("sp", va, vtsB4[:, g, kvh, 0:16, :], reads=[("vts", g, i) for i in range(16)], writes=vk)
                for c in (2 * kvh, 2 * kvh + 1):
                    for g in range(3):
                        d = DILS[g]
                        nb = 16 // d
                        qt = QTR[qcnt % 2]
                        qcnt += 1
                        qa_all, qk = qt()
                        S.dma("sp", qa_all, qsB4[g, c, :, 0:16 * 128], reads=[("qs", g, c // 4, i) for i in range(16)], writes=qk)
                        for i in range(16):
                            r, m = i // nb, i % nb
                            own = d + i
                            prev = (d + i - 1) if m >= 1 else r
                            MASK = MR if m >= 1 else MF
                            ma, mk = MASK()
                            q_i = qa_all[:, i * 128:(i + 1) * 128]

                            def kt_fn(hp, kb, g=g, own=own, prev=prev):
                                return KTB[g](own if kb else prev, slice(None), p=(64 * hp, 64 * hp + 64))

                            def vt_fn(hp, kb, g=g, own=own, prev=prev):
                                return VTB[g](own if kb else prev, slice(64 - 64 * hp, 192 - 64 * hp))
                            gen = attn_unit(kt_fn, vt_fn, lambda hp, q=q_i: q[64 * hp:64 * hp + 64], qk, ma, mk, OS, OS.keys(), PB3, pi)
                            pi += 1

                            def fin(bn, g=g, d=d, i=i):
                                cols, _ = tile_cols(d, i)
                                na, nk = ACN(cols)
                                dda, ddk = ACD(cols)
                                if g == 0:
                                    cp("dve", na, psf(bn, 0, 128), PK(bn), nk)
                                    cp("dve", dda, psf(bn, 128, 256), PK(bn), ddk)
                                else:
                                    tt("dve", na, na, psf(bn, 0, 128), ALU.add, U(nk, PK(bn)), nk)
                                    tt("dve", dda, dda, psf(bn, 128, 256), ALU.add, U(ddk, PK(bn)), ddk)
                            pipeB.push(gen, fin)
                            if pi % 8 == 0:
                                next(sgenB, None)
                    pipeB.flush()
                    for q4 in range(4):
                        sl = slice(q4 * 512, (q4 + 1) * 512)
                        dda, ddk = ACD(sl)
                        na, nk = ACN(sl)
                        S.add("dve", lambda e, dda=dda: e.reciprocal(out=dda, in_=dda), ddk, ddk)
                        oa, ok_ = OT(c, sl)
                        tt("pool", oa, na, dda, ALU.mult, U(nk, ddk), ok_)
            for _ in sgenB:
                pass
            al4 = Alloc(al_kv, al_tmp)
            oproj(OT, wob_d, 9, al4)

        if stage >= 7:
            attention_b()
        if stage >= 8:
            ffn(3, 10, 11)

        copy_burst(1000)
        for c in range(KC):
            a, k = XT(c)
            S.dma("sp", yT_d[c * 128:(c + 1) * 128, :], a, reads=k)

        S.emit(nc, es)
    return nc


def _rope_tables(core):
    half = core % 2
    inv = (500000.0 ** (-np.arange(0, 16, 2, dtype=np.float32) / 16.0)).astype(np.float32)
    out = np.zeros((3, 128, 4, 17, 8), np.float32)
    for gi, d in enumerate(DILS):
        nb = 16 // d
        for i in range(17):
            if i == 16:
                pos = np.full(128, 8192.0, np.float32)
            else:
                r, m = i // nb, i % nb
                pos = (half * 2048 + r + d * (128 * m + np.arange(128))).astype(np.float32)
            ang = pos[:, None] * inv[None, :]
            c, s = np.cos(ang).astype(np.float32), np.sin(ang).astype(np.float32)
            out[gi, :, 0, i] = c
            out[gi, :, 1, i] = s
            out[gi, :, 2, i] = c * np.float32(0.125)
            out[gi, :, 3, i] = s * np.float32(0.125)
    return out.reshape(3, 128, 4 * 17 * 8)


def _consts(core):
    cbf = np.zeros((128, 3520), np.float32)
    cbf[:, 0:128] = np.eye(128, dtype=np.float32)
    cbf[:, 128:256] = 1.0
    cbf[:, 256 + 64:256 + 128] = 1.0
    k = np.arange(128)[:, None]
    q = np.arange(128)[None, :]
    mA = (k >= q).astype(np.float32)
    mB = (k <= q).astype(np.float32)
    mr = np.concatenate([mA, mB, mA, mB], 1)
    cbf[:, 448:960] = mr
    mf = mr.copy()
    if core % 2 == 0:
        mf[:, 0:128] = 0.0
        mf[:, 256:384] = 0.0
    cbf[:, 960:1472] = mf
    oh = np.zeros((128, 16, 128), np.float32)
    for b in range(16):
        oh[b, b, :] = 1.0
    cbf[:, 1472:3520] = oh.reshape(128, 2048)
    cf = np.zeros((128, 256), np.float32)
    cf[0:64, 0:64] = 1.0
    cf[64:128, 64:128] = 1.0
    cf[:, 128:256] = 1.0
    return cbf, cf


_NC_CACHE = {}
_BUILD_ARGS = ()


def kernel(x_prompt, x_sample, cache_a_kv, cache_b_kv_w128, cache_b_kv_w512, cache_b_kv_w2048,
           norm_g, w_ffn_gu, w_ffn_dn, w_qkv_a, sink_a, w_o_a, g_kv_b, w_kv_b, w_q_b, w_o_b):
    f = lambda a: np.ascontiguousarray(np.asarray(a, dtype=np.float32))
    x_prompt, x_sample = f(x_prompt), f(x_sample)
    gains = np.concatenate([f(norm_g).reshape(12, 1024), f(g_kv_b).reshape(1, 1024)], 0)
    gains_l = np.ascontiguousarray(gains.reshape(13, 8, 128).transpose(2, 0, 1).reshape(128, 104))
    sk = f(sink_a).reshape(16)
    sinkl = np.zeros((128, 8), np.float32)
    for c in range(8):
        sinkl[0:64, c] = sk[2 * c]
        sinkl[64:128, c] = sk[2 * c + 1]
    shared = {
        "w_gu": f(w_ffn_gu).reshape(4, 1024, 2 * DFF), "w_dn": f(w_ffn_dn).reshape(4, DFF, 1024),
        "w_qkv_a": f(w_qkv_a).reshape(1024, 1536), "w_o_a": f(w_o_a).reshape(1024, 1024),
        "w_kv_b": f(w_kv_b), "w_q_b": f(w_q_b).reshape(1024, 3072), "w_o_b": f(w_o_b).reshape(1024, 1024),
        "gains": gains_l, "sinkl": sinkl,
    }
    ca = f(cache_a_kv).reshape(128, 128, 512)
    cb = [f(cache_b_kv_w128).reshape(128, 128, 512), f(cache_b_kv_w512).reshape(128, 512, 512),
          f(cache_b_kv_w2048).reshape(128, 2048, 512)]
    in_maps = []
    for core in range(8):
        b, half = core // 2, core % 2
        xt = np.empty((1024, NT), np.float32)
        xt[:, 0:NP_] = x_prompt[b, half * NP_:(half + 1) * NP_, :].T
        xt[:, NP_:NT] = x_sample[core * NS:(core + 1) * NS, 0, :].T
        cbf, cf = _consts(core)
        m = dict(shared)
        m.update({"xT": xt, "rope": _rope_tables(core), "cbf": cbf, "cf32": cf,
                  "ca": np.ascontiguousarray(ca[core * NS:(core + 1) * NS]),
                  "cb0": np.ascontiguousarray(cb[0][core * NS:(core + 1) * NS]),
                  "cb1": np.ascontiguousarray(cb[1][core * NS:(core + 1) * NS]),
                  "cb2": np.ascontiguousarray(cb[2][core * NS:(core + 1) * NS])})
        in_maps.append(m)
    if "nc" not in _NC_CACHE:
        _NC_CACHE["nc"] = build_program(*_BUILD_ARGS)
    if len(_BUILD_ARGS) > 3 and _BUILD_ARGS[3]:
        for m in in_maps:
            m["w_gu"] = np.ascontiguousarray(m["w_gu"][0:1]); m["w_dn"] = np.ascontiguousarray(m["w_dn"][0:1])
            m["cb1"] = np.ascontiguousarray(m["cb1"][:, 0:2]); m["cb2"] = np.ascontiguousarray(m["cb2"][:, 0:2])
    res = run_bass_kernel_spmd(_NC_CACHE["nc"], in_maps, core_ids=list(range(8)))
    R = res.results
    y_p = np.empty((4, 4096, 1024), np.float32)
    y_s = np.empty((128, 1, 1024), np.float32)
    for core in range(8):
        b, half = core // 2, core % 2
        yt = R[core]["yT"]
        y_p[b, half * NP_:(half + 1) * NP_, :] = yt[:, 0:NP_].T
        y_s[core * NS:(core + 1) * NS, 0, :] = yt[:, NP_:NT].T
    a_p = np.stack([R[2 * b + 1]["a_p"].reshape(128, 2, 4, 64) for b in range(4)], 0)[None]
    b_p = [np.stack([R[2 * b + 1]["b%d_p" % g].reshape(WINS[g], 2, 4, 64) for b in range(4)], 0) for g in range(3)]
    a_s = np.concatenate([R[c]["a_s"].reshape(NS, 128, 2, 4, 64) for c in range(8)], 0)[None]
    b_s = [np.concatenate([R[c]["b%d_s" % g].reshape(NS, WINS[g], 2, 4, 64) for c in range(8)], 0) for g in range(3)]
    return (y_p, y_s, np.ascontiguousarray(a_p), b_p[0], b_p[1], b_p[2], np.ascontiguousarray(a_s), b_s[0], b_s[1], b_s[2])
```

```python
import contextlib
import math
import os
DEVK = os.environ.get('DEVK', '')
import numpy as np
import concourse.bass as bass
import concourse.mybir as mybir
from concourse.bass_utils import run_bass_kernel_spmd

F32 = mybir.dt.float32
BF16 = mybir.dt.bfloat16
AF = mybir.ActivationFunctionType
ALU = mybir.AluOpType
AX = mybir.AxisListType

NP_ = 2048
NS = 16
NT = NP_ + NS
KC = 8
HB = 22
DFF = 2816
EPS = 1e-6
PAGE = 256
DILS = (1, 4, 16)
WINS = (128, 512, 2048)
RG = [[0, 1], [2, 3], [4, 5], [6, 7]]


class Op:
    __slots__ = ("eng", "fn", "deps", "sig", "count", "dma", "dsem", "dcount", "prev_on_sem", "waits", "dinc", "pidx")

    def __init__(self, eng, fn, dma):
        self.eng = eng
        self.fn = fn
        self.dma = dma
        self.deps = []
        self.sig = False
        self.count = 0
        self.dsem = None
        self.dcount = 0
        self.prev_on_sem = None
        self.waits = []
        self.dinc = 16


class Sched:
    ENGS = ("pe", "act", "dve", "pool", "sp")

    def __init__(self, n_dma_sems=8):
        self.ops = {e: [] for e in self.ENGS}
        self.last_writer = {}
        self.readers = {}
        self.n_dma_sems = n_dma_sems
        self.dma_rr = {e: 0 for e in self.ENGS}
        self.dma_last = {}

    def add(self, eng, fn, reads=(), writes=(), dma=False, dinc=16, slot=None):
        op = Op(eng, fn, dma)
        op.dinc = dinc
        psr = [k for k in reads if isinstance(k, tuple) and k[0] == "ps"]
        if psr:
            writes = list(writes) + psr
        deps = {}
        lw = self.last_writer
        rd = self.readers
        for k in reads:
            w = lw.get(k)
            if w is not None:
                deps[id(w)] = w
        for k in writes:
            w = lw.get(k)
            if w is not None and (dma or w.dma or w.eng != eng or eng != "pe"):
                deps[id(w)] = w
            rr = rd.get(k)
            if rr:
                for r in rr:
                    if dma or r.dma or r.eng != eng or eng != "pe":
                        deps[id(r)] = r
        op.deps = list(deps.values())
        for k in writes:
            lw[k] = op
            rd[k] = []
        for k in reads:
            l = rd.get(k)
            if l is None:
                rd[k] = [op]
            elif not l or l[-1] is not op:
                l.append(op)
        if dma:
            if slot is None:
                slot = (eng, self.dma_rr[eng] % self.n_dma_sems)
                self.dma_rr[eng] += 1
            op.dsem = slot
            op.prev_on_sem = self.dma_last.get(slot)
            op.dcount = (op.prev_on_sem.dcount if op.prev_on_sem else 0) + dinc
            self.dma_last[slot] = op
        self.ops[eng].append(op)
        return op

    def dma(self, eng, out, in_, reads=(), writes=()):
        return self.add(eng, lambda e: e.dma_start(out=out, in_=in_), reads, writes, dma=True)

    def emit(self, nc, es):
        for e in self.ENGS:
            for idx, op in enumerate(self.ops[e]):
                op.pidx = idx
        for e in self.ENGS:
            seen = {}
            for op in self.ops[e]:
                best = {}
                nd = []
                for d in op.deps:
                    if d.dma:
                        nd.append(d)
                    else:
                        b = best.get(d.eng)
                        if b is None or d.pidx > b.pidx:
                            best[d.eng] = d
                for pe_, d in best.items():
                    if seen.get(pe_, -1) >= d.pidx:
                        continue
                    seen[pe_] = d.pidx
                    nd.append(d)
                op.deps = nd
        for e in self.ENGS:
            for op in self.ops[e]:
                for d in op.deps:
                    d.sig = True
        esem = {}
        for e in self.ENGS:
            esem[e] = es.enter_context(nc.semaphore("s_" + e))
            c = 0
            for op in self.ops[e]:
                if op.sig and not op.dma:
                    c += 1
                    op.count = c
        dsems = {}
        for slot in self.dma_last:
            dsems[slot] = es.enter_context(nc.semaphore("d_%s_%d" % slot))
        for e in self.ENGS:
            waited = {}
            for op in self.ops[e]:
                w = {}
                if op.dma and op.prev_on_sem is not None:
                    w[op.dsem] = ("d", op.dsem, op.prev_on_sem.dcount)
                for d in op.deps:
                    if d.dma:
                        key, val, kind = d.dsem, d.dcount, "d"
                    else:
                        key, val, kind = d.eng, d.count, "e"
                    if waited.get(key, 0) >= val:
                        continue
                    if key in w and w[key][2] >= val:
                        continue
                    w[key] = (kind, key, val)
                for key, (kind, k, val) in w.items():
                    if waited.get(key, 0) < val:
                        waited[key] = val
                op.waits = list(w.values())
        finals = [(slot, op.dcount) for slot, op in self.dma_last.items()]
        engobj = {"pe": "tensor", "act": "scalar", "dve": "vector", "pool": "gpsimd", "sp": "sync"}
        with nc.Block() as block:
            def make(ename):
                def body(eng):
                    for op in self.ops[ename]:
                        for kind, k, val in op.waits:
                            eng.wait_ge(esem[k] if kind == "e" else dsems[k], val)
                        ins = op.fn(eng)
                        if op.dma:
                            ins.then_inc(dsems[op.dsem], op.dinc)
                        elif op.sig:
                            ins.then_inc(esem[ename], 1)
                    if ename == "sp":
                        for slot, cnt in finals:
                            eng.wait_ge(dsems[slot], cnt)
                return body
            for ename in self.ENGS:
                getattr(block, engobj[ename])(make(ename))


class Buf:
    def __init__(self, arena, off, dims, dt):
        assert off % 4 == 0
        self.off = off
        self.dims = list(dims)
        self.dt = dt
        self.es = 4 if dt == F32 else 2
        n = int(np.prod(dims))
        self.nbytes = n * self.es
        v = arena[:, off // 2: off // 2 + self.nbytes // 2]
        if dt == F32:
            v = v.bitcast(F32)
        if len(dims) == 2:
            v = v.rearrange("p (a b) -> p a b", a=dims[0])
        elif len(dims) == 3:
            v = v.rearrange("p (a b c) -> p a b c", a=dims[0], b=dims[1])
        elif len(dims) == 4:
            v = v.rearrange("p (a b c d) -> p a b c d", a=dims[0], b=dims[1], c=dims[2])
        self.ap = v
        self.strides = [int(np.prod(dims[i + 1:])) for i in range(len(dims))]

    def keys(self, *idx):
        idx = list(idx) + [slice(None)] * (len(self.dims) - len(idx))
        rngs = []
        for i, ix in enumerate(idx):
            if isinstance(ix, int):
                rngs.append((ix, ix + 1))
            else:
                a, b, _ = ix.indices(self.dims[i])
                rngs.append((a, b))
        outer = [0]
        last = len(rngs) - 1
        while last > 0 and rngs[last] == (0, self.dims[last]):
            last -= 1
        for i in range(last):
            outer = [o + j * self.strides[i] for o in outer for j in range(rngs[i][0], rngs[i][1])]
        lo = rngs[last][0] * self.strides[last]
        hi = rngs[last][1] * self.strides[last]
        ks = set()
        for o in outer:
            b0 = self.off + (o + lo) * self.es
            b1 = self.off + (o + hi) * self.es
            ks.update(range(b0 // PAGE, (b1 - 1) // PAGE + 1))
        return ks

    def __call__(self, *idx, p=None):
        sl = (slice(None) if p is None else slice(p[0], p[1]),) + tuple(idx)
        return self.ap[sl], self.keys(*idx)


class Alloc:
    def __init__(self, base, limit):
        self.cur = base
        self.limit = limit

    def take(self, nbytes):
        o = self.cur
        self.cur = (o + nbytes + PAGE - 1) // PAGE * PAGE
        assert self.cur <= self.limit, (self.cur, self.limit)
        return o


def tile_cols(d, i):
    if i == 16:
        return slice(NP_, NP_ + NS), NS
    nb = 16 // d
    r, m = i // nb, i % nb
    st = r + d * 128 * m
    return slice(st, st + d * 127 + 1, d), 128


def build_program(stage=99, copies=True, sub=99, small=False):
    nc = bass.Bass("TRN2", target_bir_lowering=False)

    def din(name, shape):
        return nc.dram_tensor(name, list(shape), F32, kind="ExternalInput").ap()

    def dout(name, shape):
        return nc.dram_tensor(name, list(shape), F32, kind="ExternalOutput").ap()

    xT_d = din("xT", [1024, NT])
    wgu_d = din("w_gu", [1 if small else 4, HB, 128, KC * 256])
    wdn_d = din("w_dn", [1 if small else 4, KC, 128, HB * 128])
    wqkva_d = din("w_qkv_a", [1024, 1536])
    woa_d = din("w_o_a", [1024, 1024])
    wkvb_d = din("w_kv_b", [1024, 1536])
    wqb_d = din("w_q_b", [1024, 3072])
    wob_d = din("w_o_b", [1024, 1024])
    gains_d = din("gains", [128, 13 * 8])
    sink_d = din("sinkl", [128, 8])
    rope_d = din("rope", [3, 128, 4 * 17 * 8])
    cbf_d = din("cbf", [128, 3520])
    cf32_d = din("cf32", [128, 256])
    ca_d = din("ca", [NS, 128, 512])
    cb_d = [din("cb0", [NS, 128, 512]), din("cb1", [NS, 2 if small else 512, 512]), din("cb2", [NS, 2 if small else 2048, 512])]

    yT_d = dout("yT", [1024, NT])
    ap_d = dout("a_p", [128, 512])
    bp_d = [dout("b0_p", [128, 512]), dout("b1_p", [512, 512]), dout("b2_p", [2048, 512])]
    as_d = dout("a_s", [NS, 128, 512])
    bs_d = [dout("b0_s", [NS, 128, 512]), dout("b1_s", [NS, 512, 512]), dout("b2_s", [NS, 2048, 512])]

    def dint(name, shape, dt=BF16):
        return nc.dram_tensor(name, list(shape), dt, kind="Internal").ap()

    bnA = dint("bnA", [128, 768])
    gaA = dint("gaA", [256, 768])
    NH = 21
    NHC = 7
    bnBs = [dint("bnB%d" % k, [128, NHC * 768]) for k in range(3)]
    gaBs = [dint("gaB%d" % k, [256, NHC * 768]) for k in range(3)]
    ktsB = dint("ktsB", [128, 3 * 4 * 17 * 128])
    vtsB = dint("vtsB", [128, 3 * 4 * 17 * 64])
    vtnB = dint("vtnB", [128, 3 * 4 * 16])
    qsB = dint("qsB", [3 * 8 * 128, 17 * 128])
    qssB = dint("qssB", [3 * 16, 1024])
    ktsB4 = ktsB.rearrange("p (g k i n) -> p g k i n", g=3, k=4, i=17)
    vtsB4 = vtsB.rearrange("p (g k i n) -> p g k i n", g=3, k=4, i=17)
    vtnB3 = vtnB.rearrange("p (g k n) -> p g k n", g=3, k=4)
    bnB3s = [t.rearrange("p (h n) -> p h n", h=NHC) for t in bnBs]
    gaB3s = [t.rearrange("p (h n) -> p h n", h=NHC) for t in gaBs]
    qsB4 = qsB.rearrange("(g c p) n -> g c p n", g=3, c=8)
    qssB3 = qssB.rearrange("(g b) n -> g b n", g=3)

    S = Sched()
    es = contextlib.ExitStack()
    with es:
        SB_BYTES = 212736
        arena = es.enter_context(nc.sbuf_tensor("arena", [128, SB_BYTES // 2], BF16))
        ps = es.enter_context(nc.psum_tensor("ps", [128, 4096], F32))

        st = {"rr": 0, "held": set()}

        def bank():
            while True:
                b = st["rr"] % 8
                st["rr"] += 1
                if b not in st["held"]:
                    return b

        def psf(b, lo=0, hi=512, p=None):
            sl = slice(None) if p is None else slice(p[0], p[1])
            return ps[sl, b * 512 + lo: b * 512 + hi]

        def psb(b, p=None):
            sl = slice(None) if p is None else slice(p[0], p[1])
            return ps[sl, b * 512:(b + 1) * 512].bitcast(BF16)

        def PK(b):
            return [("ps", b)]

        XT = Buf(arena, 0, [KC, NT], F32)
        GN = Buf(arena, 66048, [13, 8], F32)
        IDN = Buf(arena, 66560, [128], BF16)
        ONE = Buf(arena, 66816, [128], BF16)
        PB = 67072

        def mm(out, lhsT, rhs, start, stop, reads, writes):
            S.add("pe", lambda e: e.matmul(out, lhsT=lhsT, rhs=rhs, start=start, stop=stop), reads, writes)

        def act(out, in_, func, reads, writes, scale=1.0, bias=0.0):
            S.add("act", lambda e: e.activation(out=out, in_=in_, func=func, scale=scale, bias=bias), reads, writes)

        def tt(eng, out, in0, in1, op, reads, writes):
            S.add(eng, lambda e: e.tensor_tensor(out=out, in0=in0, in1=in1, op=op), reads, writes)

        def stt(eng, out, in0, scalar, in1, op0, op1, reads, writes):
            S.add(eng, lambda e: e.scalar_tensor_tensor(out=out, in0=in0, scalar=scalar, in1=in1, op0=op0, op1=op1), reads, writes)

        def cp(eng, out, in_, reads, writes):
            if eng == "act":
                S.add(eng, lambda e: e.activation(out=out, in_=in_, func=AF.Copy), reads, writes)
            else:
                S.add(eng, lambda e: e.tensor_copy(out=out, in_=in_), reads, writes)

        def U(*ks):
            r = set()
            for k in ks:
                r.update(k)
            return r

        for c in range(KC):
            a, k = XT(c)
            S.dma("sp", a, xT_d[c * 128:(c + 1) * 128, :], writes=k)
        a, k = GN()
        S.dma("sp", a.rearrange("p a b -> p (a b)"), gains_d, writes=k)
        a, k = IDN()
        S.dma("pool", a, cbf_d[:, 0:128], writes=k)
        a, k = ONE()
        S.dma("pool", a, cbf_d[:, 128:256], writes=k)

        copy_jobs = []
        if copies:
            copy_jobs.append((as_d[:, 0:127, :], ca_d[:, 1:128, :]))
            copy_jobs.append((bs_d[0][:, 0:127, :], cb_d[0][:, 1:128, :]))
            for b in range(NS):
                copy_jobs.append((bs_d[1][b, 0:511, :], cb_d[1][b, 1:512, :]))
            for b in range(NS):
                copy_jobs.append((bs_d[2][b, 0:1024, :], cb_d[2][b, 1:1025, :]))
                copy_jobs.append((bs_d[2][b, 1024:2047, :], cb_d[2][b, 1025:2048, :]))

        def copy_burst(n):
            for _ in range(min(n, len(copy_jobs))):
                o_, i_ = copy_jobs.pop()
                S.dma("sp", o_, i_)

        def rstd_tile(src_fn, n, al_bufs, tag, factor=1.0):
            SQ, RS = al_bufs
            b = bank()
            for c in range(KC):
                sa, sk = src_fn(c)
                qa, qk = SQ[c % len(SQ)](slice(0, n))
                act(qa, sa, AF.Square, sk, qk)
                mm(psf(b, 0, n), ONE.ap, qa, c == 0, c == KC - 1, U(qk, ONE.keys()), PK(b))
            ra, rk = RS[st.setdefault("rs", 0) % len(RS)](slice(0, n))
            st["rs"] += 1
            act(ra, psf(b, 0, n), AF.Sqrt, PK(b), rk, scale=1.0 / (1024.0 * factor * factor), bias=EPS / (factor * factor))
            S.add("dve", lambda e: e.reciprocal(out=ra, in_=ra), rk, rk)
            return ra, rk

        def prenorm(gi, tiles, dst_fn, bufs):
            for (c0, n) in tiles:
                ra, rk = rstd_tile(lambda c: XT(c, slice(c0, c0 + n)), n, bufs, "pre")
                for c in range(KC):
                    xa, xk = XT(c, slice(c0, c0 + n))
                    da, dk = dst_fn(c, c0, n)
                    stt("dve", da, xa, GN.ap[:, gi, c:c + 1], ra, ALU.mult, ALU.mult, U(xk, rk, GN.keys()), dk)

        def postnorm_residual(gi, c0, n, y_fn, factor, bufs):
            ra, rk = rstd_tile(y_fn, n, bufs, "post", factor)
            for c in range(KC):
                ya, yk = y_fn(c)
                xa, xk = XT(c, slice(c0, c0 + n))
                stt("dve", ya, ya, GN.ap[:, gi, c:c + 1], ra, ALU.mult, ALU.mult, U(yk, rk, GN.keys()), yk)
                tt("pool", xa, xa, ya, ALU.add, U(yk, xk), xk)

        def ffn(f, gpre, gpost):
            for (h0, hn) in ((0, 1032), (1032, NT - 1032)):
                copy_burst(7)
                al = Alloc(PB, SB_BYTES)
                W = 1040
                HN = Buf(arena, al.take(KC * W * 2), [KC, W], BF16)
                ACTB = Buf(arena, al.take(HB * W * 2), [HB, W], BF16)
                YST = Buf(arena, al.take(KC * W * 4), [KC, W], F32)
                WGU = [Buf(arena, al.take(8 * 256 * 2), [8, 256], BF16) for _ in range(5)]
                WDN = [Buf(arena, al.take(HB * 128 * 2), [HB, 128], BF16) for _ in range(3)]
                SG = [Buf(arena, al.take(1024), [512], BF16) for _ in range(2)]
                SQ = [Buf(arena, al.take(1024), [512], BF16) for _ in range(4)]
                RS = [Buf(arena, al.take(2048), [512], F32) for _ in range(2)]
                tiles = []
                o = 0
                while o < hn:
                    n = min(344, hn - o)
                    tiles.append((o, n))
                    o += n
                prenorm(gpre, [(h0 + o, n) for (o, n) in tiles],
                        lambda c, c0, n: HN(c, slice(c0 - h0, c0 - h0 + n)), (SQ, RS))
                sgi = 0
                for j in range(HB):
                    wb = WGU[j % 5]
                    wa, wkk = wb()
                    S.dma("pool", wa, wgu_d[f, j].rearrange("p (k n) -> p k n", k=KC), writes=wkk)
                    for (o, n) in tiles:
                        bg, bu = bank(), bank()
                        for k in range(KC):
                            ha, hk = HN(k, slice(o, o + n))
                            mm(psf(bg, 0, n), wb.ap[:, k, 0:128], ha, k == 0, k == KC - 1, U(wkk, hk), PK(bg))
                        for k in range(KC):
                            ha, hk = HN(k, slice(o, o + n))
                            mm(psf(bu, 0, n), wb.ap[:, k, 128:256], ha, k == 0, k == KC - 1, U(wkk, hk), PK(bu))
                        sa, sk = SG[sgi % 2](slice(0, n))
                        sgi += 1
                        act(sa, psf(bg, 0, n), AF.Silu, PK(bg), sk)
                        aa, ak = ACTB(j, slice(o, o + n))
                        tt("dve", aa, sa, psf(bu, 0, n), ALU.mult, U(sk, PK(bu)), ak)
                for c in range(KC):
                    wb = WDN[c % 3]
                    wa, wk = wb()
                    S.dma("pool", wa, wdn_d[f, c].rearrange("p (j n) -> p j n", j=HB), writes=wk)
                    for (o, n) in tiles:
                        b = bank()
                        for j in range(HB):
                            aa, ak = ACTB(j, slice(o, o + n))
                            mm(psf(b, 0, n), wb.ap[:, j, :], aa, j == 0, j == HB - 1, U(wk, ak), PK(b))
                        ya, yk = YST(c, slice(o, o + n))
                        act(ya, psf(b, 0, n), AF.Copy, PK(b), yk)
                for (o, n) in tiles:
                    postnorm_residual(gpost, h0 + o, n, lambda c: YST(c, slice(o, o + n)), 0.5, (SQ, RS))

        def rope_ops(src3, skeys, M, H, dst3, dkeys, cosa, sina, tkeys, RT):
            x1 = src3[:, :, 0:8]
            x2 = src3[:, :, 8:16]
            cb = cosa.unsqueeze(1).to_broadcast([M, H, 8])
            sb_ = sina.unsqueeze(1).to_broadcast([M, H, 8])
            t = []
            for i in range(4):
                a, k = RT[i](slice(0, H), p=(0, M))
                t.append((a, k))
            rk = U(skeys, tkeys)
            tt("dve", t[0][0], x1, cb, ALU.mult, rk, t[0][1])
            tt("dve", t[1][0], x2, sb_, ALU.mult, rk, t[1][1])
            tt("dve", t[2][0], x2, cb, ALU.mult, rk, t[2][1])
            tt("dve", t[3][0], x1, sb_, ALU.mult, rk, t[3][1])
            tt("dve", dst3[:, :, 0:8], t[0][0], t[1][0], ALU.subtract, U(t[0][1], t[1][1]), dkeys)
            tt("dve", dst3[:, :, 8:16], t[2][0], t[3][0], ALU.add, U(t[2][1], t[3][1]), dkeys)

        def proc_kv(b, M, ropek, KTM, VTM, KDUP, RT, out_k_dma, out_v_dma, kt_dst, v_dst, vtn_dst=None, VDUP=None):
            ka, kk = KTM(p=(0, M))
            act(ka, psf(b, 0, 256, p=(0, M)), AF.Copy, PK(b), kk)
            cosa, sina, tkeys = ropek
            k3 = ka.rearrange("p (h d) -> p h d", h=4)
            rope_ops(k3, kk, M, 4, k3, kk, cosa, sina, tkeys, RT)
            va, vk = VTM(p=(0, M))
            act(va, psf(b, 256, 512, p=(0, M)), AF.Copy, PK(b), vk)
            for fn in out_k_dma:
                fn(ka, kk)
            for fn in out_v_dma:
                fn(va, vk)
            da, dk = KDUP(p=(0, M))
            cp("dve", da.rearrange("p (h r d) -> p h r d", h=4, r=2),
               ka.rearrange("p (h d) -> p h d", h=4).unsqueeze(2).to_broadcast([M, 4, 2, 64]), kk, dk)
            bT = bank()
            for h in range(4):
                S.add("pe", lambda e, h=h: e.transpose(out=psb(bT)[:, h * 128:h * 128 + M], in_=da[:, h * 128:(h + 1) * 128],
                                                        identity=IDN.ap[0:M, 0:M]), U(dk, IDN.keys()), PK(bT))
            kt_dst(bT)
            v_dst(va, vk)
            if vtn_dst is not None:
                da2, dk2 = VDUP(p=(0, M))
                cp("dve", da2.rearrange("p (h r d) -> p h r d", h=4, r=2),
                   va.rearrange("p (h d) -> p h d", h=4).unsqueeze(2).to_broadcast([M, 4, 2, 64]), vk, dk2)
                bT2 = bank()
                for h in range(4):
                    S.add("pe", lambda e, h=h: e.transpose(out=psb(bT2)[:, h * 128:h * 128 + M], in_=da2[:, h * 128:(h + 1) * 128],
                                                            identity=IDN.ap[0:M, 0:M]), U(dk2, IDN.keys()), PK(bT2))
                vtn_dst(bT2)

        def proc_q(b, M, ropeq, qtm, qk, RT, qt_dst, QR):
            act(qtm, psf(b, 0, 512, p=(0, M)), AF.Copy, PK(b), qk, scale=0.125)
            qra, qrk = QR(p=(0, M))
            act(qra, psf(b, 0, 512, p=(0, M)).rearrange("p (h d) -> p h d", h=8)[:, :, 0:16], AF.Copy, PK(b), qrk, scale=0.125)
            cosa, sina, tkeys = ropeq
            rope_ops(qra, qrk, M, 8, qtm.rearrange("p (h d) -> p h d", h=8), qk, cosa, sina, tkeys, RT)
            bT = bank()
            for j in range(4):
                S.add("pe", lambda e, j=j: e.transpose(out=psb(bT)[:, j * 128:j * 128 + M], in_=qtm[:, j * 128:(j + 1) * 128],
                                                        identity=IDN.ap[0:M, 0:M]), U(qk, IDN.keys()), PK(bT))
            qt_dst(bT)

        def project(HN, d, i, WG, wk):
            cols, M = tile_cols(d, i)
            b = bank()
            for k in range(KC):
                ha, hk = HN(k, cols)
                mm(psf(b, 0, 512, p=(0, M)), ha, WG.ap[:, k, :], k == 0, k == KC - 1, U(hk, wk), PK(b))
            return b, M

        def attn_unit(kt_fn, vt_fn, q_ap, q_keys, MASK, mk, ONESEL, ok, PB3, pi):
            bss = [bank(), bank()]
            for hp in range(2):
                for kb in range(2):
                    ka, kk = kt_fn(hp, kb)
                    mm(psf(bss[hp], kb * 128, kb * 128 + 128), ka, q_ap(hp), True, True, U(kk, q_keys), PK(bss[hp]))
            pa, pk = PB3[pi % len(PB3)]()
            for hp in range(2):
                act(pa[:, hp * 256:(hp + 1) * 256], psf(bss[hp], 0, 256), AF.Exp, PK(bss[hp]), pk)
            tt("pool", pa, pa, MASK, ALU.mult, U(pk, mk), pk)
            yield None
            bn = bank()
            idx = 0
            for hp in range(2):
                for kb in range(2):
                    va, vk = vt_fn(hp, kb)
                    blk = hp * 2 + kb
                    mm(psf(bn, 0, 128), va, pa[:, blk * 128:(blk + 1) * 128], idx == 0, idx == 3, U(vk, pk), PK(bn))
                    idx += 1
            idx = 0
            for hp in range(2):
                for kb in range(2):
                    blk = hp * 2 + kb
                    mm(psf(bn, 128, 256), ONESEL.ap[:, 64 - 64 * hp:192 - 64 * hp], pa[:, blk * 128:(blk + 1) * 128],
                       idx == 0, idx == 3, U(ok, pk), PK(bn))
                    idx += 1
            return bn

        class UnitPipe:
            def __init__(self):
                self.prev = None

            def _finish(self):
                g, fin = self.prev
                self.prev = None
                try:
                    next(g)
                    raise RuntimeError("unit generator did not finish")
                except StopIteration as e:
                    fin(e.value)

            def push(self, gen, fin):
                next(gen)
                if self.prev is not None:
                    self._finish()
                self.prev = (gen, fin)

            def flush(self):
                if self.prev is not None:
                    self._finish()

        def oproj(OT, wo_d, gi, al):
            WO = [Buf(arena, al.take(8 * 512 * 2), [8, 512], BF16) for _ in range(2)]
            YS = Buf(arena, al.take(KC * 512 * 4), [KC, 512], F32)
            SQ = [Buf(arena, al.take(1024), [512], BF16) for _ in range(4)]
            RS = [Buf(arena, al.take(2048), [512], F32) for _ in range(2)]
            wks = []
            for h in range(2):
                wa, wk = WO[h]()
                S.dma("pool", wa, wo_d[:, h * 512:(h + 1) * 512].rearrange("(k p) n -> p k n", p=128), writes=wk)
                wks.append(wk)
            o = 0
            while o < NT:
                n = min(512, NT - o)
                for cp_ in range(KC):
                    b = bank()
                    h, cc = cp_ // 4, cp_ % 4
                    for c in range(KC):
                        oa, okk = OT(c, slice(o, o + n))
                        mm(psf(b, 0, n), WO[h].ap[:, c, cc * 128:(cc + 1) * 128], oa, c == 0, c == KC - 1, U(okk, wks[h]), PK(b))
                    ya, yk = YS(cp_, slice(0, n))
                    act(ya, psf(b, 0, n), AF.Copy, PK(b), yk)
                postnorm_residual(gi, o, n, lambda c: YS(c, slice(0, n)), 1.0, (SQ, RS))
                o += n

        def sample_attn(al, cache_d, WR, d, QS, qsk, QTs_fn, KTN, ktnk, VTN, vtnk, sink, accum):
            TC = [Buf(arena, al.take(768 * 4), [768], F32) for _ in range(2)]
            PR = Buf(arena, al.take(1024 * 4), [16, 64], F32)
            SS = [Buf(arena, al.take(64), [16], F32) for _ in range(2)]
            PP = [Buf(arena, al.take(64), [16], F32) for _ in range(2)]
            PD = Buf(arena, al.take(128 * 4), [8, 16], F32)
            P0 = Buf(arena, al.take(128 * 4), [8, 16], F32)
            TN = Buf(arena, al.take(128 * 4), [8, 16], F32)
            TD = Buf(arena, al.take(128 * 4), [8, 16], F32)
            OH = Buf(arena, al.take(2048 * 2), [16, 128], BF16)
            CF = Buf(arena, al.take(256 * 4), [2, 128], F32)
            a, ohk = OH()
            S.dma("pool", a.rearrange("p a b -> p (a b)"), cbf_d[:, 1472:3520], writes=ohk)
            a, cfk = CF()
            S.dma("sp", a.rearrange("p a b -> p (a b)"), cf32_d, writes=cfk)
            qa, qk_ = QTs_fn()
            pa, pk = PD()
            tt("dve", pa.rearrange("p (k r) b -> p k r b", r=2), qa.rearrange("p (k r) b -> p k r b", r=2),
               KTN.unsqueeze(2).to_broadcast([128, 4, 2, 16]), ALU.mult, U(qk_, ktnk), pk)
            b0 = bank()
            mm(psf(b0, 0, 128), CF.ap[:, 0, :], pa.rearrange("p a b -> p (a b)"), True, True, U(cfk, pk), PK(b0))
            p0a, p0k = P0()
            act(p0a.rearrange("p a b -> p (a b)"), psf(b0, 0, 128), AF.Exp, PK(b0), p0k)
            bN = bank()
            st["held"].add(bN)
            for b in range(NS):
                yield None
                tc = TC[b % 2]
                ta, tk = tc()
                src_k = bass.AP(cache_d.tensor, b * WR * 512, [[d * 512, 128], [1, 256]])
                S.dma("sp", ta[:, 0:256], src_k, writes=tk)
                src_v = bass.AP(cache_d.tensor, b * WR * 512 + 256, [[d * 512, 128], [64, 4], [1, 64]])
                for r_ in range(2):
                    S.dma("sp", ta[:, 256:768].rearrange("p (k r d) -> p k r d", k=4, r=2)[:, :, r_, :], src_v, writes=tk)
                bq = [bank(), bank()]
                for h in range(2):
                    mm(psf(bq[h]), OH.ap[0:16, b, :], QS[0:16, h * 512:(h + 1) * 512], True, True, U(ohk, qsk), PK(bq[h]))
                pra, prk = PR()
                for h in range(2):
                    tt("dve", pra[:, h * 8:(h + 1) * 8, :].rearrange("p (k j) d -> p k j d", k=2),
                       psf(bq[h]).rearrange("p (k j d) -> p k j d", k=2, j=4),
                       ta[:, h * 128:(h + 1) * 128].rearrange("p (k d) -> p k d", k=2).unsqueeze(2).to_broadcast([128, 2, 4, 64]),
                       ALU.mult, U(PK(bq[h]), tk), prk)
                sa, sk = SS[b % 2]()
                S.add("dve", lambda e, sa=sa, pra=pra: e.tensor_reduce(out=sa, in_=pra, axis=AX.X, op=ALU.add), prk, sk)
                ppa, ppk = PP[b % 2]()
                act(ppa, sa, AF.Exp, sk, ppk)
                for kv in range(4):
                    mm(psf(bN, b * 16 + kv * 4, b * 16 + kv * 4 + 4), ta[:, 256 + kv * 128:256 + (kv + 1) * 128],
                       ppa[:, kv * 4:(kv + 1) * 4], True, True, U(tk, ppk), PK(bN))
                mm(psf(bN, 256 + b * 16, 256 + b * 16 + 16), CF.ap[:, 1, :], ppa, True, True, U(cfk, ppk), PK(bN))
            tna, tnk = TN()
            tda, tdk = TD()
            tt("dve", tna.rearrange("p (k r) b -> p k r b", r=2), p0a.rearrange("p (k r) b -> p k r b", r=2),
               VTN.unsqueeze(2).to_broadcast([128, 4, 2, 16]), ALU.mult, U(p0k, vtnk), tnk)
            for hp in range(2):
                pr = (64 * hp, 64 * hp + 64)
                srcN = psf(bN, 0, 256, p=pr).rearrange("p (b c r) -> p c b r", b=16, c=8)[:, :, :, hp]
                srcD = psf(bN, 256, 512, p=pr).rearrange("p (b c r) -> p c b r", b=16, c=8)[:, :, :, hp]
                tt("dve", tna[pr[0]:pr[1]], tna[pr[0]:pr[1]], srcN, ALU.add, U(tnk, PK(bN)), tnk)
                if sink is not None:
                    ska, skk = sink
                    tt("dve", tda[pr[0]:pr[1]], p0a[pr[0]:pr[1]], ska[pr[0]:pr[1]].unsqueeze(2).to_broadcast([64, 8, 16]), ALU.add, U(p0k, skk), tdk)
                    tt("dve", tda[pr[0]:pr[1]], tda[pr[0]:pr[1]], srcD, ALU.add, U(tdk, PK(bN)), tdk)
                else:
                    tt("dve", tda[pr[0]:pr[1]], p0a[pr[0]:pr[1]], srcD, ALU.add, U(p0k, PK(bN)), tdk)
            st["held"].discard(bN)
            accum(tna, tda, U(tnk, tdk))

        def load_consts(al, d_idx):
            MR = Buf(arena, al.take(1024), [512], BF16)
            MF = Buf(arena, al.take(1024), [512], BF16)
            OS = Buf(arena, al.take(384), [192], BF16)
            a, k1 = MR()
            S.dma("pool", a, cbf_d[:, 448:960], writes=k1)
            a, k2 = MF()
            S.dma("pool", a, cbf_d[:, 960:1472], writes=k2)
            a, k3 = OS()
            S.dma("pool", a, cbf_d[:, 256:448], writes=k3)
            return MR, MF, OS

        def load_rope(al, d_idx):
            RP = Buf(arena, al.take(4 * 17 * 8 * 4), [4, 17, 8], F32)
            a, k = RP()
            S.dma("sp", a.rearrange("p a b c -> p (a b c)"), rope_d[d_idx], writes=k)
            return RP, k

        ffn(0, 0, 1)

        def attention_a():
            al = Alloc(PB, SB_BYTES)
            HN = Buf(arena, al.take(KC * NT * 2), [KC, NT], BF16)
            WG = [Buf(arena, al.take(8 * 512 * 2), [8, 512], BF16) for _ in range(2)]
            QT = Buf(arena, al.take(KC * NT * 2), [KC, NT], BF16)
            KT = Buf(arena, al.take(4 * 18 * 128 * 2), [4, 18, 128], BF16)
            VT = Buf(arena, al.take(18 * 4 * 192 * 2), [18, 4, 192], BF16)
            MR, MF, OS = load_consts(al, 0)
            RP, rpk = load_rope(al, 0)
            SK = Buf(arena, al.take(32), [8], F32)
            QS = Buf(arena, al.take(2048), [1024], BF16)
            KTM = Buf(arena, al.take(1024), [256], F32)
            VTM = Buf(arena, al.take(1024), [256], F32)
            KDUP = Buf(arena, al.take(1024), [512], BF16)
            VDUP = KDUP
            VTN = Buf(arena, al.take(128), [4, 16], BF16)
            QTM = [Buf(arena, al.take(1024), [512], BF16) for _ in range(2)]
            QR = Buf(arena, al.take(512), [8, 16], F32)
            RT = [Buf(arena, al.take(256), [8, 8], F32) for _ in range(4)]
            PB3 = [Buf(arena, al.take(1024), [512], BF16) for _ in range(2)]
            FT = [Buf(arena, al.take(512), [128], F32) for _ in range(2)]
            alq = Alloc(QT.off, SB_BYTES)
            SQ = [Buf(arena, alq.take(1024), [512], BF16) for _ in range(4)]
            RS = [Buf(arena, alq.take(2048), [512], F32) for _ in range(2)]
            tiles = [(o, min(512, NT - o)) for o in range(0, NT, 512)]
            prenorm(2, tiles, lambda c, c0, n: HN(c, slice(c0, c0 + n)), (SQ, RS))
            a, vk_all = VT()
            S.add("pool", lambda e, a=a: e.memset(a, 0.0), (), vk_all)
            a, k = SK()
            S.dma("sp", a, sink_d, writes=k)
            S.add("act", lambda e, a=a: e.activation(out=a, in_=a, func=AF.Exp), k, k)
            grp_cols = [(1024, 1536), (0, 512), (512, 1024)]
            for gi, (g0, g1) in enumerate(grp_cols):
                if gi >= 1 and 'C' in DEVK:
                    continue
                wb = WG[gi % 2]
                wa, wk = wb()
                S.dma("pool", wa, wqkva_d[:, g0:g1].rearrange("(k p) n -> p k n", p=128), writes=wk)
                def tile_body(gi, i, b, M, wb=wb, wk=wk):
                    slot = i + 1
                    if gi == 0:
                        okd, ovd = [], []
                        if i == 15 and 'B' not in DEVK:
                            okd.append(lambda ka, kk: S.dma("sp", ap_d[:, 0:256], ka, reads=kk))
                            ovd.append(lambda va, vk: S.dma("sp", ap_d[:, 256:512], va, reads=vk))
                        if i == 16 and 'B' not in DEVK:
                            okd.append(lambda ka, kk: S.dma("sp", as_d[:, 127, 0:256], ka, reads=kk))
                            ovd.append(lambda va, vk: S.dma("sp", as_d[:, 127, 256:512], va, reads=vk))

                        def kt_dst(bT, slot=slot, M=M):
                            da, dk = KT(slice(None), slot, slice(0, M))
                            cp("act", da, psb(bT)[:, 0:512].rearrange("p (h n) -> p h n", h=4)[:, :, 0:M], PK(bT), dk)

                        def v_dst(va, vk, slot=slot, M=M):
                            da, dk = VT(slot, slice(None), slice(64, 128), p=(0, M))
                            cp("dve", da, va.rearrange("p (h d) -> p h d", h=4), vk, dk)

                        def vtn_dst(bT2):
                            da, dk = VTN()
                            cp("act", da, psb(bT2)[:, 0:512].rearrange("p (h n) -> p h n", h=4)[:, :, 0:16], PK(bT2), dk)
                        if 'D' not in DEVK:
                            proc_kv(b, M, (RP.ap[0:M, 0, i, :], RP.ap[0:M, 1, i, :], rpk), KTM, VTM, KDUP, RT, okd, ovd, kt_dst, v_dst,
                                    vtn_dst if i == 16 else None, VDUP)
                        if i == 15 and 'A' not in DEVK:
                            ka, kk = KT(slice(None), 16, slice(None))
                            S.dma("sp", bnA[:, 0:512].rearrange("p (h n) -> p h n", h=4), ka, reads=kk, writes=[("bnA", 0)])
                            va, vk = VT(16, slice(None), slice(64, 128))
                            S.dma("sp", bnA[:, 512:768].rearrange("p (h n) -> p h n", h=4), va, reads=vk, writes=[("bnA", 1)])
                            S.add("pool", lambda e: e.collective_compute("AllGather", ALU.bypass, replica_groups=RG, ins=[bnA], outs=[gaA]),
                                  [("bnA", 0), ("bnA", 1)], ["gaA"], dma=True, dinc=1, slot=("cc", 0))
                            ka, kk = KT(slice(None), 0, slice(None))
                            S.dma("sp", ka, gaA[0:128, 0:512].rearrange("p (h n) -> p h n", h=4), reads=["gaA"], writes=kk)
                            va, vk = VT(0, slice(None), slice(64, 128))
                            S.dma("sp", va, gaA[0:128, 512:768].rearrange("p (h n) -> p h n", h=4), reads=["gaA"], writes=vk)
                    else:
                        c0 = (gi - 1) * 4
                        if i == 16:
                            qa, qk = QS(slice(c0 * 128, c0 * 128 + 512), p=(0, M))
                        else:
                            qa, qk = QTM[i % 2]()

                        def qt_dst(bT, c0=c0, i=i, M=M):
                            da, dk = QT(slice(c0, c0 + 4), slice(i * 128, i * 128 + M))
                            cp("act", da, psb(bT)[:, 0:512].rearrange("p (h n) -> p h n", h=4)[:, :, 0:M], PK(bT), dk)
                        proc_q(b, M, (RP.ap[0:M, 0, i, :], RP.ap[0:M, 1, i, :], rpk), qa, qk, RT, qt_dst, QR)
                pend = None
                for i in range(17):
                    b, M = project(HN, 1, i, wb, wk)
                    if pend is not None:
                        tile_body(*pend)
                    pend = (gi, i, b, M)
                tile_body(*pend)
            al2 = Alloc(PB, QT.off)

            def accum(tna, tda, keys):
                S.add("dve", lambda e: e.reciprocal(out=tda, in_=tda), keys, keys)
                da, dk = QT(slice(None), slice(NP_, NT))
                tt("dve", da, tna, tda, ALU.mult, keys, dk)
            ktn_a, ktn_k = KT(slice(None), 17, slice(0, 16))
            vtn_a, vtn_k = VTN()
            qsa, qsk = QS(p=(0, 16))
            sgen = iter(())
            if sub >= 3:
                sgen = sample_attn(al2, ca_d, 128, 1, QS.ap, qsk, lambda: QT(slice(None), slice(NP_, NT)), ktn_a, ktn_k, vtn_a, vtn_k,
                                   (SK.ap, SK.keys()), accum)
            pi = 0
            pipeA = UnitPipe()
            for i in (list(range(1, 16)) + [0] if sub >= 2 else []):
                MASK = MF if i == 0 else MR
                ma, mk = MASK()
                for c in range(KC):
                    kvh = c // 2
                    qa_full, qk = QT(c, slice(i * 128, i * 128 + 128))

                    def kt_fn(hp, kb, kvh=kvh, i=i):
                        return KT(kvh, i + kb, slice(None), p=(64 * hp, 64 * hp + 64))

                    def vt_fn(hp, kb, kvh=kvh, i=i):
                        return VT(i + kb, kvh, slice(64 - 64 * hp, 192 - 64 * hp))
                    gen = attn_unit(kt_fn, vt_fn, lambda hp, q=qa_full: q[64 * hp:64 * hp + 64], qk, ma, mk, OS, OS.keys(), PB3, pi)
                    pi += 1

                    def fin(bn, c=c, qa_full=qa_full, qk=qk, fi=pi):
                        fa, fk = FT[fi % 2]()
                        S.add("dve", lambda e, fa=fa, bn=bn, c=c: e.tensor_scalar_add(out=fa, in0=psf(bn, 128, 256), scalar1=SK.ap[:, c:c + 1]), U(PK(bn), SK.keys()), fk)
                        S.add("dve", lambda e, fa=fa: e.reciprocal(out=fa, in_=fa), fk, fk)
                        tt("dve", qa_full, psf(bn, 0, 128), fa, ALU.mult, U(PK(bn), fk), qk)
                    pipeA.push(gen, fin)
                    if pi % 8 == 0:
                        next(sgen, None)
            pipeA.flush()
            for _ in sgen:
                pass
            al3 = Alloc(PB, QT.off)
            if 'E' not in DEVK:
                oproj(QT, woa_d, 3, al3)

        if stage >= 2:
            attention_a()
        if stage >= 3:
            ffn(1, 4, 5)
        if stage >= 4:
            ffn(2, 6, 7)

        HALO_IDX = {}
        hcount = 0
        for g in range(3):
            d = DILS[g]
            nb = 16 // d
            for r in range(d):
                HALO_IDX[(g, r * nb + nb - 1)] = hcount
                hcount += 1
        assert hcount == NH
        HBASE = [0, 1, 5]

        def kv_b():
            al = Alloc(PB, SB_BYTES)
            HN = Buf(arena, al.take(KC * NT * 2), [KC, NT], BF16)
            WG = [Buf(arena, al.take(8 * 512 * 2), [8, 512], BF16) for _ in range(2)]
            RPs = [load_rope(al, g) for g in range(3)]
            KTM = [Buf(arena, al.take(1024), [256], F32) for _ in range(2)]
            VTM = [Buf(arena, al.take(1024), [256], F32) for _ in range(2)]
            KDUP = Buf(arena, al.take(1024), [512], BF16)
            VDUP = Buf(arena, al.take(1024), [512], BF16)
            KST = [Buf(arena, al.take(1024), [4, 128], BF16) for _ in range(2)]
            VST = [Buf(arena, al.take(512), [256], BF16) for _ in range(2)]
            VNS = Buf(arena, al.take(128), [4, 16], BF16)
            RT = [Buf(arena, al.take(256), [8, 8], F32) for _ in range(4)]
            SQ = [Buf(arena, al.take(1024), [512], BF16) for _ in range(4)]
            RS = [Buf(arena, al.take(2048), [512], F32) for _ in range(2)]
            tiles = [(o, min(512, NT - o)) for o in range(0, NT, 512)]
            prenorm(12, tiles, lambda c, c0, n: HN(c, slice(c0, c0 + n)), (SQ, RS))
            cnt = 0
            for g in range(3):
                d = DILS[g]
                W = WINS[g]
                nb = 16 // d
                wb = WG[g % 2]
                wa, wk = wb()
                S.dma("pool", wa, wkvb_d[:, g * 512:(g + 1) * 512].rearrange("(k p) n -> p k n", p=128), writes=wk)
                RP, rpk = RPs[g]
                def tile_body(i, b, M, g=g, d=d, W=W, nb=nb, RP=RP, rpk=rpk):
                    nonlocal cnt
                    okd, ovd = [], []
                    hidx = HALO_IDX.get((g, i))
                    if hidx is not None:
                        r = i // nb
                        okd.append(lambda ka, kk, g=g, r=r, d=d: S.dma("sp", bp_d[g][r::d, 0:256], ka, reads=kk))
                        ovd.append(lambda va, vk, g=g, r=r, d=d: S.dma("sp", bp_d[g][r::d, 256:512], va, reads=vk))
                    if i == 16:
                        okd.append(lambda ka, kk, g=g, W=W: S.dma("sp", bs_d[g][:, W - 1, 0:256], ka, reads=kk))
                        ovd.append(lambda va, vk, g=g, W=W: S.dma("sp", bs_d[g][:, W - 1, 256:512], va, reads=vk))
                    kst = KST[cnt % 2]
                    vst = VST[cnt % 2]
                    ktm = KTM[cnt % 2]
                    vtm = VTM[cnt % 2]
                    cnt += 1

                    def kt_dst(bT, g=g, i=i, M=M, kst=kst, hidx=hidx):
                        da, dk = kst(slice(None), slice(0, M))
                        cp("act", da, psb(bT)[:, 0:512].rearrange("p (h n) -> p h n", h=4)[:, :, 0:M], PK(bT), dk)
                        S.dma("sp", ktsB4[:, g, :, i, 0:M], da, reads=dk, writes=[("kts", g, i)])
                        if hidx is not None:
                            S.dma("sp", bnB3s[hidx // NHC][:, hidx % NHC, 0:512].rearrange("p (h n) -> p h n", h=4), da, reads=dk, writes=[("bnB", hidx, 0)])

                    def v_dst(va, vk, g=g, i=i, M=M, vst=vst, hidx=hidx):
                        da, dk = vst(p=(0, M))
                        cp("dve", da, va, vk, dk)
                        S.dma("sp", vtsB4[0:M, g, :, i, :], da.rearrange("p (h d) -> p h d", h=4), reads=dk, writes=[("vts", g, i)])
                        if hidx is not None:
                            S.dma("sp", bnB3s[hidx // NHC][0:M, hidx % NHC, 512:768], da, reads=dk, writes=[("bnB", hidx, 1)])

                    def vtn_dst(bT2, g=g):
                        da, dk = VNS()
                        cp("act", da, psb(bT2)[:, 0:512].rearrange("p (h n) -> p h n", h=4)[:, :, 0:16], PK(bT2), dk)
                        S.dma("sp", vtnB3[:, g, :, :], da, reads=dk, writes=[("vtn", g)])
                    proc_kv(b, M, (RP.ap[0:M, 0, i, :], RP.ap[0:M, 1, i, :], rpk), ktm, vtm, KDUP, RT, okd, ovd, kt_dst, v_dst,
                            vtn_dst if i == 16 else None, VDUP)
                pend = None
                for i in range(17):
                    b, M = project(HN, d, i, wb, wk)
                    if pend is not None:
                        tile_body(*pend)
                    pend = (i, b, M)
                tile_body(*pend)
            for kx in range(3):
                S.add("pool", lambda e, kx=kx: e.collective_compute("AllGather", ALU.bypass, replica_groups=RG, ins=[bnBs[kx]], outs=[gaBs[kx]]),
                      [("bnB", h, j) for h in range(kx * NHC, (kx + 1) * NHC) for j in range(2)], [("gaB", kx)], dma=True, dinc=1, slot=("cc", 1 + kx))

        if stage >= 5:
            kv_b()

        def q_b():
            al = Alloc(PB, SB_BYTES)
            HN = Buf(arena, al.take(KC * NT * 2), [KC, NT], BF16)
            WG = [Buf(arena, al.take(8 * 512 * 2), [8, 512], BF16) for _ in range(2)]
            RPs = [load_rope(al, g) for g in range(3)]
            QTM = [Buf(arena, al.take(1024), [512], BF16) for _ in range(2)]
            QST = [Buf(arena, al.take(1024), [4, 128], BF16) for _ in range(2)]
            QR = Buf(arena, al.take(512), [8, 16], F32)
            RT = [Buf(arena, al.take(256), [8, 8], F32) for _ in range(4)]
            SQ = [Buf(arena, al.take(1024), [512], BF16) for _ in range(4)]
            RS = [Buf(arena, al.take(2048), [512], F32) for _ in range(2)]
            tiles = [(o, min(512, NT - o)) for o in range(0, NT, 512)]
            prenorm(8, tiles, lambda c, c0, n: HN(c, slice(c0, c0 + n)), (SQ, RS))
            cnt = 0
            for g in range(3):
                d = DILS[g]
                RP, rpk = RPs[g]
                for h in range(2):
                    wb = WG[cnt % 2]
                    wa, wk = wb()
                    S.dma("pool", wa, wqb_d[:, g * 1024 + h * 512:g * 1024 + (h + 1) * 512].rearrange("(k p) n -> p k n", p=128), writes=wk)
                    def tile_body(i, b, M, g=g, d=d, h=h, RP=RP, rpk=rpk):
                        nonlocal cnt
                        qa, qk = QTM[cnt % 2](p=(0, M))
                        qst = QST[cnt % 2]
                        cnt += 1

                        def qt_dst(bT, g=g, h=h, i=i, M=M, qst=qst):
                            da, dk = qst(slice(None), slice(0, M))
                            cp("act", da, psb(bT)[:, 0:512].rearrange("p (h n) -> p h n", h=4)[:, :, 0:M], PK(bT), dk)
                            S.dma("sp", qsB4[g, h * 4:h * 4 + 4, :, i * 128:i * 128 + M].rearrange("c p n -> p c n"), da, reads=dk, writes=[("qs", g, h, i)])
                        proc_q(b, M, (RP.ap[0:M, 0, i, :], RP.ap[0:M, 1, i, :], rpk), qa, qk, RT, qt_dst, QR)
                        if i == 16:
                            S.dma("sp", qssB3[g, :, h * 512:(h + 1) * 512], qa, reads=qk, writes=[("qss", g)])
                    pend = None
                    for i in range(17):
                        b, M = project(HN, d, i, wb, wk)
                        if pend is not None:
                            tile_body(*pend)
                        pend = (i, b, M)
                    tile_body(*pend)

        if stage >= 6:
            q_b()

        def halo_pieces(h0, d):
            out = []
            h = h0
            while h < h0 + d:
                kx = h // NHC
                hi = min(h0 + d, (kx + 1) * NHC)
                out.append((kx, h - kx * NHC, hi - kx * NHC, h - h0))
                h = hi
            return out

        def attention_b():
            al = Alloc(PB, SB_BYTES)
            OT = Buf(arena, al.take(KC * NT * 2), [KC, NT], BF16)
            MR, MF, OS = load_consts(al, 0)
            al_kv = al.cur
            KTB = [Buf(arena, al.take((DILS[g] + 16) * 128 * 2), [DILS[g] + 16, 128], BF16) for g in range(3)]
            VTB = [Buf(arena, al.take((DILS[g] + 16) * 192 * 2), [DILS[g] + 16, 192], BF16) for g in range(3)]
            ACN = Buf(arena, al.take(NP_ * 4), [NP_], F32)
            ACD = Buf(arena, al.take(NP_ * 4), [NP_], F32)
            QTR = [Buf(arena, al.take(16 * 128 * 2), [16 * 128], BF16) for _ in range(2)]
            PB3 = [Buf(arena, al.take(1024), [512], BF16) for _ in range(3)]
            al_tmp = al.cur
            for g in range(3):
                a, k = VTB[g]()
                S.add("pool", lambda e, a=a: e.memset(a, 0.0), (), k)
            def sample_b_gen():
                al2 = Alloc(al_tmp, SB_BYTES)
                SN = Buf(arena, al2.take(512), [8, 16], F32)
                SD = Buf(arena, al2.take(512), [8, 16], F32)
                QS = Buf(arena, al2.take(2048), [1024], BF16)
                QTS = Buf(arena, al2.take(256), [8, 16], BF16)
                KTN = Buf(arena, al2.take(128), [4, 16], BF16)
                VTN = Buf(arena, al2.take(128), [4, 16], BF16)
                base2 = al2.cur
                for g in range(3):
                    al3 = Alloc(base2, SB_BYTES)
                    qsa, qsk = QS(p=(0, 16))
                    S.dma("sp", qsa, qssB3[g], reads=[("qss", g)], writes=qsk)
                    qta, qtk = QTS()
                    S.dma("sp", qta, qsB4[g, :, :, 2048:2064].rearrange("c p n -> p c n"), reads=[("qs", g, h, 16) for h in range(2)], writes=qtk)
                    kna, knk = KTN()
                    S.dma("sp", kna, ktsB4[:, g, :, 16, 0:16], reads=[("kts", g, 16)], writes=knk)
                    vna, vnk = VTN()
                    S.dma("sp", vna, vtnB3[:, g, :, :], reads=[("vtn", g)], writes=vnk)

                    def accum(tna, tda, keys, g=g):
                        sna, snk = SN()
                        sda, sdk = SD()
                        if g == 0:
                            cp("dve", sna, tna, keys, snk)
                            cp("dve", sda, tda, keys, sdk)
                        else:
                            tt("dve", sna, sna, tna, ALU.add, U(snk, keys), snk)
                            tt("dve", sda, sda, tda, ALU.add, U(sdk, keys), sdk)
                    yield from sample_attn(al3, cb_d[g], WINS[g], DILS[g], QS.ap, qsk, lambda: QTS(), kna, knk, vna, vnk, None, accum)
                sna, snk = SN()
                sda, sdk = SD()
                S.add("dve", lambda e: e.reciprocal(out=sda, in_=sda), sdk, sdk)
                da, dk = OT(slice(None), slice(NP_, NT))
                tt("dve", da, sna, sda, ALU.mult, U(snk, sdk), dk)
            sgenB = sample_b_gen()
            pi = 0
            qcnt = 0
            pipeB = UnitPipe()
            for kvh in range(4):
                for g in range(3):
                    d = DILS[g]
                    h0 = HBASE[g]
                    for (kx, lo, hi, so) in halo_pieces(h0, d):
                        ka, kk = KTB[g](slice(so, so + hi - lo), slice(None))
                        S.dma("sp", ka, gaB3s[kx][0:128, lo:hi, kvh * 128:(kvh + 1) * 128], reads=[("gaB", kx)], writes=kk)
                        va, vk = VTB[g](slice(so, so + hi - lo), slice(64, 128))
                        S.dma("sp", va, gaB3s[kx][0:128, lo:hi, 512 + kvh * 64:512 + (kvh + 1) * 64], reads=[("gaB", kx)], writes=vk)
                    ka, kk = KTB[g](slice(d, d + 16), slice(None))
                    S.dma("sp", ka, ktsB4[:, g, kvh, 0:16, :], reads=[("kts", g, i) for i in range(16)], writes=kk)
                    va, vk = VTB[g](slice(d, d + 16), slice(64, 128))
                    S.dma("sp", va, vtsB4[:, g, kvh, 0:16, :], reads=[("vts", g, i) for i in range(16)], writes=vk)
                for c in (2 * kvh, 2 * kvh + 1):
                    for g in range(3):
                        d = DILS[g]
                        nb = 16 // d
                        qt = QTR[qcnt % 2]
                        qcnt += 1
                        qa_all, qk = qt()
                        S.dma("sp", qa_all, qsB4[g, c, :, 0:16 * 128], reads=[("qs", g, c // 4, i) for i in range(16)], writes=qk)
                        for i in range(16):
                            r, m = i // nb, i % nb
                            own = d + i
                            prev = (d + i - 1) if m >= 1 else r
                            MASK = MR if m >= 1 else MF
                            ma, mk = MASK()
                            q_i = qa_all[:, i * 128:(i + 1) * 128]

                            def kt_fn(hp, kb, g=g, own=own, prev=prev):
                                return KTB[g](own if kb else prev, slice(None), p=(64 * hp, 64 * hp + 64))

                            def vt_fn(hp, kb, g=g, own=own, prev=prev):
                                return VTB[g](own if kb else prev, slice(64 - 64 * hp, 192 - 64 * hp))
                            gen = attn_unit(kt_fn, vt_fn, lambda hp, q=q_i: q[64 * hp:64 * hp + 64], qk, ma, mk, OS, OS.keys(), PB3, pi)
                            pi += 1

                            def fin(bn, g=g, d=d, i=i):
                                cols, _ = tile_cols(d, i)
                                na, nk = ACN(cols)
                                dda, ddk = ACD(cols)
                                if g == 0:
                                    cp("dve", na, psf(bn, 0, 128), PK(bn), nk)
                                    cp("dve", dda, psf(bn, 128, 256), PK(bn), ddk)
                                else:
                                    tt("dve", na, na, psf(bn, 0, 128), ALU.add, U(nk, PK(bn)), nk)
                                    tt("dve", dda, dda, psf(bn, 128, 256), ALU.add, U(ddk, PK(bn)), ddk)
                            pipeB.push(gen, fin)
                            if pi % 8 == 0:
                                next(sgenB, None)
                    pipeB.flush()
                    for q4 in range(4):
                        sl = slice(q4 * 512, (q4 + 1) * 512)
                        dda, ddk = ACD(sl)
                        na, nk = ACN(sl)
                        S.add("dve", lambda e, dda=dda: e.reciprocal(out=dda, in_=dda), ddk, ddk)
                        oa, ok_ = OT(c, sl)
                        tt("pool", oa, na, dda, ALU.mult, U(nk, ddk), ok_)
            for _ in sgenB:
                pass
            al4 = Alloc(al_kv, al_tmp)
            oproj(OT, wob_d, 9, al4)

        if stage >= 7:
            attention_b()
        if stage >= 8:
            ffn(3, 10, 11)

        copy_burst(1000)
        for c in range(KC):
            a, k = XT(c)
            S.dma("sp", yT_d[c * 128:(c + 1) * 128, :], a, reads=k)

        S.emit(nc, es)
    return nc


def _rope_tables(core):
    half = core % 2
    inv = (500000.0 ** (-np.arange(0, 16, 2, dtype=np.float32) / 16.0)).astype(np.float32)
    out = np.zeros((3, 128, 4, 17, 8), np.float32)
    for gi, d in enumerate(DILS):
        nb = 16 // d
        for i in range(17):
            if i == 16:
                pos = np.full(128, 8192.0, np.float32)
            else:
                r, m = i // nb, i % nb
                pos = (half * 2048 + r + d * (128 * m + np.arange(128))).astype(np.float32)
            ang = pos[:, None] * inv[None, :]
            c, s = np.cos(ang).astype(np.float32), np.sin(ang).astype(np.float32)
            out[gi, :, 0, i] = c
            out[gi, :, 1, i] = s
            out[gi, :, 2, i] = c * np.float32(0.125)
            out[gi, :, 3, i] = s * np.float32(0.125)
    return out.reshape(3, 128, 4 * 17 * 8)


def _consts(core):
    cbf = np.zeros((128, 3520), np.float32)
    cbf[:, 0:128] = np.eye(128, dtype=np.float32)
    cbf[:, 128:256] = 1.0
    cbf[:, 256 + 64:256 + 128] = 1.0
    k = np.arange(128)[:, None]
    q = np.arange(128)[None, :]
    mA = (k >= q).astype(np.float32)
    mB = (k <= q).astype(np.float32)
    mr = np.concatenate([mA, mB, mA, mB], 1)
    cbf[:, 448:960] = mr
    mf = mr.copy()
    if core % 2 == 0:
        mf[:, 0:128] = 0.0
        mf[:, 256:384] = 0.0
    cbf[:, 960:1472] = mf
    oh = np.zeros((128, 16, 128), np.float32)
    for b in range(16):
        oh[b, b, :] = 1.0
    cbf[:, 1472:3520] = oh.reshape(128, 2048)
    cf = np.zeros((128, 256), np.float32)
    cf[0:64, 0:64] = 1.0
    cf[64:128, 64:128] = 1.0
    cf[:, 128:256] = 1.0
    return cbf, cf


_NC_CACHE = {}
_BUILD_ARGS = ()


def kernel(x_prompt, x_sample, cache_a_kv, cache_b_kv_w128, cache_b_kv_w512, cache_b_kv_w2048,
           norm_g, w_ffn_gu, w_ffn_dn, w_qkv_a, sink_a, w_o_a, g_kv_b, w_kv_b, w_q_b, w_o_b):
    f = lambda a: np.ascontiguousarray(np.asarray(a, dtype=np.float32))
    x_prompt, x_sample = f(x_prompt), f(x_sample)
    gains = np.concatenate([f(norm_g).reshape(12, 1024), f(g_kv_b).reshape(1, 1024)], 0)
    gains_l = np.ascontiguousarray(gains.reshape(13, 8, 128).transpose(2, 0, 1).reshape(128, 104))
    sk = f(sink_a).reshape(16)
    sinkl = np.zeros((128, 8), np.float32)
    for c in range(8):
        sinkl[0:64, c] = sk[2 * c]
        sinkl[64:128, c] = sk[2 * c + 1]
    shared = {
        "w_gu": np.ascontiguousarray(f(w_ffn_gu).reshape(4, KC, 128, 2, HB, 128).transpose(0, 4, 2, 1, 3, 5)).reshape(4, HB, 128, KC * 256),
        "w_dn": np.ascontiguousarray(f(w_ffn_dn).reshape(4, HB, 128, KC, 128).transpose(0, 3, 2, 1, 4)).reshape(4, KC, 128, HB * 128),
        "w_qkv_a": f(w_qkv_a).reshape(1024, 1536), "w_o_a": f(w_o_a).reshape(1024, 1024),
        "w_kv_b": f(w_kv_b), "w_q_b": f(w_q_b).reshape(1024, 3072), "w_o_b": f(w_o_b).reshape(1024, 1024),
        "gains": gains_l, "sinkl": sinkl,
    }
    ca = f(cache_a_kv).reshape(128, 128, 512)
    cb = [f(cache_b_kv_w128).reshape(128, 128, 512), f(cache_b_kv_w512).reshape(128, 512, 512),
          f(cache_b_kv_w2048).reshape(128, 2048, 512)]
    in_maps = []
    for core in range(8):
        b, half = core // 2, core % 2
        xt = np.empty((1024, NT), np.float32)
        xt[:, 0:NP_] = x_prompt[b, half * NP_:(half + 1) * NP_, :].T
        xt[:, NP_:NT] = x_sample[core * NS:(core + 1) * NS, 0, :].T
        cbf, cf = _consts(core)
        m = dict(shared)
        m.update({"xT": xt, "rope": _rope_tables(core), "cbf": cbf, "cf32": cf,
                  "ca": np.ascontiguousarray(ca[core * NS:(core + 1) * NS]),
                  "cb0": np.ascontiguousarray(cb[0][core * NS:(core + 1) * NS]),
                  "cb1": np.ascontiguousarray(cb[1][core * NS:(core + 1) * NS]),
                  "cb2": np.ascontiguousarray(cb[2][core * NS:(core + 1) * NS])})
        in_maps.append(m)
    if "nc" not in _NC_CACHE:
        _NC_CACHE["nc"] = build_program(*_BUILD_ARGS)
    if len(_BUILD_ARGS) > 3 and _BUILD_ARGS[3]:
        for m in in_maps:
            m["w_gu"] = np.ascontiguousarray(m["w_gu"][0:1]); m["w_dn"] = np.ascontiguousarray(m["w_dn"][0:1])
            m["cb1"] = np.ascontiguousarray(m["cb1"][:, 0:2]); m["cb2"] = np.ascontiguousarray(m["cb2"][:, 0:2])
    res = run_bass_kernel_spmd(_NC_CACHE["nc"], in_maps, core_ids=list(range(8)))
    R = res.results
    y_p = np.empty((4, 4096, 1024), np.float32)
    y_s = np.empty((128, 1, 1024), np.float32)
    for core in range(8):
        b, half = core // 2, core % 2
        yt = R[core]["yT"]
        y_p[b, half * NP_:(half + 1) * NP_, :] = yt[:, 0:NP_].T
        y_s[core * NS:(core + 1) * NS, 0, :] = yt[:, NP_:NT].T
    a_p = np.stack([R[2 * b + 1]["a_p"].reshape(128, 2, 4, 64) for b in range(4)], 0)[None]
    b_p = [np.stack([R[2 * b + 1]["b%d_p" % g].reshape(WINS[g], 2, 4, 64) for b in range(4)], 0) for g in range(3)]
    a_s = np.concatenate([R[c]["a_s"].reshape(NS, 128, 2, 4, 64) for c in range(8)], 0)[None]
    b_s = [np.concatenate([R[c]["b%d_s" % g].reshape(NS, WINS[g], 2, 4, 64) for c in range(8)], 0) for g in range(3)]
    return (y_p, y_s, np.ascontiguousarray(a_p), b_p[0], b_p[1], b_p[2], np.ascontiguousarray(a_s), b_s[0], b_s[1], b_s[2])
```
